# Optimizing a Trainium2 kernel written in Bass

```python
import jax, jax.numpy as jnp
from jax import lax
import numpy as np

D_MODEL = 1024
BATCH = 8
SEQ = 2048
DEPTH = 1
DEC_BATCH = 128
DEC_SEQ = 1
PAST_LEN = 8192
PAGE_SIZE = 128

HEAD_DIM = 64
GROUPS = ((128, 1), (512, 4), (2048, 16))
H_G = 4
N_HEADS = H_G * len(GROUPS)
ATTN_WIDTH = N_HEADS * HEAD_DIM
ATTN_OUT = H_G * HEAD_DIM
ROT_DIM = HEAD_DIM // 4
ROPE_THETA = 500000.0
C_CONV = D_MODEL
CONV_WIDTH = 31
D_FF = -(-8 * D_MODEL // (3 * 256)) * 256
PLE_DIM = 256
NORM_EPS = 1e-6
SPLITS = [2 * C_CONV, ATTN_WIDTH, ATTN_WIDTH, ATTN_WIDTH, 2 * D_MODEL]
IN_COLS = sum(SPLITS)

kernel_name = "gated_conformer_dilated_attn_decoder_step"


def rms_norm(x, g):
    xf = x.astype(jnp.float32)
    y = xf * lax.rsqrt(jnp.mean(xf * xf, axis=-1, keepdims=True) + NORM_EPS)
    return (y * g.astype(jnp.float32)).astype(x.dtype)


def layer_norm(x, g, b):
    xf = x.astype(jnp.float32)
    mu = jnp.mean(xf, axis=-1, keepdims=True)
    var = jnp.mean(jnp.square(xf - mu), axis=-1, keepdims=True)
    y = (xf - mu) * lax.rsqrt(var + NORM_EPS)
    return (y * g.astype(jnp.float32) + b.astype(jnp.float32)).astype(x.dtype)


def rope_partial(x, pos):
    half = ROT_DIM // 2
    inv_freq = ROPE_THETA ** (-jnp.arange(half, dtype=jnp.float32) / half)
    ang = pos.astype(jnp.float32)[:, None] * inv_freq[None, :]
    cos = jnp.cos(ang)[:, None, :]
    sin = jnp.sin(ang)[:, None, :]
    xr = x[..., :ROT_DIM].astype(jnp.float32)
    x1, x2 = xr[..., :half], xr[..., half:]
    rot = jnp.concatenate([x1 * cos - x2 * sin, x2 * cos + x1 * sin], axis=-1)
    return jnp.concatenate([rot.astype(x.dtype), x[..., ROT_DIM:]], axis=-1)


def causal_dwconv(u, prev, w, b):
    full = jnp.concatenate([prev, u], axis=1)
    out = lax.conv_general_dilated(full, w[:, None, :].astype(full.dtype), window_strides=(1,),
                                   padding='VALID', dimension_numbers=('NWC', 'WIO', 'NWC'),
                                   feature_group_count=C_CONV)
    return out + b, full[:, full.shape[1] - (CONV_WIDTH - 1):]


def dilated_group_prompt(q, k, v, window, dil):
    B, S, H, Dh = q.shape
    nk = window // dil
    span = nk * dil
    s_pad = -(-S // span) * span
    M = s_pad // dil
    nb = M // nk

    def to_blocks(t):
        t = jnp.pad(t, ((0, 0), (0, s_pad - S), (0, 0), (0, 0)))
        t = t.reshape(B, M, dil, H, Dh).transpose(0, 2, 1, 3, 4)
        return t.reshape(B, dil, nb, nk, H, Dh)

    def with_prev(t):
        prev = jnp.pad(t[:, :, :-1], ((0, 0), (0, 0), (1, 0), (0, 0), (0, 0), (0, 0)))
        return jnp.concatenate([prev, t], axis=3)

    qb = to_blocks(q)
    kb = with_prev(to_blocks(k))
    vb = with_prev(to_blocks(v))
    s = jnp.einsum('brnqhd,brnkhd->brnhqk', qb, kb,
                   preferred_element_type=jnp.float32) * (HEAD_DIM ** -0.5)
    qi = jnp.arange(nk)[:, None]
    kj = jnp.arange(2 * nk)[None, :]
    dist = nk + qi - kj
    band = (dist >= 0) & (dist <= nk)
    blk_ok = (jnp.arange(nb)[:, None, None] > 0) | (kj >= nk)[None]
    mask = band[None] & blk_ok
    s = jnp.where(mask[:, None], s, -jnp.inf)
    m = jnp.max(s, axis=-1, keepdims=True)
    p = jnp.exp(s - m)
    l = jnp.sum(p, axis=-1)
    o = jnp.einsum('brnhqk,brnkhd->brnqhd', p, vb.astype(jnp.float32))
    o = o / jnp.swapaxes(l, -1, -2)[..., None]
    lse = jnp.swapaxes(m[..., 0] + jnp.log(l), -1, -2)

    def from_blocks(t):
        t = t.reshape((B, dil, M) + t.shape[4:])
        t = jnp.swapaxes(t, 1, 2)
        return t.reshape((B, s_pad) + t.shape[3:])[:, :S]

    return from_blocks(o), from_blocks(lse)


def dilated_group_sample(q, k, v, buf, window, dil):
    L = buf.shape[1]
    T = q.shape[1]
    nk = window // dil
    kv_all = jnp.concatenate([buf, jnp.stack([k, v], axis=2)], axis=1)
    idx = L + jnp.arange(T)[:, None] - dil * jnp.arange(nk + 1)[None, :]
    valid = idx >= 0
    g = jnp.take(kv_all, jnp.clip(idx, 0), axis=1)
    s = jnp.einsum('bthd,btkhd->bthk', q, g[:, :, :, 0],
                   preferred_element_type=jnp.float32) * (HEAD_DIM ** -0.5)
    s = jnp.where(valid[None, :, None, :], s, -jnp.inf)
    m = jnp.max(s, axis=-1, keepdims=True)
    p = jnp.exp(s - m)
    l = jnp.sum(p, axis=-1)
    o = jnp.einsum('bthk,btkhd->bthd', p, g[:, :, :, 1].astype(jnp.float32)) / l[..., None]
    lse = m[..., 0] + jnp.log(l)
    return o, lse, kv_all[:, kv_all.shape[1] - L:]


def window_tail(k, v, length):
    kv = jnp.stack([k, v], axis=2)
    S = kv.shape[1]
    if S < length:
        kv = jnp.pad(kv, ((0, 0), (length - S, 0), (0, 0), (0, 0), (0, 0)))
    return kv[:, kv.shape[1] - length:]


def hybrid_layer(x, pe, pos, conv_prev, win_bufs, is_prompt, w_in, g_mix, w_dw, b_dw, ln_g, ln_b,
                 w_conv_out, w_attn_out, w_o, g_ffn, w_ffn_in, w_ffn_out, g_ple, w_ple_gate, w_ple_proj):
    B, S, _ = x.shape
    h = rms_norm(x, g_mix)
    z = h @ w_in
    u_pre, q, k, v, gate_logits = jnp.split(z, list(np.cumsum(SPLITS)[:-1]), axis=-1)
    u = u_pre[..., :C_CONV] * jax.nn.sigmoid(u_pre[..., C_CONV:])
    c, conv_new = causal_dwconv(u, conv_prev, w_dw, b_dw)
    a_out = jax.nn.silu(layer_norm(c, ln_g, ln_b)) @ w_conv_out
    q = rope_partial(q.reshape(B, S, N_HEADS, HEAD_DIM), pos)
    k = rope_partial(k.reshape(B, S, N_HEADS, HEAD_DIM), pos)
    v = v.reshape(B, S, N_HEADS, HEAD_DIM)
    outs, lses, new_bufs = [], [], []
    for gi, (win, dil) in enumerate(GROUPS):
        hs = slice(gi * H_G, (gi + 1) * H_G)
        qg, kg, vg = q[:, :, hs], k[:, :, hs], v[:, :, hs]
        if is_prompt:
            o, lse = dilated_group_prompt(qg, kg, vg, win, dil)
            nbuf = window_tail(kg, vg, min(win, PAST_LEN))
        else:
            o, lse, nbuf = dilated_group_sample(qg, kg, vg, win_bufs[gi], win, dil)
        outs.append(o)
        lses.append(lse)
        new_bufs.append(nbuf)
    wts = jax.nn.softmax(jnp.stack(lses, axis=0), axis=0)
    o = jnp.sum(wts[..., None] * jnp.stack(outs, axis=0), axis=0)
    b_out = o.astype(x.dtype).reshape(B, S, ATTN_OUT) @ w_attn_out
    gates = jax.nn.sigmoid(gate_logits)
    merged = gates[..., :D_MODEL] * a_out + gates[..., D_MODEL:] * b_out
    x = x + merged @ w_o
    gu = rms_norm(x, g_ffn) @ w_ffn_in
    x = x + (jax.nn.silu(gu[..., :D_FF]) * gu[..., D_FF:]) @ w_ffn_out
    x = x + jax.nn.sigmoid(rms_norm(x, g_ple) @ w_ple_gate) * (pe.astype(x.dtype) @ w_ple_proj)
    return x, conv_new, new_bufs


def setup_inputs(seed: int = 0) -> dict:
    key = jax.random.key(seed)
    ks = jax.random.split(key, 32)
    f32 = jnp.float32
    nrm = lambda k, shape, scale: jax.random.normal(k, shape, f32) * scale
    win_lens = [min(w, PAST_LEN) for w, _ in GROUPS]
    return {
        'x_prompt': nrm(ks[0], (BATCH, SEQ, D_MODEL), 1.0),
        'x_sample': nrm(ks[1], (DEC_BATCH, DEC_SEQ, D_MODEL), 1.0),
        'state_conv': nrm(ks[2], (DEPTH, DEC_BATCH, CONV_WIDTH - 1, C_CONV), 0.5),
        'cache_win_a': nrm(ks[3], (DEPTH, DEC_BATCH, win_lens[0], 2, H_G, HEAD_DIM), 1.0),
        'cache_win_b': nrm(ks[4], (DEPTH, DEC_BATCH, win_lens[1], 2, H_G, HEAD_DIM), 1.0),
        'cache_win_c': nrm(ks[5], (DEPTH, DEC_BATCH, win_lens[2], 2, H_G, HEAD_DIM), 1.0),
        'p_prompt': nrm(ks[6], (DEPTH, BATCH, SEQ, PLE_DIM), 1.0),
        'p_sample': nrm(ks[7], (DEPTH, DEC_BATCH, DEC_SEQ, PLE_DIM), 1.0),
        'w_in': nrm(ks[8], (DEPTH, D_MODEL, IN_COLS), D_MODEL ** -0.5),
        'g_mix': 1.0 + nrm(ks[9], (DEPTH, D_MODEL), 0.01),
        'w_dw': nrm(ks[10], (DEPTH, CONV_WIDTH, C_CONV), CONV_WIDTH ** -0.5),
        'b_dw': nrm(ks[11], (DEPTH, C_CONV), 0.01),
        'ln_g': 1.0 + nrm(ks[12], (DEPTH, C_CONV), 0.01),
        'ln_b': nrm(ks[13], (DEPTH, C_CONV), 0.01),
        'w_conv_out': nrm(ks[14], (DEPTH, C_CONV, D_MODEL), C_CONV ** -0.5),
        'w_attn_out': nrm(ks[15], (DEPTH, ATTN_OUT, D_MODEL), ATTN_OUT ** -0.5),
        'w_o': nrm(ks[16], (DEPTH, D_MODEL, D_MODEL), D_MODEL ** -0.5),
        'g_ffn': 1.0 + nrm(ks[17], (DEPTH, D_MODEL), 0.01),
        'w_ffn_in': nrm(ks[18], (DEPTH, D_MODEL, 2 * D_FF), D_MODEL ** -0.5),
        'w_ffn_out': nrm(ks[19], (DEPTH, D_FF, D_MODEL), D_FF ** -0.5),
        'g_ple': 1.0 + nrm(ks[20], (DEPTH, D_MODEL), 0.01),
        'w_ple_gate': nrm(ks[21], (DEPTH, D_MODEL, D_MODEL), D_MODEL ** -0.5),
        'w_ple_proj': nrm(ks[22], (DEPTH, PLE_DIM, D_MODEL), PLE_DIM ** -0.5),
        'g_final': 1.0 + nrm(ks[23], (D_MODEL,), 0.01),
    }


def reference(x_prompt, x_sample, state_conv, cache_win_a, cache_win_b, cache_win_c, p_prompt, p_sample,
              w_in, g_mix, w_dw, b_dw, ln_g, ln_b, w_conv_out, w_attn_out, w_o, g_ffn, w_ffn_in, w_ffn_out,
              g_ple, w_ple_gate, w_ple_proj, g_final):
    pos_p = jnp.arange(x_prompt.shape[1], dtype=jnp.int32)
    pos_s = PAST_LEN + jnp.arange(x_sample.shape[1], dtype=jnp.int32)
    xp, xs = x_prompt, x_sample
    conv_p, conv_s = [], []
    win_p, win_s = [[], [], []], [[], [], []]
    for i in range(DEPTH):
        lw = (w_in[i], g_mix[i], w_dw[i], b_dw[i], ln_g[i], ln_b[i], w_conv_out[i], w_attn_out[i], w_o[i],
              g_ffn[i], w_ffn_in[i], w_ffn_out[i], g_ple[i], w_ple_gate[i], w_ple_proj[i])
        zeros_ctx = jnp.zeros((xp.shape[0], CONV_WIDTH - 1, C_CONV), xp.dtype)
        xp, cp, bp = hybrid_layer(xp, p_prompt[i], pos_p, zeros_ctx, None, True, *lw)
        xs, cs, bs = hybrid_layer(xs, p_sample[i], pos_s, state_conv[i],
                                  (cache_win_a[i], cache_win_b[i], cache_win_c[i]), False, *lw)
        conv_p.append(cp)
        conv_s.append(cs)
        for gi in range(len(GROUPS)):
            win_p[gi].append(bp[gi])
            win_s[gi].append(bs[gi])
    y_prompt = rms_norm(xp, g_final)
    y_sample = rms_norm(xs, g_final)
    new_conv_prompt = jnp.stack(conv_p, axis=0)
    new_win_a_prompt = jnp.stack(win_p[0], axis=0)
    new_win_b_prompt = jnp.stack(win_p[1], axis=0)
    new_win_c_prompt = jnp.stack(win_p[2], axis=0)
    new_conv_sample = jnp.stack(conv_s, axis=0)
    new_win_a_sample = jnp.stack(win_s[0], axis=0)
    new_win_b_sample = jnp.stack(win_s[1], axis=0)
    new_win_c_sample = jnp.stack(win_s[2], axis=0)
    return (y_prompt, y_sample, new_conv_prompt, new_win_a_prompt, new_win_b_prompt, new_win_c_prompt,
            new_conv_sample, new_win_a_sample, new_win_b_sample, new_win_c_sample)
```

```python
import os
import numpy as np
from contextlib import ExitStack
import concourse.bass as bass
import concourse.mybir as mybir
from concourse.bass_utils import run_bass_kernel_spmd

F32 = mybir.dt.float32
BF16 = mybir.dt.bfloat16
AF = mybir.ActivationFunctionType
ALU = mybir.AluOpType
AX = mybir.AxisListType

ENGINES = ("pe", "act", "dve", "pool", "sp")
COMPUTE = ("pe", "act", "dve", "pool")

D = 1024
KC = 8
S = 2048
T = 512
NT = 4
TS = 16
DFF = 2816
FC = 22
PAST = 8192
EPS = 1e-6
NCORES = 8
RING = 5
SLOTW = 11 * 128


class Op:
    __slots__ = ("eng", "fn", "reads", "writes", "dma", "deps", "sig", "sigval",
                 "dsem", "dval", "idx", "eidx", "waits", "tag")

    def __init__(self, eng, fn, reads, writes, dma):
        self.eng = eng
        self.fn = fn
        self.reads = tuple(reads)
        self.writes = tuple(writes)
        self.dma = dma
        self.deps = []
        self.sig = False
        self.sigval = 0
        self.dsem = None
        self.dval = 0
        self.waits = []


class Prog:
    def __init__(self, nc, n_dma_sems=24):
        self.nc = nc
        self.ops = []
        self.n_dma_sems = n_dma_sems
        self.tag = ""
        self.trace_tags = None

    def op(self, eng, fn, reads=(), writes=(), dma=False):
        pr = [r for r in reads if isinstance(r, tuple) and r[0] == "ps"]
        if pr:
            reads = [r for r in reads if not (isinstance(r, tuple) and r[0] == "ps")]
            writes = list(writes) + pr
        o = Op(eng, fn, reads, writes, dma)
        o.idx = len(self.ops)
        o.tag = self.tag
        self.ops.append(o)
        return o

    def analyze(self):
        last_w = {}
        readers = {}
        for o in self.ops:
            deps = set()
            for r in o.reads:
                w = last_w.get(r)
                if w is not None:
                    deps.add(w)
            for w_ in o.writes:
                w = last_w.get(w_)
                if w is not None:
                    deps.add(w)
                for rd in readers.get(w_, ()):
                    deps.add(rd)
            deps.discard(o.idx)
            o.deps = sorted(deps)
            for r in o.reads:
                readers.setdefault(r, []).append(o.idx)
            for w_ in o.writes:
                last_w[w_] = o.idx
                readers[w_] = []
        ecount = {e: 0 for e in ENGINES}
        for o in self.ops:
            o.eidx = ecount[o.eng]
            ecount[o.eng] += 1
        dma_uses = {}
        dma_rr = {e: 0 for e in ENGINES}
        wm = {e: {c: -1 for c in COMPUTE} for e in ENGINES}
        dwm = {e: {} for e in ENGINES}
        by_eng = {e: [] for e in ENGINES}
        for o in self.ops:
            by_eng[o.eng].append(o)
        for o in self.ops:
            waits_c = {}
            waits_d = {}
            if o.dma:
                k = dma_rr[o.eng] % self.n_dma_sems
                dma_rr[o.eng] += 1
                key = (o.eng, k)
                dma_uses[key] = dma_uses.get(key, 0) + 1
                o.dsem = key
                o.dval = 16 * dma_uses[key]
                if dma_uses[key] > 1:
                    waits_d[key] = o.dval - 16
            for d in o.deps:
                p = self.ops[d]
                if p.dma:
                    waits_d[p.dsem] = max(waits_d.get(p.dsem, 0), p.dval)
                else:
                    if p.eng == o.eng and not o.dma:
                        if o.eng == "pe":
                            continue
                        if o.eidx - p.eidx > 2:
                            continue
                    waits_c[p.eng] = max(waits_c.get(p.eng, -1), p.eidx)
            o.waits = []
            for ce, ei in waits_c.items():
                if ei > wm[o.eng][ce]:
                    wm[o.eng][ce] = ei
                    o.waits.append(("c", ce, ei))
            for key, val in waits_d.items():
                if val > dwm[o.eng].get(key, 0):
                    dwm[o.eng][key] = val
                    o.waits.append(("d", key, val))
        for o in self.ops:
            for w in o.waits:
                if w[0] == "c":
                    by_eng[w[1]][w[2]].sig = True
        for e in COMPUTE:
            c = 0
            for o in by_eng[e]:
                if o.sig:
                    c += 1
                o.sigval = c
        self.by_eng = by_eng

    def emit(self):
        nc = self.nc
        self.analyze()
        by_eng = self.by_eng
        with ExitStack() as es:
            csem = {e: es.enter_context(nc.semaphore("s_" + e)) for e in COMPUTE}
            dsem = {}
            for e in ENGINES:
                if any(o.dma for o in by_eng[e]):
                    for k in range(self.n_dma_sems):
                        dsem[(e, k)] = es.enter_context(nc.semaphore("d_%s_%d" % (e, k)))
            block = es.enter_context(nc.Block())
            dma_final = {}
            for o in self.ops:
                if o.dma:
                    dma_final[o.dsem] = max(dma_final.get(o.dsem, 0), o.dval)

            def run(ename, eng):
                for o in by_eng[ename]:
                    for w in o.waits:
                        if w[0] == "c":
                            p = by_eng[w[1]][w[2]]
                            eng.wait_ge(csem[w[1]], p.sigval)
                        else:
                            eng.wait_ge(dsem[w[1]], w[2])
                    if self.trace_tags is not None and ename in ("pe",):
                        cnt_ = [0]

                        class _Px:
                            def __getattr__(s_, nm, eng=eng, cnt_=cnt_):
                                a = getattr(eng, nm)
                                if nm in ("matmul", "transpose"):
                                    def w(*aa, **kk):
                                        cnt_[0] += 1
                                        return a(*aa, **kk)
                                    return w
                                return a
                        ins = o.fn(_Px())
                        self.trace_tags.append((o.tag, cnt_[0]))
                    else:
                        ins = o.fn(eng)
                    if o.dma:
                        ins.then_inc(dsem[o.dsem], 16)
                    elif o.sig:
                        ins.then_inc(csem[ename], 1)
                if ename == "sp":
                    for key, val in dma_final.items():
                        eng.wait_ge(dsem[key], val)

            @block.tensor
            def _(pe):
                run("pe", pe)

            @block.scalar
            def _(act):
                run("act", act)

            @block.vector
            def _(dve):
                run("dve", dve)

            @block.gpsimd
            def _(pool):
                run("pool", pool)

            @block.sync
            def _(sp):
                run("sp", sp)


def slab_list():
    L = []
    for j in range(8):
        L.append(("w_in", 1024 + 128 * j, 0, 8))
        L.append(("w_in", 128 * j, 0, 8))
    for c in range(6):
        L.append(("w_in", 2048 + 128 * c, 0, 8))
    for c in range(6):
        L.append(("w_in", 2816 + 128 * c, 0, 8))
    for c in range(6):
        L.append(("w_in", 3584 + 128 * c, 0, 8))
    for j in range(8):
        L.append(("w_in", 4352 + 128 * j, 0, 8))
        L.append(("w_conv_out", 128 * j, 0, 8))
        L.append(("w_in", 5376 + 128 * j, 0, 8))
        L.append(("w_attn_out", 128 * j, 0, 2))
    for j in range(8):
        L.append(("w_o", 128 * j, 0, 8))
    for half in range(2):
        for f in range(11 * half, 11 * half + 11):
            L.append(("w_ffn_in", 128 * f, 0, 8))
            L.append(("w_ffn_in", DFF + 128 * f, 0, 8))
        for j in range(8):
            L.append(("w_ffn_out", 128 * j, 11 * half, 11))
    for j in range(8):
        L.append(("w_ple_gate", 128 * j, 0, 8))
        L.append(("w_ple_proj", 128 * j, 0, 2))
    return L


def slab_offsets():
    L = slab_list()
    offs = []
    o = 0
    for (_, _, _, nk) in L:
        offs.append(o)
        o += nk * 128
    return L, offs, o


CA_ONES, CA_RM, CA_BLK, CA_MOWN, CA_MPREV, CA_MC, CA_SEL, CA_ID = 0, 128, 256, 384, 896, 1408, 3456, 3520
CA_N = 3648
CF_ID, CF_EPS, CF_GMIX, CF_GFFN, CF_GPLE, CF_GFIN, CF_BDW, CF_LNG, CF_LNB, CF_WDW = 0, 128, 129, 137, 145, 153, 161, 169, 177, 185
CF_N = 185 + 248
NPOS = S + TS


def build_consts():
    ca = np.zeros((128, CA_N), np.float32)
    ca[:, CA_ONES:CA_ONES + 128] = 1.0
    m = np.arange(128)
    d = m % 64
    partner = np.where(d < 8, m + 8, np.where(d < 16, m - 8, m))
    rm = np.zeros((128, 128), np.float32)
    rm[partner, m] = 1.0
    ca[:, CA_RM:CA_RM + 128] = rm
    ca[:, CA_BLK:CA_BLK + 128] = (m[:, None] // 64 == m[None, :] // 64).astype(np.float32)
    own = (m[:, None] <= m[None, :]).astype(np.float32)
    prev = (m[:, None] >= m[None, :]).astype(np.float32)
    ca[:, CA_MOWN:CA_MOWN + 512] = np.tile(own, (1, 4))
    ca[:, CA_MPREV:CA_MPREV + 512] = np.tile(prev, (1, 4))
    for tt in range(4):
        mc = (m[:, None] <= (32 * tt + np.arange(32))[None, :]).astype(np.float32)
        ca[:, CA_MC + 512 * tt:CA_MC + 512 * (tt + 1)] = np.tile(mc, (1, 16))
    sel = np.zeros((128, 4, 16), np.float32)
    for g in range(4):
        for sl in range(4):
            sel[sl * 30:(sl + 1) * 30, g, 4 * g + sl] = 1.0
    ca[:, CA_SEL:CA_SEL + 64] = sel.reshape(128, 64)
    ca[:, CA_ID:CA_ID + 128] = np.eye(128, dtype=np.float32)
    half = 8
    inv_freq = (500000.0 ** (-np.arange(half, dtype=np.float32) / half)).astype(np.float32)
    pos = np.concatenate([np.arange(S), np.full(TS, PAST)]).astype(np.float32)
    ang = pos[None, :] * inv_freq[:, None]
    cos = np.cos(ang).astype(np.float32)
    sin = np.sin(ang).astype(np.float32)
    rope = np.zeros((128, 2, NPOS), np.float32)
    rope[:, 0, :] = 1.0
    for p in range(128):
        dd = p % 64
        if dd < 8:
            rope[p, 0] = cos[dd]
            rope[p, 1] = -sin[dd]
        elif dd < 16:
            rope[p, 0] = cos[dd - 8]
            rope[p, 1] = sin[dd - 8]
    return ca, rope


def colvec(v):
    return np.ascontiguousarray(np.asarray(v, np.float32).reshape(8, 128).T)


def build_program(debug=False):
    nc = bass.Bass("TRN2", target_bir_lowering=False)
    slabs, soffs, WTOT = slab_offsets()
    NSL = len(slabs)

    def din(name, shape):
        return nc.dram_tensor(name, list(shape), F32, kind="ExternalInput").ap()

    def dout(name, shape):
        return nc.dram_tensor(name, list(shape), F32, kind="ExternalOutput").ap()

    x_d = din("x", [S, D])
    p_d = din("p", [S, 256])
    xs_d = din("xs", [TS, D])
    pss_d = din("pss", [TS, 256])
    st_d = din("state", [TS * 30, D])
    ca_d = din("cache_a", [TS, 128, 512])
    cb_d = din("cache_b", [TS, 512, 512])
    cc_d = din("cache_c", [TS, 2048, 512])
    ws_d = din("wstream", [128, WTOT])
    cA_d = din("constA", [128, CA_N])
    cF_d = din("constF", [128, CF_N])
    rope_d = din("rope", [128, 2, NPOS])
    wrep_d = din("wrep", [120, D])

    y_d = dout("y", [S, D])
    ys_d = dout("ys", [TS, D])
    convp_d = dout("conv_p", [30, D])
    wap_d = dout("wa_p", [128, 512])
    wbp_d = dout("wb_p", [512, 512])
    wcp_d = dout("wc_p", [2048, 512])
    convs_d = dout("conv_s", [TS, 30 * D])
    was_d = dout("wa_s", [TS, 128 * 512])
    wbs_d = dout("wb_s", [TS, 512 * 512])
    wcs_d = dout("wc_s", [TS, 2048 * 512])
    dbg = {}

    P = Prog(nc)
    with ExitStack() as es:
        def sb(name, shape, dt):
            return es.enter_context(nc.sbuf_tensor("sb_" + name, list(shape), dt))

        xT = sb("xT", [128, 8, T], F32)
        hT = sb("hT", [128, 8, T], BF16)
        UW = 30 + T
        uT = sb("uT", [128, 8, UW], BF16)
        cT = sb("cT", [128, 8, T], BF16)
        qm = sb("qm", [128, 2, 6, T], BF16)
        kAB = sb("kAB", [128, 4, 2, T], BF16)
        kC = sb("kC", [128, 2, S], BF16)
        vAB = sb("vAB", [128, 2, 8, 256], BF16)
        vC = sb("vC", [128, 16, 256], BF16)
        accO = sb("accO", [128, 2, T], F32)
        accL = sb("accL", [128, 2, T], F32)
        oT = sb("oT", [128, 2, T], BF16)
        mg = sb("mg", [128, 8, T], BF16)
        actb = sb("actb", [128, 11, T], BF16)
        ring = sb("ring", [128, RING, SLOTW], BF16)
        diag = sb("diag", [128, 2, 31, 128], BF16)
        rope = sb("rope", [128, 2, T], F32)
        cA = sb("cA", [128, CA_N], BF16)
        cF = sb("cF", [128, CF_N], F32)
        sig = sb("sig", [128, 2, T], F32)
        zf = sb("zf", [128, 2, T], F32)
        zb = sb("zb", [128, 2, T], BF16)
        t1 = sb("t1", [128, 2, T], F32)
        rs = sb("rs", [128, 3, T], F32)
        sq = sb("sq", [128, 2, T], BF16)
        pexp = sb("pexp", [128, 2, T], BF16)
        xst = sb("xst", [128, 2, D], F32)
        pst = sb("pst", [128, 256], F32)
        pT = sb("pT", [128, 2, T], BF16)
        cst = sb("cst", [128, 2, 256], F32)
        stS = sb("stS", [120, 1, D], F32)
        prod = kC[0:120, :, :].rearrange("p a (b c) -> p (a b) c", c=D)
        wrep = rope[0:120, :, :].rearrange("p a b -> p (a b)")
        ktile = sb("ktile", [128, 2, 512], F32)
        ssc = sb("ssc", [128, 192], F32)
        pss_ = sb("pssb", [128, 192], BF16)
        qrep = sb("qrep", [128, 2, 2, 128], BF16)
        utf = sb("utf", [128, 8, 32], F32)
        kvn = sb("kvn", [128, 12, TS], F32)
        ps = [es.enter_context(nc.psum_tensor("ps%d" % i, [128, 512], F32)) for i in range(8)]

        identf = cF[:, CF_ID:CF_ID + 128]
        identb = cA[:, CA_ID:CA_ID + 128]
        onesb = cA[:, CA_ONES:CA_ONES + 128]
        rmb = cA[:, CA_RM:CA_RM + 128]
        blkb = cA[:, CA_BLK:CA_BLK + 128]
        epsc = cF[:, CF_EPS:CF_EPS + 1]

        cnt = {"bank": 0, "slab": 0, "pref": 0, "sig": 0, "zf": 0, "t1": 0, "sq": 0, "pexp": 0,
               "xst": 0, "cst": 0, "diag": 0, "zb": 0, "kt": 0, "qrep": 0, "stS": 0}

        reserved = set()

        def nb():
            while True:
                b = cnt["bank"] % 8
                cnt["bank"] += 1
                if b not in reserved:
                    return b

        def reserve():
            b = nb()
            reserved.add(b)
            return b

        def release(b):
            reserved.discard(b)

        def rot(name, n=2):
            i = cnt[name] % n
            cnt[name] += 1
            return i

        NTILES = NT + 1
        TOTSL = NSL * NTILES

        wscr = nc.dram_tensor("wscr", [128, WTOT], BF16, kind="Internal").ap()

        def prefetch():
            g = cnt["pref"]
            if g >= TOTSL:
                return
            cnt["pref"] += 1
            s = g % NSL
            slot = g % RING
            nk = slabs[s][3]
            off = soffs[s]
            if g < NSL:
                P.op("pool", lambda e, slot=slot, nk=nk, off=off: e.dma_start(
                    out=ring[:, slot, 0:nk * 128], in_=ws_d[:, off:off + nk * 128]),
                    writes=[("ring", slot)], dma=True)
                P.op("sp", lambda e, slot=slot, nk=nk, off=off: e.dma_start(
                    out=wscr[:, off:off + nk * 128], in_=ring[:, slot, 0:nk * 128]),
                    reads=[("ring", slot)], writes=[("wscr", s)], dma=True)
            else:
                P.op("pool", lambda e, slot=slot, nk=nk, off=off: e.dma_start(
                    out=ring[:, slot, 0:nk * 128], in_=wscr[:, off:off + nk * 128]),
                    reads=[("wscr", s)], writes=[("ring", slot)], dma=True)

        def take_slab():
            g = cnt["slab"]
            cnt["slab"] += 1
            return g % RING, slabs[g % NSL][3]

        def linear(rhs_fn, rhs_reads, N):
            slot, nk = take_slab()
            b = nb()

            def f(e, slot=slot, nk=nk, b=b):
                for k in range(nk):
                    ins = e.matmul(ps[b][:, 0:N], lhsT=ring[:, slot, k * 128:(k + 1) * 128],
                                   rhs=rhs_fn(k), start=(k == 0), stop=(k == nk - 1))
                return ins
            P.op("pe", f, reads=[("ring", slot)] + list(rhs_reads), writes=[("ps", b)])
            prefetch()
            if cnt["slab"] % 3 == 0:
                pump_shift()
            return b

        def rmsnorm_to_h(N, gcol0, out_fp32=None):
            bs = nb()
            for kc in range(8):
                i = rot("sq")
                P.op("act", lambda e, kc=kc, i=i: e.activation(out=sq[:, i, 0:N], in_=xT[:, kc, 0:N], func=AF.Square),
                     reads=[("xT", kc)], writes=[("sq", i)])
                P.op("pe", lambda e, kc=kc, i=i, bs=bs: e.matmul(ps[bs][:, 0:N], lhsT=onesb, rhs=sq[:, i, 0:N],
                                                                start=(kc == 0), stop=(kc == 7)),
                     reads=[("sq", i), "cA"], writes=[("ps", bs)])
            P.op("act", lambda e, bs=bs: e.activation(out=rs[:, 0, 0:N], in_=ps[bs][:, 0:N], func=AF.Sqrt,
                                                     scale=1.0 / D, bias=epsc),
                 reads=[("ps", bs), "cF"], writes=[("rs", 0)])
            P.op("dve", lambda e: e.reciprocal(out=rs[:, 0, 0:N], in_=rs[:, 0, 0:N]),
                 reads=[("rs", 0)], writes=[("rs", 0)])
            for kc in range(8):
                if out_fp32 is None:
                    P.op("dve", lambda e, kc=kc: e.scalar_tensor_tensor(
                        out=hT[:, kc, 0:N], in0=xT[:, kc, 0:N], scalar=cF[:, gcol0 + kc:gcol0 + kc + 1],
                        in1=rs[:, 0, 0:N], op0=ALU.mult, op1=ALU.mult),
                        reads=[("xT", kc), ("rs", 0), "cF"], writes=[("hT", kc)])
                else:
                    out_fp32(kc)

        def load_xT(src_d, row0, N):
            nblk = max(1, N // 128)
            nblk = min(nblk, int(os.environ.get("KNBLK", "99")))
            bw = min(128, N)
            for blk in range(nblk):
                i = rot("xst")
                P.op("sp", lambda e, blk=blk, i=i: e.dma_start(out=xst[0:bw, i, :],
                                                            in_=src_d[row0 + blk * 128:row0 + blk * 128 + bw, :]),
                     writes=[("xst", i)], dma=True)
                for half in range(2):
                    b = nb()

                    def f(e, i=i, b=b, half=half):
                        for q in range(4):
                            kc = half * 4 + q
                            ins = e.transpose(ps[b][:, q * 128:q * 128 + bw], xst[0:bw, i, kc * 128:(kc + 1) * 128],
                                              identf[0:bw, 0:bw])
                        return ins
                    P.op("pe", f, reads=[("xst", i), "cF"], writes=[("ps", b)])
                    for q in range(4 if not int(os.environ.get("KNOEVAC", "0")) else 0):
                        kc = half * 4 + q
                        eng = "act" if q % 2 == 0 else "dve"
                        if eng == "act":
                            P.op("act", lambda e, b=b, q=q, kc=kc, blk=blk: e.copy(
                                out=xT[:, kc, blk * 128:blk * 128 + bw], in_=ps[b][:, q * 128:q * 128 + bw]),
                                reads=[("ps", b)], writes=[("xT", kc)])
                        else:
                            P.op("dve", lambda e, b=b, q=q, kc=kc, blk=blk: e.tensor_copy(
                                out=xT[:, kc, blk * 128:blk * 128 + bw], in_=ps[b][:, q * 128:q * 128 + bw]),
                                reads=[("ps", b)], writes=[("xT", kc)])

        def store_rows(src_fn, src_reads, dst_d, row0, N, ncol_chunks, dst_col0=0):
            nblk = max(1, N // 128)
            bw = min(128, N)
            for blk in range(nblk):
                i = rot("xst")
                for c0 in range(0, ncol_chunks, 4):
                    b = nb()
                    nq = min(4, ncol_chunks - c0)

                    def f(e, b=b, c0=c0, nq=nq, blk=blk):
                        for q in range(nq):
                            ins = e.transpose(ps[b][0:bw, q * 128:(q + 1) * 128],
                                              src_fn(c0 + q)[:, blk * 128:blk * 128 + bw], identf)
                        return ins
                    P.op("pe", f, reads=list(src_reads) + ["cF"], writes=[("ps", b)])
                    eng = "act" if (c0 // 4) % 2 == 0 else "dve"
                    if eng == "act":
                        P.op("act", lambda e, b=b, c0=c0, nq=nq, i=i: e.copy(
                            out=xst[0:bw, i, c0 * 128:(c0 + nq) * 128], in_=ps[b][0:bw, 0:nq * 128]),
                            reads=[("ps", b)], writes=[("xst", i)])
                    else:
                        P.op("dve", lambda e, b=b, c0=c0, nq=nq, i=i: e.tensor_copy(
                            out=xst[0:bw, i, c0 * 128:(c0 + nq) * 128], in_=ps[b][0:bw, 0:nq * 128]),
                            reads=[("ps", b)], writes=[("xst", i)])
                P.op("act", lambda e, i=i, blk=blk: e.dma_start(
                    out=dst_d[row0 + blk * 128:row0 + blk * 128 + bw, dst_col0:dst_col0 + ncol_chunks * 128],
                    in_=xst[0:bw, i, 0:ncol_chunks * 128]),
                    reads=[("xst", i)], dma=True)

        P.op("pool", lambda e: e.dma_start(out=cA[:, :], in_=cA_d[:, :]), writes=["cA"], dma=True)
        P.op("sp", lambda e: e.dma_start(out=cF[:, :], in_=cF_d[:, :]), writes=["cF"], dma=True)
        dscr = nc.dram_tensor("dscr", [8, 128, 31 * 128], BF16, kind="Internal").ap()
        for j in range(8):
            dbuf = j % 2
            for tap in range(31):
                col = CF_WDW + j * 31 + tap
                if tap % 2 == 0:
                    P.op("dve", lambda e, tap=tap, dbuf=dbuf, col=col: e.tensor_scalar(
                        out=diag[:, dbuf, tap, :], in0=identb, scalar1=cF[:, col:col + 1], scalar2=None, op0=ALU.mult),
                        reads=["cA", "cF"], writes=[("diag", dbuf, tap)])
                else:
                    P.op("act", lambda e, tap=tap, dbuf=dbuf, col=col: e.activation(
                        out=diag[:, dbuf, tap, :], in_=identb, func=AF.Copy, scale=cF[:, col:col + 1]),
                        reads=["cA", "cF"], writes=[("diag", dbuf, tap)])
            P.op("sp", lambda e, j=j, dbuf=dbuf: e.dma_start(out=dscr[j], in_=diag[:, dbuf, :, :].rearrange("p a b -> p (a b)")),
                 reads=[("diag", dbuf, tap) for tap in range(31)], writes=[("dscr", j)], dma=True)
        for _ in range(RING):
            prefetch()
        P.op("dve", lambda e: e.memset(qm[:, :, :, :], 0.0), writes=[("qm", c) for c in range(6)])
        P.op("pool", lambda e: e.memset(kC[:, :, :], 0.0), writes=["kC"])
        P.op("pool", lambda e: e.memset(vC[:, :, :], 0.0), writes=["vC"])
        P.op("dve", lambda e: e.memset(uT[:, :, 0:30], 0.0), writes=[("uT", j) for j in range(8)])

        KSTAGE = int(os.environ.get("KSTAGE", "99"))
        KTILES = int(os.environ.get("KTILES", str(NT)))
        KSAMPLE = int(os.environ.get("KSAMPLE", "1"))
        KSHIFT = int(os.environ.get("KSHIFT", "1"))
        shift_q = []

        def shift_piece(src, dst, L, s_, k, nk):
            n = (L - 1) * 512 // nk
            P.op("act", lambda e: e.dma_start(
                out=dst[s_, k * n:(k + 1) * n].rearrange("(a e) -> a e", a=16),
                in_=src[s_, 1:L, :].rearrange("l c -> (l c)")[k * n:(k + 1) * n].rearrange("(a e) -> a e", a=16)),
                reads=[], writes=[], dma=True)

        for s_ in range(TS):
            for k in range(8):
                shift_q.append((cc_d, wcs_d, 2048, s_, k, 8))
            for k in range(2):
                shift_q.append((cb_d, wbs_d, 512, s_, k, 2))
            shift_q.append((ca_d, was_d, 128, s_, 0, 1))

        def pump_shift():
            if KSHIFT and shift_q:
                shift_piece(*shift_q.pop(0))


        def finish_tile(sample, t0, N):
            if int(os.environ.get("KNOSTORE", "0")):
                return
            store_rows(lambda c: xT[:, c, 0:N], [("xT", kc) for kc in range(8)], ys_d if sample else y_d,
                       0 if sample else t0, N, 8)

        def tile(tt, sample):
            N = TS if sample else T
            t0 = S if sample else tt * T
            par = tt % 2
            last = (tt == NT - 1) and not sample
            if not int(os.environ.get("KNOROPE", "0")):
                P.op("sp", lambda e: e.dma_start(out=rope[:, :, 0:N], in_=rope_d[:, :, t0:t0 + N]),
                     writes=["rope"], dma=True)
            P.tag = "%dA" % tt
            if sample:
                load_xT(xs_d, 0, N)
            else:
                load_xT(x_d, t0, N)
            if KSTAGE <= 1:
                return finish_tile(sample, t0, N)
            rmsnorm_to_h(N, CF_GMIX)
            nblk = max(1, N // 128)
            bw = min(128, N)
            psrc = pss_d if sample else p_d
            prow0 = 0 if sample else t0
            for blk in range(nblk):
                P.op("sp", lambda e, blk=blk: e.dma_start(out=pst[0:bw, :], in_=psrc[prow0 + blk * 128:prow0 + blk * 128 + bw, :]),
                     writes=["pst"], dma=True)
                b = nb()

                def f(e, b=b):
                    for cc in range(2):
                        ins = e.transpose(ps[b][:, cc * 128:cc * 128 + bw], pst[0:bw, cc * 128:(cc + 1) * 128], identf[0:bw, 0:bw])
                    return ins
                P.op("pe", f, reads=["pst", "cF"], writes=[("ps", b)])
                P.op("act", lambda e, b=b, blk=blk: e.copy(
                    out=pT[:, :, blk * 128:blk * 128 + bw], in_=ps[b][:, 0:256].rearrange("p (c n) -> p c n", c=2)[:, :, 0:bw]),
                    reads=[("ps", b)], writes=["pT"])

            hreads = [("hT", kc) for kc in range(8)]
            hfn = lambda k: hT[:, k, 0:N]
            P.tag = "%dB1" % tt
            for j in range(8):
                bb = linear(hfn, hreads, N)
                i = rot("sig")
                P.op("act", lambda e, bb=bb, i=i: e.activation(out=sig[:, i, 0:N], in_=ps[bb][:, 0:N], func=AF.Sigmoid),
                     reads=[("ps", bb)], writes=[("sig", i)])
                ba = linear(hfn, hreads, N)
                if sample:
                    P.op("dve", lambda e, ba=ba, i=i, j=j: e.tensor_tensor(
                        out=utf[:, j, 0:N], in0=ps[ba][:, 0:N], in1=sig[:, i, 0:N], op=ALU.mult),
                        reads=[("ps", ba), ("sig", i)], writes=[("utf", j)])
                else:
                    P.op("dve", lambda e, ba=ba, i=i, j=j: e.tensor_tensor(
                        out=uT[:, j, 30:30 + N], in0=ps[ba][:, 0:N], in1=sig[:, i, 0:N], op=ALU.mult),
                        reads=[("ps", ba), ("sig", i)], writes=[("uT", j)])
                    if last:
                        P.op("dve", lambda e, ba=ba, i=i, j=j: e.tensor_tensor(
                            out=utf[:, j, 0:32], in0=ps[ba][:, N - 32:N], in1=sig[:, i, N - 32:N], op=ALU.mult),
                            reads=[("ps", ba), ("sig", i)], writes=[("utf", j)])
            if KSTAGE <= 2:
                return finish_tile(sample, t0, N)
            if last:
                store_rows(lambda c: utf[:, c, 2:32], [("utf", j) for j in range(8)], convp_d, 0, 30, 8)
            P.tag = "%dB2" % tt
            Cc = rope[:, 0, 0:N]
            Ss = rope[:, 1, 0:N]

            def roped(bz):
                i = rot("zf")
                ib = rot("zb")
                P.op("act", lambda e: e.copy(out=zb[:, ib, 0:N], in_=ps[bz][:, 0:N]),
                     reads=[("ps", bz)], writes=[("zb", ib)])
                br = nb()
                P.op("pe", lambda e: e.matmul(ps[br][:, 0:N], lhsT=rmb, rhs=zb[:, ib, 0:N], start=True, stop=True),
                     reads=[("zb", ib), "cA"], writes=[("ps", br)])
                P.op("dve", lambda e: e.tensor_tensor(out=zf[:, i, 0:N], in0=ps[bz][:, 0:N], in1=Cc, op=ALU.mult),
                     reads=[("ps", bz), "rope"], writes=[("zf", i)])
                it = rot("t1")
                P.op("dve", lambda e: e.tensor_tensor(out=t1[:, it, 0:N], in0=ps[br][:, 0:N], in1=Ss, op=ALU.mult),
                     reads=[("ps", br), "rope"], writes=[("t1", it)])
                P.op("dve", lambda e: e.tensor_tensor(out=zf[:, i, 0:N], in0=zf[:, i, 0:N], in1=t1[:, it, 0:N], op=ALU.add),
                     reads=[("zf", i), ("t1", it)], writes=[("zf", i)])
                return i

            def cache_rows_out(src_ap_fn, src_reads, g, kv, qb_list):
                dst, L = [(wap_d, 128), (wbp_d, 512), (wcp_d, 2048)][g]
                for qb in qb_list:
                    tok0 = t0 + qb * 128
                    r0 = tok0 - (S - L)
                    if r0 < 0:
                        continue
                    ic = rot("cst")
                    b = nb()

                    def f(e, b=b, qb=qb):
                        for cc in range(2):
                            ins = e.transpose(ps[b][:, cc * 128:(cc + 1) * 128],
                                              src_ap_fn(cc)[:, qb * 128:(qb + 1) * 128], identf)
                        return ins
                    P.op("pe", f, reads=list(src_reads) + ["cF"], writes=[("ps", b)])
                    P.op("act", lambda e, b=b, ic=ic: e.copy(out=cst[:, ic, 0:256], in_=ps[b][:, 0:256]),
                         reads=[("ps", b)], writes=[("cst", ic)])
                    P.op("sp", lambda e, ic=ic, r0=r0, dst=dst, kv=kv: e.dma_start(
                        out=dst[r0:r0 + 128, kv * 256:(kv + 1) * 256], in_=cst[:, ic, 0:256]),
                        reads=[("cst", ic)], dma=True)

            for c in range(6):
                bz = linear(hfn, hreads, N)
                i = roped(bz)
                if sample:
                    P.op("act", lambda e, i=i, c=c: e.copy(out=zbq[:, c, 0:N], in_=zf[:, i, 0:N]),
                         reads=[("zf", i)], writes=["zbq"])
                    continue
                P.op("act", lambda e, i=i, c=c: e.copy(out=qm[0:64, 0, c, 0:N], in_=zf[0:64, i, 0:N]),
                     reads=[("zf", i)], writes=[("qm", c)])
                P.op("dve", lambda e, i=i, c=c: e.tensor_copy(out=qm[64:128, 1, c, 0:N], in_=zf[64:128, i, 0:N]),
                     reads=[("zf", i)], writes=[("qm", c)])
            kpair = {}
            for c in range(6):
                bz = linear(hfn, hreads, N)
                i = roped(bz)
                g = c // 2
                if sample:
                    P.op("act", lambda e, i=i, c=c: e.copy(out=kvn[:, c, 0:N], in_=zf[:, i, 0:N]),
                         reads=[("zf", i)], writes=[("kvn", c)])
                else:
                    if g < 2:
                        P.op("act", lambda e, i=i, c=c: e.copy(out=kAB[:, c, par, 0:N], in_=zf[:, i, 0:N]),
                             reads=[("zf", i)], writes=["kAB"])
                    else:
                        P.op("act", lambda e, i=i, c=c: e.copy(out=kC[:, c - 4, t0:t0 + N], in_=zf[:, i, 0:N]),
                             reads=[("zf", i)], writes=["kC"])
                    dst, L = [(wap_d, 128), (wbp_d, 512), (wcp_d, 2048)][g]
                    for qb in range(4):
                        r0 = t0 + qb * 128 - (S - L)
                        if r0 < 0:
                            continue
                        ic = rot("cst")
                        b = nb()
                        P.op("pe", lambda e, b=b, i=i, qb=qb: e.transpose(ps[b][:, 0:128], zf[:, i, qb * 128:(qb + 1) * 128], identf),
                             reads=[("zf", i), "cF"], writes=[("ps", b)])
                        P.op("act", lambda e, b=b, ic=ic: e.copy(out=cst[:, ic, 0:128], in_=ps[b][:, 0:128]),
                             reads=[("ps", b)], writes=[("cst", ic)])
                        P.op("act", lambda e, ic=ic, r0=r0, dst=dst, c=c: e.dma_start(
                            out=dst[r0:r0 + 128, (c % 2) * 128:(c % 2) * 128 + 128], in_=cst[:, ic, 0:128]),
                            reads=[("cst", ic)], dma=True)
            for c in range(6):
                bz = linear(hfn, hreads, N)
                g = c // 2
                cc = c % 2
                i = rot("zf")
                P.op("act", lambda e, bz=bz, i=i: e.copy(out=zf[:, i, 0:N], in_=ps[bz][:, 0:N]),
                     reads=[("ps", bz)], writes=[("zf", i)])
                if sample:
                    P.op("dve", lambda e, i=i, c=c: e.tensor_copy(out=kvn[:, 6 + c, 0:N], in_=zf[:, i, 0:N]),
                         reads=[("zf", i)], writes=[("kvn", 6 + c)])
                    continue
                dst, L = [(wap_d, 128), (wbp_d, 512), (wcp_d, 2048)][g]
                for qb in range(4):
                    r0 = t0 + qb * 128 - (S - L)
                    if r0 < 0 and g != 0:
                        continue
                    b = nb()
                    P.op("pe", lambda e, b=b, i=i, qb=qb: e.transpose(ps[b][:, 0:128], zf[:, i, qb * 128:(qb + 1) * 128], identf),
                         reads=[("zf", i), "cF"], writes=[("ps", b)])
                    if g == 0:
                        P.op("dve", lambda e, b=b, qb=qb, cc=cc: e.tensor_copy(
                            out=vAB[:, par, qb, cc * 128:(cc + 1) * 128], in_=ps[b][:, 0:128]),
                            reads=[("ps", b)], writes=["vAB"])
                    if r0 >= 0:
                        ic = rot("cst")
                        P.op("act", lambda e, b=b, ic=ic: e.copy(out=cst[:, ic, 0:128], in_=ps[b][:, 0:128]),
                             reads=[("ps", b)], writes=[("cst", ic)])
                        P.op("act", lambda e, ic=ic, r0=r0, dst=dst, cc=cc: e.dma_start(
                            out=dst[r0:r0 + 128, 256 + cc * 128:256 + cc * 128 + 128], in_=cst[:, ic, 0:128]),
                            reads=[("cst", ic)], dma=True)
                if g == 1:
                    for r in range(4):
                        b = nb()
                        P.op("pe", lambda e, b=b, i=i, r=r: e.transpose(ps[b][:, 0:128], zf[:, i, r:T:4], identf),
                             reads=[("zf", i), "cF"], writes=[("ps", b)])
                        P.op("dve", lambda e, b=b, r=r, cc=cc: e.tensor_copy(
                            out=vAB[:, par, 4 + r, cc * 128:(cc + 1) * 128], in_=ps[b][:, 0:128]),
                            reads=[("ps", b)], writes=["vAB"])
                if g == 2:
                    for r0_ in range(0, 16, 4):
                        b = nb()

                        def f(e, b=b, i=i, r0_=r0_):
                            for q in range(4):
                                ins = e.transpose(ps[b][0:32, q * 128:(q + 1) * 128], zf[:, i, (r0_ + q):T:16], identf)
                            return ins
                        P.op("pe", f, reads=[("zf", i), "cF"], writes=[("ps", b)])
                        P.op("dve", lambda e, b=b, r0_=r0_, cc=cc: e.tensor_copy(
                            out=vC[32 * tt:32 * tt + 32, r0_:r0_ + 4, cc * 128:(cc + 1) * 128],
                            in_=ps[b][0:32, 0:512].rearrange("p (q c) -> p q c", q=4)),
                            reads=[("ps", b)], writes=["vC"])

            if KSTAGE <= 3:
                return finish_tile(sample, t0, N)
            P.tag = "%dC" % tt
            attn_gen = None
            if not sample and KSTAGE > 4:
                attn_gen = prompt_attention(tt, N)

            def attn_step(n):
                nonlocal attn_gen
                for _ in range(n):
                    if attn_gen is None:
                        return
                    try:
                        next(attn_gen)
                    except StopIteration:
                        attn_gen = None
            if not sample:
                for j in range(8):
                    P.tag = "%dC" % tt
                    dbuf = rot("diag")
                    P.op("sp", lambda e, j=j, dbuf=dbuf: e.dma_start(
                        out=diag[:, dbuf, :, :].rearrange("p a b -> p (a b)"), in_=dscr[j]),
                        reads=[("dscr", j)], writes=[("diag", dbuf, tap) for tap in range(31)], dma=True)
                    b = nb()

                    def f(e, j=j, dbuf=dbuf, b=b):
                        for tap in range(31):
                            ins = e.matmul(ps[b][:, 0:N], lhsT=diag[:, dbuf, tap, :], rhs=uT[:, j, tap:tap + N],
                                           start=(tap == 0), stop=(tap == 30))
                        return ins
                    P.op("pe", f, reads=[("diag", dbuf, tap) for tap in range(31)] + [("uT", j)], writes=[("ps", b)])
                    conv_epilogue(j, b, N)
                    P.tag = "%dD" % tt
                    attn_step(2)
                P.tag = "%dC" % tt
                if not last:
                    P.op("dve", lambda e: e.tensor_copy(out=uT[:, :, 0:30], in_=uT[:, :, T:T + 30]),
                         reads=[("uT", j) for j in range(8)], writes=[("uT", j) for j in range(8)])
            else:
                sample_conv(N)
            ln_finish(N)

            if KSTAGE <= 4:
                return finish_tile(sample, t0, N)
            P.tag = "%dD" % tt
            if sample:
                sample_attention(N)
            else:
                attn_step(1000)

            if KSTAGE <= 5:
                return finish_tile(sample, t0, N)
            P.tag = "%dE" % tt
            cTreads = [("cT", kc) for kc in range(8)]
            for j in range(8):
                bgA = linear(hfn, hreads, N)
                i = rot("sig")
                P.op("act", lambda e, bgA=bgA, i=i: e.activation(out=sig[:, i, 0:N], in_=ps[bgA][:, 0:N], func=AF.Sigmoid),
                     reads=[("ps", bgA)], writes=[("sig", i)])
                bcv = linear(lambda k: cT[:, k, 0:N], cTreads, N)
                it = rot("t1")
                P.op("dve", lambda e, bcv=bcv, i=i, it=it: e.tensor_tensor(
                    out=t1[:, it, 0:N], in0=ps[bcv][:, 0:N], in1=sig[:, i, 0:N], op=ALU.mult),
                    reads=[("ps", bcv), ("sig", i)], writes=[("t1", it)])
                bgB = linear(hfn, hreads, N)
                i2 = rot("sig")
                P.op("act", lambda e, bgB=bgB, i2=i2: e.activation(out=sig[:, i2, 0:N], in_=ps[bgB][:, 0:N], func=AF.Sigmoid),
                     reads=[("ps", bgB)], writes=[("sig", i2)])
                bat = linear(lambda k: oT[:, k, 0:N], ["oT"], N)
                P.op("dve", lambda e, bat=bat, i2=i2: e.tensor_tensor(
                    out=sig[:, i2, 0:N], in0=ps[bat][:, 0:N], in1=sig[:, i2, 0:N], op=ALU.mult),
                    reads=[("ps", bat), ("sig", i2)], writes=[("sig", i2)])
                P.op("dve", lambda e, i2=i2, it=it, j=j: e.tensor_tensor(
                    out=mg[:, j, 0:N], in0=sig[:, i2, 0:N], in1=t1[:, it, 0:N], op=ALU.add),
                    reads=[("sig", i2), ("t1", it)], writes=[("mg", j)])
            P.tag = "%dF" % tt
            mreads = [("mg", kc) for kc in range(8)]
            for j in range(8):
                b = linear(lambda k: mg[:, k, 0:N], mreads, N)
                P.op("dve", lambda e, b=b, j=j: e.tensor_tensor(out=xT[:, j, 0:N], in0=ps[b][:, 0:N], in1=xT[:, j, 0:N], op=ALU.add),
                     reads=[("ps", b), ("xT", j)], writes=[("xT", j)])
            P.tag = "%dG" % tt
            rmsnorm_to_h(N, CF_GFFN)
            for half in range(2):
                for f_ in range(11):
                    bg = linear(hfn, hreads, N)
                    i = rot("sig")
                    P.op("act", lambda e, bg=bg, i=i: e.activation(out=sig[:, i, 0:N], in_=ps[bg][:, 0:N], func=AF.Silu),
                         reads=[("ps", bg)], writes=[("sig", i)])
                    bu = linear(hfn, hreads, N)
                    P.op("dve", lambda e, bu=bu, i=i, f_=f_: e.tensor_tensor(
                        out=actb[:, f_, 0:N], in0=ps[bu][:, 0:N], in1=sig[:, i, 0:N], op=ALU.mult),
                        reads=[("ps", bu), ("sig", i)], writes=[("actb", f_)])
                areads = [("actb", f_) for f_ in range(11)]
                for j in range(8):
                    b = linear(lambda k: actb[:, k, 0:N], areads, N)
                    P.op("dve", lambda e, b=b, j=j: e.tensor_tensor(out=xT[:, j, 0:N], in0=ps[b][:, 0:N], in1=xT[:, j, 0:N], op=ALU.add),
                         reads=[("ps", b), ("xT", j)], writes=[("xT", j)])
            P.tag = "%dH" % tt
            rmsnorm_to_h(N, CF_GPLE)
            for j in range(8):
                bg = linear(hfn, hreads, N)
                i = rot("sig")
                P.op("act", lambda e, bg=bg, i=i: e.activation(out=sig[:, i, 0:N], in_=ps[bg][:, 0:N], func=AF.Sigmoid),
                     reads=[("ps", bg)], writes=[("sig", i)])
                bp = linear(lambda k: pT[:, k, 0:N], ["pT"], N)
                P.op("dve", lambda e, bp=bp, i=i: e.tensor_tensor(out=sig[:, i, 0:N], in0=ps[bp][:, 0:N], in1=sig[:, i, 0:N], op=ALU.mult),
                     reads=[("ps", bp), ("sig", i)], writes=[("sig", i)])
                P.op("dve", lambda e, i=i, j=j: e.tensor_tensor(out=xT[:, j, 0:N], in0=sig[:, i, 0:N], in1=xT[:, j, 0:N], op=ALU.add),
                     reads=[("sig", i), ("xT", j)], writes=[("xT", j)])
            P.tag = "%dI" % tt
            def fin(kc):
                P.op("dve", lambda e, kc=kc: e.scalar_tensor_tensor(
                    out=xT[:, kc, 0:N], in0=xT[:, kc, 0:N], scalar=cF[:, CF_GFIN + kc:CF_GFIN + kc + 1],
                    in1=rs[:, 0, 0:N], op0=ALU.mult, op1=ALU.mult),
                    reads=[("xT", kc), ("rs", 0), "cF"], writes=[("xT", kc)])
            rmsnorm_to_h(N, CF_GFIN, out_fp32=fin)
            store_rows(lambda c: xT[:, c, 0:N], [("xT", kc) for kc in range(8)], ys_d if sample else y_d,
                       0 if sample else t0, N, 8)

        lnb = {}

        def conv_epilogue(j, b, N, src_is_sbuf=None):
            if j == 0:
                lnb["sum"] = reserve()
                lnb["sq"] = reserve()
            src = ps[b][:, 0:N] if src_is_sbuf is None else src_is_sbuf
            rd = [("ps", b)] if src_is_sbuf is None else [("t1", 0)]
            P.op("act", lambda e: e.activation(out=cT[:, j, 0:N], in_=src, func=AF.Identity,
                                              bias=cF[:, CF_BDW + j:CF_BDW + j + 1]),
                 reads=rd + ["cF"], writes=[("cT", j)])
            i = rot("sq")
            P.op("act", lambda e: e.activation(out=sq[:, i, 0:N], in_=src, func=AF.Square,
                                              bias=cF[:, CF_BDW + j:CF_BDW + j + 1]),
                 reads=rd + ["cF"], writes=[("sq", i)])
            bs, bq = lnb["sum"], lnb["sq"]
            P.op("pe", lambda e: e.matmul(ps[bs][:, 0:N], lhsT=onesb, rhs=cT[:, j, 0:N], start=(j == 0), stop=(j == 7)),
                 reads=[("cT", j), "cA"], writes=[("ps", bs)])
            P.op("pe", lambda e: e.matmul(ps[bq][:, 0:N], lhsT=onesb, rhs=sq[:, i, 0:N], start=(j == 0), stop=(j == 7)),
                 reads=[("sq", i), "cA"], writes=[("ps", bq)])

        def ln_finish(N):
            bs, bq = lnb["sum"], lnb["sq"]
            P.op("act", lambda e: e.mul(out=rs[:, 1, 0:N], in_=ps[bs][:, 0:N], mul=1.0 / D),
                 reads=[("ps", bs)], writes=[("rs", 1)])
            P.op("dve", lambda e: e.tensor_tensor(out=rs[:, 2, 0:N], in0=rs[:, 1, 0:N], in1=rs[:, 1, 0:N], op=ALU.mult),
                 reads=[("rs", 1)], writes=[("rs", 2)])
            P.op("dve", lambda e: e.scalar_tensor_tensor(out=rs[:, 2, 0:N], in0=ps[bq][:, 0:N], scalar=1.0 / D,
                                                        in1=rs[:, 2, 0:N], op0=ALU.mult, op1=ALU.subtract),
                 reads=[("ps", bq), ("rs", 2)], writes=[("rs", 2)])
            P.op("act", lambda e: e.activation(out=rs[:, 2, 0:N], in_=rs[:, 2, 0:N], func=AF.Sqrt, bias=epsc),
                 reads=[("rs", 2), "cF"], writes=[("rs", 2)])
            P.op("dve", lambda e: e.reciprocal(out=rs[:, 2, 0:N], in_=rs[:, 2, 0:N]),
                 reads=[("rs", 2)], writes=[("rs", 2)])
            for j in range(8):
                it = rot("t1")
                P.op("dve", lambda e, j=j, it=it: e.tensor_tensor(out=t1[:, it, 0:N], in0=cT[:, j, 0:N], in1=rs[:, 1, 0:N], op=ALU.subtract),
                     reads=[("cT", j), ("rs", 1)], writes=[("t1", it)])
                P.op("dve", lambda e, j=j, it=it: e.tensor_tensor(out=t1[:, it, 0:N], in0=t1[:, it, 0:N], in1=rs[:, 2, 0:N], op=ALU.mult),
                     reads=[("t1", it), ("rs", 2)], writes=[("t1", it)])
                P.op("act", lambda e, j=j, it=it: e.activation(out=cT[:, j, 0:N], in_=t1[:, it, 0:N], func=AF.Silu,
                                                             scale=cF[:, CF_LNG + j:CF_LNG + j + 1],
                                                             bias=cF[:, CF_LNB + j:CF_LNB + j + 1]),
                     reads=[("t1", it), "cF"], writes=[("cT", j)])
            release(bs)
            release(bq)

        def prompt_attention(tt, N):
            first_write = {0: True, 1: True}
            for g in range(3):
                nsb = 16 if g == 2 else 4
                wsb = 32 if g == 2 else 128
                for cc in range(2):
                    c = 2 * g + cc
                    bO = reserve()
                    bL = reserve()
                    P.op("dve", lambda e, bO=bO: e.memset(ps[bO][:, :], 0.0), writes=[("ps", bO)])
                    P.op("dve", lambda e, bL=bL: e.memset(ps[bL][:, :], 0.0), writes=[("ps", bL)])
                    started = set()
                    pending = [None]
                    for hh in range(2):
                        for ksel in (0, 1):
                            if g == 2 and ksel == 1:
                                continue
                            if g == 1 and ksel == 1 and tt == 0:
                                continue
                            sc = []
                            for sbk in range(nsb):
                                if g == 0:
                                    B = 4 * tt + sbk - ksel
                                    if B < 0:
                                        continue
                                    kt = kAB[:, c, (B // 4) % 2, (B % 4) * 128:(B % 4) * 128 + 128]
                                    qv = qm[:, hh, c, sbk * 128:(sbk + 1) * 128]
                                    vt = vAB[:, (B // 4) % 2, B % 4, cc * 128 + hh * 64:cc * 128 + hh * 64 + 64]
                                elif g == 1:
                                    kp = (tt - ksel) % 2
                                    kt = kAB[:, c, kp, sbk:T:4]
                                    qv = qm[:, hh, c, sbk:T:4]
                                    vt = vAB[:, kp, 4 + sbk, cc * 128 + hh * 64:cc * 128 + hh * 64 + 64]
                                else:
                                    kt = kC[:, cc, sbk:S:16]
                                    qv = qm[:, hh, c, sbk:T:16]
                                    vt = vC[:, sbk, cc * 128 + hh * 64:cc * 128 + hh * 64 + 64]
                                sc.append((sbk, kt, qv, vt))
                            if not sc:
                                continue
                            bsc = nb()

                            def fsc(e, sc=sc, bsc=bsc, wsb=wsb):
                                for (sbk, kt, qv, vt) in sc:
                                    ins = e.matmul(ps[bsc][:, sbk * wsb:(sbk + 1) * wsb], lhsT=kt, rhs=qv, start=True, stop=True)
                                return ins
                            P.op("pe", fsc, reads=["kAB", "kC", ("qm", c)], writes=[("ps", bsc)])
                            ip = rot("pexp")
                            P.op("act", lambda e, bsc=bsc, ip=ip: e.activation(out=pexp[:, ip, 0:512], in_=ps[bsc][:, 0:512],
                                                                             func=AF.Exp, scale=0.125),
                                 reads=[("ps", bsc)], writes=[("pexp", ip)])
                            if g == 2:
                                mcol = CA_MC + 512 * tt
                            else:
                                mcol = CA_MOWN if ksel == 0 else CA_MPREV
                            P.op("dve", lambda e, ip=ip, mcol=mcol: e.tensor_tensor(
                                out=pexp[:, ip, 0:512], in0=pexp[:, ip, 0:512], in1=cA[:, mcol:mcol + 512], op=ALU.mult),
                                reads=[("pexp", ip), "cA"], writes=[("pexp", ip)])
                            pv = []
                            for (sbk, kt, qv, vt) in sc:
                                key = (hh, sbk)
                                st = key not in started
                                started.add(key)
                                if g == 2:
                                    fin_ = True
                                elif g == 1:
                                    fin_ = (ksel == 1) or tt == 0
                                else:
                                    fin_ = (ksel == 1) or (4 * tt + sbk - 1 < 0)
                                pv.append((sbk, vt, st, fin_))

                            def fpv(e, pv=pv, hh=hh, ip=ip, bO=bO, bL=bL, wsb=wsb):
                                for (sbk, vt, st, fin_) in pv:
                                    e.matmul(ps[bO][hh * 64:(hh + 1) * 64, sbk * wsb:(sbk + 1) * wsb], lhsT=vt,
                                             rhs=pexp[:, ip, sbk * wsb:(sbk + 1) * wsb], start=False, stop=fin_,
                                             skip_group_check=True)
                                    ins = e.matmul(ps[bL][hh * 64:(hh + 1) * 64, sbk * wsb:(sbk + 1) * wsb],
                                                   lhsT=onesb[:, 0:64],
                                                   rhs=pexp[:, ip, sbk * wsb:(sbk + 1) * wsb], start=False, stop=fin_,
                                                   skip_group_check=True)
                                return ins
                            if pending[0] is not None:
                                pending[0]()
                            pending[0] = (lambda fpv=fpv, ip=ip, bO=bO, bL=bL: P.op(
                                "pe", fpv, reads=[("pexp", ip), "vAB", "vC", "cA"], writes=[("ps", bO), ("ps", bL)]))
                            yield
                    if pending[0] is not None:
                        pending[0]()
                        pending[0] = None
                    if g == 0:
                        dO = accO[:, cc, 0:N]
                        dL = accL[:, cc, 0:N]
                        P.op("act", lambda e, dO=dO, bO=bO: e.copy(out=dO, in_=ps[bO][:, 0:N]),
                             reads=[("ps", bO)], writes=[("accO", cc)])
                        P.op("dve", lambda e, dL=dL, bL=bL: e.tensor_copy(out=dL, in_=ps[bL][:, 0:N]),
                             reads=[("ps", bL)], writes=[("accL", cc)])
                    else:
                        accumulate_v(bO, bL, cc, g, N, first_write)
                    release(bO)
                    release(bL)
                    yield
            for cc in range(2):
                P.op("dve", lambda e, cc=cc: e.reciprocal(out=accL[:, cc, 0:N], in_=accL[:, cc, 0:N]),
                     reads=[("accL", cc)], writes=[("accL", cc)])
                P.op("dve", lambda e, cc=cc: e.tensor_tensor(out=oT[:, cc, 0:N], in0=accO[:, cc, 0:N], in1=accL[:, cc, 0:N], op=ALU.mult),
                     reads=[("accL", cc), ("accO", cc)], writes=["oT"])

        def accumulate_v(bO, bL, cc, g, N, first_write):
            r = 4 if g == 1 else 16
            m = N // r
            dO = accO[:, cc, 0:N].rearrange("p (m r) -> p r m", r=r)
            dL = accL[:, cc, 0:N].rearrange("p (m r) -> p r m", r=r)
            sO = ps[bO][:, 0:N].rearrange("p (r m) -> p r m", r=r)
            sL = ps[bL][:, 0:N].rearrange("p (r m) -> p r m", r=r)
            P.op("dve", lambda e: e.tensor_tensor(out=dO, in0=sO, in1=dO, op=ALU.add),
                 reads=[("ps", bO), ("accO", cc)], writes=[("accO", cc)])
            P.op("dve", lambda e: e.tensor_tensor(out=dL, in0=sL, in1=dL, op=ALU.add),
                 reads=[("ps", bL), ("accL", cc)], writes=[("accL", cc)])

        def sample_conv(N):
            P.op("sp", lambda e: e.dma_start(out=wrep, in_=wrep_d[:, :]), writes=["rope"], dma=True)
            for g4 in range(4):
                i = 0
                P.op("sp", lambda e, g4=g4, i=i: e.dma_start(out=stS[:, i, :], in_=st_d[g4 * 120:(g4 + 1) * 120, :]),
                     writes=[("stS", i)], dma=True)
                P.op("dve", lambda e, g4=g4, i=i: e.tensor_tensor(out=prod[:, g4, :], in0=stS[:, i, :], in1=wrep[:, :], op=ALU.mult),
                     reads=[("stS", i), "rope"], writes=["kC"])
            for j in range(8):
                b = nb()

                def f(e, j=j, b=b):
                    for g4 in range(4):
                        ins = e.matmul(ps[b][:, 0:N], lhsT=prod[:, g4, j * 128:(j + 1) * 128],
                                       rhs=cA[0:120, CA_SEL + 16 * g4:CA_SEL + 16 * g4 + 16], start=(g4 == 0), stop=(g4 == 3))
                    return ins
                P.op("pe", f, reads=["kC", "cA"], writes=[("ps", b)])
                P.op("dve", lambda e, j=j, b=b: e.scalar_tensor_tensor(
                    out=t1[:, 0, 0:N], in0=utf[:, j, 0:N], scalar=cF[:, CF_WDW + j * 31 + 30:CF_WDW + j * 31 + 31],
                    in1=ps[b][:, 0:N], op0=ALU.mult, op1=ALU.add),
                    reads=[("utf", j), ("ps", b), "cF"], writes=[("t1", 0)])
                conv_epilogue(j, b, N, src_is_sbuf=t1[:, 0, 0:N])
            P.op("sp", lambda e: e.dma_start(out=convs_d[:, 0:29 * D],
                                            in_=st_d.rearrange("(s j) c -> s (j c)", j=30)[:, D:30 * D]),
                 dma=True)
            store_rows(lambda c: utf[:, c, 0:N], [("utf", j) for j in range(8)], convs_d, 0, N, 8, dst_col0=29 * D)

        def sample_attention(N):
            caches = [(ca_d, 128, 1), (cb_d, 512, 4), (cc_d, 2048, 16)]
            outs = [was_d, wbs_d, wcs_d]
            for g in range(3):
                L = caches[g][1]
                for kv in range(2):
                    store_rows(lambda c, g=g, kv=kv: kvn[:, kv * 6 + 2 * g + c, 0:N],
                               [("kvn", kv * 6 + 2 * g + c) for c in range(2)], outs[g], 0, N, 2,
                               dst_col0=(L - 1) * 512 + kv * 256)
            bO = reserve()
            bL = reserve()
            P.op("dve", lambda e: e.memset(ps[bO][:, :], 0.0), writes=[("ps", bO)])
            P.op("dve", lambda e: e.memset(ps[bL][:, :], 0.0), writes=[("ps", bL)])
            for s in range(N):
                for g in range(3):
                    src, L, dil = caches[g]
                    ik = rot("kt")
                    P.op("sp", lambda e, s=s, src=src, L=L, dil=dil, ik=ik: e.dma_start(
                        out=ktile[:, ik, :], in_=src[s, 0:L:dil, :]), writes=[("kt", ik)], dma=True)
                    iq = rot("qrep")
                    bq = nb()
                    for cc in range(2):
                        c = 2 * g + cc
                        P.op("dve", lambda e, c=c, cc=cc, s=s, iq=iq: e.tensor_tensor(
                            out=qrep[:, iq, cc, :], in0=identb, in1=zbq[:, c, s:s + 1].to_broadcast([128, 128]), op=ALU.mult),
                            reads=["zbq", "cA"], writes=[("qrep", iq, cc)])
                        P.op("pe", lambda e, cc=cc, bq=bq, iq=iq: e.matmul(ps[bq][:, cc * 128:(cc + 1) * 128], lhsT=onesb, rhs=qrep[:, iq, cc, :],
                                                                  start=True, stop=True),
                             reads=[("qrep", iq, cc), "cA"], writes=[("ps", bq)])
                    it = rot("t1")
                    P.op("dve", lambda e, ik=ik, bq=bq, it=it: e.tensor_tensor(
                        out=t1[:, it, 0:256], in0=ktile[:, ik, 0:256], in1=ps[bq][:, 0:256], op=ALU.mult),
                        reads=[("kt", ik), ("ps", bq)], writes=[("t1", it)])
                    col = (s * 3 + g) * 4
                    P.op("dve", lambda e, it=it, col=col: e.tensor_reduce(
                        out=ssc[:, col:col + 4], in_=t1[:, it, 0:256].rearrange("p (h d) -> p h d", h=4),
                        axis=AX.X, op=ALU.add),
                        reads=[("t1", it)], writes=["ssc"])
                    P.op("act", lambda e, col=col: e.activation(out=pss_[:, col:col + 4], in_=ssc[:, col:col + 4], func=AF.Exp, scale=0.125),
                         reads=["ssc"], writes=["pss"])
                    P.op("act", lambda e, ik=ik: e.copy(out=zb[:, 0, 0:256], in_=ktile[:, ik, 256:512]),
                         reads=[("kt", ik)], writes=[("zb", 0)])

                    def fpv(e, s=s, g=g, col=col):
                        ins = None
                        for slot in range(4):
                            cc, hh = slot // 2, slot % 2
                            e.matmul(ps[bO][hh * 64:(hh + 1) * 64, cc * 16 + s:cc * 16 + s + 1],
                                     lhsT=zb[:, 0, slot * 64:(slot + 1) * 64], rhs=pss_[:, col + slot:col + slot + 1],
                                     start=False, stop=(g == 2), skip_group_check=True)
                            ins = e.matmul(ps[bL][hh * 64:(hh + 1) * 64, cc * 16 + s:cc * 16 + s + 1],
                                           lhsT=onesb[:, 0:64], rhs=pss_[:, col + slot:col + slot + 1],
                                           start=False, stop=(g == 2), skip_group_check=True)
                        return ins
                    P.op("pe", fpv, reads=[("zb", 0), "pss", "cA"], writes=[("ps", bO), ("ps", bL)])
            bS = nb()
            P.op("dve", lambda e: e.tensor_tensor(out=zbk[:, :, 0:N], in0=zbq[:, :, 0:N], in1=kvn[:, 0:6, 0:N], op=ALU.mult),
                 reads=["zbq"] + [("kvn", c) for c in range(6)], writes=["zbk"])
            P.op("pe", lambda e: e.matmul(ps[bS][:, 0:6 * N], lhsT=blkb, rhs=zbk[:, :, 0:N].rearrange("p c n -> p (c n)"),
                                          start=True, stop=True),
                 reads=["zbk", "cA"], writes=[("ps", bS)])
            P.op("act", lambda e: e.activation(out=sig[:, 0, 0:6 * N], in_=ps[bS][:, 0:6 * N], func=AF.Exp, scale=0.125),
                 reads=[("ps", bS)], writes=[("sig", 0)])
            P.op("dve", lambda e: e.tensor_tensor(out=sig[:, 1, 0:6 * N], in0=sig[:, 0, 0:6 * N],
                                                 in1=kvn[:, 6:12, 0:N].rearrange("p c n -> p (c n)"), op=ALU.mult),
                 reads=[("sig", 0)] + [("kvn", 6 + c) for c in range(6)], writes=[("sig", 1)])
            for cc in range(2):
                P.op("dve", lambda e, cc=cc: e.tensor_copy(out=accO[:, cc, 0:N], in_=ps[bO][:, cc * 16:cc * 16 + N]),
                     reads=[("ps", bO)], writes=[("accO", cc)])
                P.op("dve", lambda e, cc=cc: e.tensor_copy(out=accL[:, cc, 0:N], in_=ps[bL][:, cc * 16:cc * 16 + N]),
                     reads=[("ps", bL)], writes=[("accL", cc)])
                for g in range(3):
                    c = 2 * g + cc
                    P.op("dve", lambda e, cc=cc, c=c: e.tensor_tensor(out=accO[:, cc, 0:N], in0=accO[:, cc, 0:N],
                                                                     in1=sig[:, 1, c * N:(c + 1) * N], op=ALU.add),
                         reads=[("sig", 1), ("accO", cc)], writes=[("accO", cc)])
                    P.op("dve", lambda e, cc=cc, c=c: e.tensor_tensor(out=accL[:, cc, 0:N], in0=accL[:, cc, 0:N],
                                                                     in1=sig[:, 0, c * N:(c + 1) * N], op=ALU.add),
                         reads=[("sig", 0), ("accL", cc)], writes=[("accL", cc)])
                P.op("dve", lambda e, cc=cc: e.reciprocal(out=accL[:, cc, 0:N], in_=accL[:, cc, 0:N]),
                     reads=[("accL", cc)], writes=[("accL", cc)])
                P.op("dve", lambda e, cc=cc: e.tensor_tensor(out=oT[:, cc, 0:N], in0=accO[:, cc, 0:N], in1=accL[:, cc, 0:N], op=ALU.mult),
                     reads=[("accL", cc), ("accO", cc)], writes=["oT"])
            release(bO)
            release(bL)

        zbq = sb("zbq", [128, 6, TS], BF16)
        zbk = sb("zbk", [128, 6, TS], BF16)

        per = TS // 4
        for tt in range(NT):
            if tt < KTILES and KSTAGE > 0:
                tile(tt, False)
        if KSAMPLE and KSTAGE > 0:
            tile(NT, True)
        while shift_q:
            pump_shift()
        if os.environ.get("KTAGS"):
            P.trace_tags = []
        P.emit()
        if P.trace_tags is not None:
            import json
            json.dump(P.trace_tags, open(os.environ["KTAGS"], "w"))
    return nc


_CACHE = {}


def pack_wstream(w):
    L, offs, tot = slab_offsets()
    out = np.empty((128, tot), np.float32)
    for (name, col0, k0, nk), off in zip(L, offs):
        W = w[name]
        blk = W[k0 * 128:(k0 + nk) * 128, col0:col0 + 128]
        out[:, off:off + nk * 128] = blk.reshape(nk, 128, 128).transpose(1, 0, 2).reshape(128, nk * 128)
    return out


def make_in_maps(x_prompt, x_sample, state_conv, cache_win_a, cache_win_b, cache_win_c, p_prompt, p_sample,
                 w_in, g_mix, w_dw, b_dw, ln_g, ln_b, w_conv_out, w_attn_out, w_o, g_ffn, w_ffn_in, w_ffn_out,
                 g_ple, w_ple_gate, w_ple_proj, g_final, cores=range(NCORES)):
    f = lambda a: np.asarray(a, np.float32)
    w = dict(w_in=f(w_in)[0], w_conv_out=f(w_conv_out)[0], w_attn_out=f(w_attn_out)[0], w_o=f(w_o)[0],
             w_ffn_in=f(w_ffn_in)[0], w_ffn_out=f(w_ffn_out)[0], w_ple_gate=f(w_ple_gate)[0],
             w_ple_proj=f(w_ple_proj)[0])
    wstream = pack_wstream(w)
    ca, rope = build_consts()
    cf = np.zeros((128, CF_N), np.float32)
    cf[:, CF_ID:CF_ID + 128] = np.eye(128, dtype=np.float32)
    cf[:, CF_EPS] = EPS
    cf[:, CF_GMIX:CF_GMIX + 8] = colvec(f(g_mix)[0])
    cf[:, CF_GFFN:CF_GFFN + 8] = colvec(f(g_ffn)[0])
    cf[:, CF_GPLE:CF_GPLE + 8] = colvec(f(g_ple)[0])
    cf[:, CF_GFIN:CF_GFIN + 8] = colvec(f(g_final))
    cf[:, CF_BDW:CF_BDW + 8] = colvec(f(b_dw)[0])
    cf[:, CF_LNG:CF_LNG + 8] = colvec(f(ln_g)[0])
    cf[:, CF_LNB:CF_LNB + 8] = colvec(f(ln_b)[0])
    wd = f(w_dw)[0]
    cf[:, CF_WDW:CF_WDW + 248] = wd.reshape(31, 8, 128).transpose(2, 1, 0).reshape(128, 248)
    wrep = np.ascontiguousarray(np.tile(wd[0:30], (4, 1)))
    xp, xs = f(x_prompt), f(x_sample)
    pp, psm = f(p_prompt)[0], f(p_sample)[0]
    stc = f(state_conv)[0]
    cwa, cwb, cwc = f(cache_win_a)[0], f(cache_win_b)[0], f(cache_win_c)[0]
    in_maps = []
    for c in cores:
        sl = slice(c * TS, (c + 1) * TS)
        in_maps.append(dict(
            x=np.ascontiguousarray(xp[c]), p=np.ascontiguousarray(pp[c]),
            xs=np.ascontiguousarray(xs[sl, 0]), pss=np.ascontiguousarray(psm[sl, 0]),
            state=np.ascontiguousarray(stc[sl].reshape(TS * 30, D)),
            cache_a=np.ascontiguousarray(cwa[sl].reshape(TS, 128, 512)),
            cache_b=np.ascontiguousarray(cwb[sl].reshape(TS, 512, 512)),
            cache_c=np.ascontiguousarray(cwc[sl].reshape(TS, 2048, 512)),
            wstream=wstream, constA=ca, constF=cf, rope=rope, wrep=wrep))
    return in_maps


def kernel(**inputs):
    in_maps = make_in_maps(**inputs)
    if "nc" not in _CACHE:
        _CACHE["nc"] = build_program()
    nc = _CACHE["nc"]
    res = run_bass_kernel_spmd(nc, in_maps, core_ids=list(range(NCORES)))
    R = res.results
    cat = lambda k: np.stack([r[k] for r in R], axis=0)
    y_prompt = cat("y").reshape(8, S, D)
    y_sample = np.concatenate([r["ys"] for r in R], axis=0).reshape(128, 1, D)
    new_conv_prompt = cat("conv_p").reshape(1, 8, 30, D)
    nwa_p = cat("wa_p").reshape(1, 8, 128, 2, 4, 64)
    nwb_p = cat("wb_p").reshape(1, 8, 512, 2, 4, 64)
    nwc_p = cat("wc_p").reshape(1, 8, 2048, 2, 4, 64)
    new_conv_sample = np.concatenate([r["conv_s"] for r in R], axis=0).reshape(1, 128, 30, D)
    nwa_s = np.concatenate([r["wa_s"] for r in R], axis=0).reshape(1, 128, 128, 2, 4, 64)
    nwb_s = np.concatenate([r["wb_s"] for r in R], axis=0).reshape(1, 128, 512, 2, 4, 64)
    nwc_s = np.concatenate([r["wc_s"] for r in R], axis=0).reshape(1, 128, 2048, 2, 4, 64)
    return (y_prompt, y_sample, new_conv_prompt, nwa_p, nwb_p, nwc_p, new_conv_sample, nwa_s, nwb_s, nwc_s)
```

```python
import os
import numpy as np
from contextlib import ExitStack
import concourse.bass as bass
import concourse.mybir as mybir
from concourse.bass_utils import run_bass_kernel_spmd

F32 = mybir.dt.float32
BF16 = mybir.dt.bfloat16
AF = mybir.ActivationFunctionType
ALU = mybir.AluOpType
AX = mybir.AxisListType

ENGINES = ("pe", "act", "dve", "pool", "sp")
COMPUTE = ("pe", "act", "dve", "pool")

D = 1024
KC = 8
S = 2048
T = 512
NT = 4
TS = 16
DFF = 2816
FC = 22
PAST = 8192
EPS = 1e-6
NCORES = 8
RING = 5
SLOTW = 11 * 128


class Op:
    __slots__ = ("eng", "fn", "reads", "writes", "dma", "deps", "sig", "sigval",
                 "dsem", "dval", "idx", "eidx", "waits", "tag")

    def __init__(self, eng, fn, reads, writes, dma):
        self.eng = eng
        self.fn = fn
        self.reads = tuple(reads)
        self.writes = tuple(writes)
        self.dma = dma
        self.deps = []
        self.sig = False
        self.sigval = 0
        self.dsem = None
        self.dval = 0
        self.waits = []


class Prog:
    def __init__(self, nc, n_dma_sems=24):
        self.nc = nc
        self.ops = []
        self.n_dma_sems = n_dma_sems
        self.tag = ""
        self.trace_tags = None

    def op(self, eng, fn, reads=(), writes=(), dma=False):
        pr = [r for r in reads if isinstance(r, tuple) and r[0] == "ps"]
        if pr:
            reads = [r for r in reads if not (isinstance(r, tuple) and r[0] == "ps")]
            writes = list(writes) + pr
        o = Op(eng, fn, reads, writes, dma)
        o.idx = len(self.ops)
        o.tag = self.tag
        self.ops.append(o)
        return o

    def analyze(self):
        last_w = {}
        readers = {}
        for o in self.ops:
            deps = set()
            for r in o.reads:
                w = last_w.get(r)
                if w is not None:
                    deps.add(w)
            for w_ in o.writes:
                w = last_w.get(w_)
                if w is not None:
                    deps.add(w)
                for rd in readers.get(w_, ()):
                    deps.add(rd)
            deps.discard(o.idx)
            o.deps = sorted(deps)
            for r in o.reads:
                readers.setdefault(r, []).append(o.idx)
            for w_ in o.writes:
                last_w[w_] = o.idx
                readers[w_] = []
        ecount = {e: 0 for e in ENGINES}
        for o in self.ops:
            o.eidx = ecount[o.eng]
            ecount[o.eng] += 1
        dma_uses = {}
        dma_rr = {e: 0 for e in ENGINES}
        wm = {e: {c: -1 for c in COMPUTE} for e in ENGINES}
        dwm = {e: {} for e in ENGINES}
        by_eng = {e: [] for e in ENGINES}
        for o in self.ops:
            by_eng[o.eng].append(o)
        for o in self.ops:
            waits_c = {}
            waits_d = {}
            if o.dma:
                k = dma_rr[o.eng] % self.n_dma_sems
                dma_rr[o.eng] += 1
                key = (o.eng, k)
                dma_uses[key] = dma_uses.get(key, 0) + 1
                o.dsem = key
                o.dval = 16 * dma_uses[key]
                if dma_uses[key] > 1:
                    waits_d[key] = o.dval - 16
            for d in o.deps:
                p = self.ops[d]
                if p.dma:
                    waits_d[p.dsem] = max(waits_d.get(p.dsem, 0), p.dval)
                else:
                    if p.eng == o.eng and not o.dma:
                        if o.eng == "pe":
                            continue
                        if o.eidx - p.eidx > 2:
                            continue
                    waits_c[p.eng] = max(waits_c.get(p.eng, -1), p.eidx)
            o.waits = []
            for ce, ei in waits_c.items():
                if ei > wm[o.eng][ce]:
                    wm[o.eng][ce] = ei
                    o.waits.append(("c", ce, ei))
            for key, val in waits_d.items():
                if val > dwm[o.eng].get(key, 0):
                    dwm[o.eng][key] = val
                    o.waits.append(("d", key, val))
        for o in self.ops:
            for w in o.waits:
                if w[0] == "c":
                    by_eng[w[1]][w[2]].sig = True
        for e in COMPUTE:
            c = 0
            for o in by_eng[e]:
                if o.sig:
                    c += 1
                o.sigval = c
        self.by_eng = by_eng

    def emit(self):
        nc = self.nc
        self.analyze()
        by_eng = self.by_eng
        with ExitStack() as es:
            csem = {e: es.enter_context(nc.semaphore("s_" + e)) for e in COMPUTE}
            dsem = {}
            for e in ENGINES:
                if any(o.dma for o in by_eng[e]):
                    for k in range(self.n_dma_sems):
                        dsem[(e, k)] = es.enter_context(nc.semaphore("d_%s_%d" % (e, k)))
            block = es.enter_context(nc.Block())
            dma_final = {}
            for o in self.ops:
                if o.dma:
                    dma_final[o.dsem] = max(dma_final.get(o.dsem, 0), o.dval)

            def run(ename, eng):
                for o in by_eng[ename]:
                    for w in o.waits:
                        if w[0] == "c":
                            p = by_eng[w[1]][w[2]]
                            eng.wait_ge(csem[w[1]], p.sigval)
                        else:
                            eng.wait_ge(dsem[w[1]], w[2])
                    if self.trace_tags is not None and ename in ("pe",):
                        cnt_ = [0]

                        class _Px:
                            def __getattr__(s_, nm, eng=eng, cnt_=cnt_):
                                a = getattr(eng, nm)
                                if nm in ("matmul", "transpose"):
                                    def w(*aa, **kk):
                                        cnt_[0] += 1
                                        return a(*aa, **kk)
                                    return w
                                return a
                        ins = o.fn(_Px())
                        self.trace_tags.append((o.tag, cnt_[0]))
                    else:
                        ins = o.fn(eng)
                    if o.dma:
                        ins.then_inc(dsem[o.dsem], 16)
                    elif o.sig:
                        ins.then_inc(csem[ename], 1)
                if ename == "sp":
                    for key, val in dma_final.items():
                        eng.wait_ge(dsem[key], val)

            @block.tensor
            def _(pe):
                run("pe", pe)

            @block.scalar
            def _(act):
                run("act", act)

            @block.vector
            def _(dve):
                run("dve", dve)

            @block.gpsimd
            def _(pool):
                run("pool", pool)

            @block.sync
            def _(sp):
                run("sp", sp)


def slab_list():
    L = []
    for j in range(8):
        L.append(("w_in", 1024 + 128 * j, 0, 8))
        L.append(("w_in", 128 * j, 0, 8))
    for c in range(6):
        L.append(("w_in", 2048 + 128 * c, 0, 8))
    for c in range(6):
        L.append(("w_in", 2816 + 128 * c, 0, 8))
    for c in range(6):
        L.append(("w_in", 3584 + 128 * c, 0, 8))
    for j in range(8):
        L.append(("w_in", 4352 + 128 * j, 0, 8))
        L.append(("w_conv_out", 128 * j, 0, 8))
        L.append(("w_in", 5376 + 128 * j, 0, 8))
        L.append(("w_attn_out", 128 * j, 0, 2))
    for j in range(8):
        L.append(("w_o", 128 * j, 0, 8))
    for half in range(2):
        for f in range(11 * half, 11 * half + 11):
            L.append(("w_ffn_in", 128 * f, 0, 8))
            L.append(("w_ffn_in", DFF + 128 * f, 0, 8))
        for j in range(8):
            L.append(("w_ffn_out", 128 * j, 11 * half, 11))
    for j in range(8):
        L.append(("w_ple_gate", 128 * j, 0, 8))
        L.append(("w_ple_proj", 128 * j, 0, 2))
    return L


def slab_offsets():
    L = slab_list()
    offs = []
    o = 0
    for (_, _, _, nk) in L:
        offs.append(o)
        o += nk * 128
    return L, offs, o


CA_ONES, CA_RM, CA_BLK, CA_MOWN, CA_MPREV, CA_MC, CA_SEL, CA_ID = 0, 128, 256, 384, 896, 1408, 3456, 3520
CA_N = 3648
CF_ID, CF_EPS, CF_GMIX, CF_GFFN, CF_GPLE, CF_GFIN, CF_BDW, CF_LNG, CF_LNB, CF_WDW = 0, 128, 129, 137, 145, 153, 161, 169, 177, 185
CF_N = 185 + 248
NPOS = S + TS


def build_consts():
    ca = np.zeros((128, CA_N), np.float32)
    ca[:, CA_ONES:CA_ONES + 128] = 1.0
    m = np.arange(128)
    d = m % 64
    partner = np.where(d < 8, m + 8, np.where(d < 16, m - 8, m))
    rm = np.zeros((128, 128), np.float32)
    rm[partner, m] = 1.0
    ca[:, CA_RM:CA_RM + 128] = rm
    ca[:, CA_BLK:CA_BLK + 128] = (m[:, None] // 64 == m[None, :] // 64).astype(np.float32)
    own = (m[:, None] <= m[None, :]).astype(np.float32)
    prev = (m[:, None] >= m[None, :]).astype(np.float32)
    ca[:, CA_MOWN:CA_MOWN + 512] = np.tile(own, (1, 4))
    ca[:, CA_MPREV:CA_MPREV + 512] = np.tile(prev, (1, 4))
    for tt in range(4):
        mc = (m[:, None] <= (32 * tt + np.arange(32))[None, :]).astype(np.float32)
        ca[:, CA_MC + 512 * tt:CA_MC + 512 * (tt + 1)] = np.tile(mc, (1, 16))
    sel = np.zeros((128, 4, 16), np.float32)
    for g in range(4):
        for sl in range(4):
            sel[sl * 30:(sl + 1) * 30, g, 4 * g + sl] = 1.0
    ca[:, CA_SEL:CA_SEL + 64] = sel.reshape(128, 64)
    ca[:, CA_ID:CA_ID + 128] = np.eye(128, dtype=np.float32)
    half = 8
    inv_freq = (500000.0 ** (-np.arange(half, dtype=np.float32) / half)).astype(np.float32)
    pos = np.concatenate([np.arange(S), np.full(TS, PAST)]).astype(np.float32)
    ang = pos[None, :] * inv_freq[:, None]
    cos = np.cos(ang).astype(np.float32)
    sin = np.sin(ang).astype(np.float32)
    rope = np.zeros((128, 2, NPOS), np.float32)
    rope[:, 0, :] = 1.0
    for p in range(128):
        dd = p % 64
        if dd < 8:
            rope[p, 0] = cos[dd]
            rope[p, 1] = -sin[dd]
        elif dd < 16:
            rope[p, 0] = cos[dd - 8]
            rope[p, 1] = sin[dd - 8]
    return ca, rope


def colvec(v):
    return np.ascontiguousarray(np.asarray(v, np.float32).reshape(8, 128).T)


def build_program(debug=False):
    nc = bass.Bass("TRN2", target_bir_lowering=False)
    slabs, soffs, WTOT = slab_offsets()
    NSL = len(slabs)

    def din(name, shape):
        return nc.dram_tensor(name, list(shape), F32, kind="ExternalInput").ap()

    def dout(name, shape):
        return nc.dram_tensor(name, list(shape), F32, kind="ExternalOutput").ap()

    x_d = din("x", [S, D])
    p_d = din("p", [S, 256])
    xs_d = din("xs", [TS, D])
    pss_d = din("pss", [TS, 256])
    st_d = din("state", [TS * 30, D])
    ca_d = din("cache_a", [TS, 128, 512])
    cb_d = din("cache_b", [TS, 512, 512])
    cc_d = din("cache_c", [TS, 2048, 512])
    ws_d = din("wstream", [128, WTOT])
    cA_d = din("constA", [128, CA_N])
    cF_d = din("constF", [128, CF_N])
    rope_d = din("rope", [128, 2, NPOS])
    wrep_d = din("wrep", [120, D])

    y_d = dout("y", [S, D])
    ys_d = dout("ys", [TS, D])
    convp_d = dout("conv_p", [30, D])
    wap_d = dout("wa_p", [128, 512])
    wbp_d = dout("wb_p", [512, 512])
    wcp_d = dout("wc_p", [2048, 512])
    convs_d = dout("conv_s", [TS, 30 * D])
    was_d = dout("wa_s", [TS, 128 * 512])
    wbs_d = dout("wb_s", [TS, 512 * 512])
    wcs_d = dout("wc_s", [TS, 2048 * 512])
    dbg = {}

    P = Prog(nc)
    with ExitStack() as es:
        def sb(name, shape, dt):
            return es.enter_context(nc.sbuf_tensor("sb_" + name, list(shape), dt))

        xT = sb("xT", [128, 8, T], F32)
        hT = sb("hT", [128, 8, T], BF16)
        UW = 30 + T
        uT = sb("uT", [128, 8, UW], BF16)
        cT = sb("cT", [128, 8, T], BF16)
        qm = sb("qm", [128, 2, 6, T], BF16)
        kAB = sb("kAB", [128, 4, 2, T], BF16)
        kC = sb("kC", [128, 2, S], BF16)
        vAB = sb("vAB", [128, 2, 8, 256], BF16)
        vC = sb("vC", [128, 16, 256], BF16)
        accO = sb("accO", [128, 2, T], F32)
        accL = sb("accL", [128, 2, T], F32)
        oT = sb("oT", [128, 2, T], BF16)
        mg = sb("mg", [128, 8, T], BF16)
        actb = sb("actb", [128, 11, T], BF16)
        ring = sb("ring", [128, RING, SLOTW], BF16)
        diag = sb("diag", [128, 2, 31, 128], BF16)
        rope = sb("rope", [128, 2, T], F32)
        cA = sb("cA", [128, CA_N], BF16)
        cF = sb("cF", [128, CF_N], F32)
        sig = sb("sig", [128, 2, T], F32)
        zf = sb("zf", [128, 2, T], F32)
        zb = sb("zb", [128, 2, T], BF16)
        t1 = sb("t1", [128, 2, T], F32)
        rs = sb("rs", [128, 3, T], F32)
        sq = sb("sq", [128, 2, T], BF16)
        pexp = sb("pexp", [128, 2, T], BF16)
        xst = sb("xst", [128, 2, D], F32)
        pst = sb("pst", [128, 256], F32)
        pT = sb("pT", [128, 2, T], BF16)
        cst = sb("cst", [128, 2, 256], F32)
        stS = sb("stS", [120, 1, D], F32)
        prod = kC[0:120, :, :].rearrange("p a (b c) -> p (a b) c", c=D)
        wrep = rope[0:120, :, :].rearrange("p a b -> p (a b)")
        ktile = sb("ktile", [128, 2, 512], F32)
        ssc = sb("ssc", [128, 192], F32)
        pss_ = sb("pssb", [128, 192], BF16)
        qrep = sb("qrep", [128, 2, 2, 128], BF16)
        utf = sb("utf", [128, 8, 32], F32)
        kvn = sb("kvn", [128, 12, TS], F32)
        ps = [es.enter_context(nc.psum_tensor("ps%d" % i, [128, 512], F32)) for i in range(8)]

        identf = cF[:, CF_ID:CF_ID + 128]
        identb = cA[:, CA_ID:CA_ID + 128]
        onesb = cA[:, CA_ONES:CA_ONES + 128]
        rmb = cA[:, CA_RM:CA_RM + 128]
        blkb = cA[:, CA_BLK:CA_BLK + 128]
        epsc = cF[:, CF_EPS:CF_EPS + 1]

        cnt = {"bank": 0, "slab": 0, "pref": 0, "sig": 0, "zf": 0, "t1": 0, "sq": 0, "pexp": 0,
               "xst": 0, "cst": 0, "diag": 0, "zb": 0, "kt": 0, "qrep": 0, "stS": 0}

        reserved = set()

        def nb():
            while True:
                b = cnt["bank"] % 8
                cnt["bank"] += 1
                if b not in reserved:
                    return b

        def reserve():
            b = nb()
            reserved.add(b)
            return b

        def release(b):
            reserved.discard(b)

        def rot(name, n=2):
            i = cnt[name] % n
            cnt[name] += 1
            return i

        NTILES = NT + 1
        TOTSL = NSL * NTILES

        wscr = nc.dram_tensor("wscr", [128, WTOT], BF16, kind="Internal").ap()

        def prefetch():
            g = cnt["pref"]
            if g >= TOTSL:
                return
            cnt["pref"] += 1
            s = g % NSL
            slot = g % RING
            nk = slabs[s][3]
            off = soffs[s]
            if g < NSL:
                P.op("pool", lambda e, slot=slot, nk=nk, off=off: e.dma_start(
                    out=ring[:, slot, 0:nk * 128], in_=ws_d[:, off:off + nk * 128]),
                    writes=[("ring", slot)], dma=True)
                P.op("sp", lambda e, slot=slot, nk=nk, off=off: e.dma_start(
                    out=wscr[:, off:off + nk * 128], in_=ring[:, slot, 0:nk * 128]),
                    reads=[("ring", slot)], writes=[("wscr", s)], dma=True)
            else:
                P.op("pool", lambda e, slot=slot, nk=nk, off=off: e.dma_start(
                    out=ring[:, slot, 0:nk * 128], in_=wscr[:, off:off + nk * 128]),
                    reads=[("wscr", s)], writes=[("ring", slot)], dma=True)

        def take_slab():
            g = cnt["slab"]
            cnt["slab"] += 1
            return g % RING, slabs[g % NSL][3]

        def linear(rhs_fn, rhs_reads, N, fine=False):
            slot, nk = take_slab()
            b = nb()
            if fine:
                for k in range(nk):
                    P.op("pe", lambda e, slot=slot, nk=nk, b=b, k=k: e.matmul(
                        ps[b][:, 0:N], lhsT=ring[:, slot, k * 128:(k + 1) * 128], rhs=rhs_fn(k),
                        start=(k == 0), stop=(k == nk - 1)),
                        reads=[("ring", slot), rhs_reads[k]], writes=[("ps", b)])
                prefetch()
                if cnt["slab"] % 3 == 0:
                    pump_shift()
                return b

            def f(e, slot=slot, nk=nk, b=b):
                for k in range(nk):
                    ins = e.matmul(ps[b][:, 0:N], lhsT=ring[:, slot, k * 128:(k + 1) * 128],
                                   rhs=rhs_fn(k), start=(k == 0), stop=(k == nk - 1))
                return ins
            P.op("pe", f, reads=[("ring", slot)] + list(rhs_reads), writes=[("ps", b)])
            prefetch()
            if cnt["slab"] % 3 == 0:
                pump_shift()
            return b

        def rmsnorm_to_h(N, gcol0, out_fp32=None):
            bs = nb()
            for kc in range(8):
                i = rot("sq")
                P.op("act", lambda e, kc=kc, i=i: e.activation(out=sq[:, i, 0:N], in_=xT[:, kc, 0:N], func=AF.Square),
                     reads=[("xT", kc)], writes=[("sq", i)])
                P.op("pe", lambda e, kc=kc, i=i, bs=bs: e.matmul(ps[bs][:, 0:N], lhsT=onesb, rhs=sq[:, i, 0:N],
                                                                start=(kc == 0), stop=(kc == 7)),
                     reads=[("sq", i), "cA"], writes=[("ps", bs)])
            P.op("act", lambda e, bs=bs: e.activation(out=rs[:, 0, 0:N], in_=ps[bs][:, 0:N], func=AF.Ln,
                                                     scale=1.0 / D, bias=epsc),
                 reads=[("ps", bs), "cF"], writes=[("rs", 0)])
            P.op("act", lambda e: e.activation(out=rs[:, 0, 0:N], in_=rs[:, 0, 0:N], func=AF.Exp, scale=-0.5),
                 reads=[("rs", 0)], writes=[("rs", 0)])
            for kc in range(8):
                if out_fp32 is None:
                    P.op("dve", lambda e, kc=kc: e.scalar_tensor_tensor(
                        out=hT[:, kc, 0:N], in0=xT[:, kc, 0:N], scalar=cF[:, gcol0 + kc:gcol0 + kc + 1],
                        in1=rs[:, 0, 0:N], op0=ALU.mult, op1=ALU.mult),
                        reads=[("xT", kc), ("rs", 0), "cF"], writes=[("hT", kc)])
                else:
                    out_fp32(kc)

        def load_xT(src_d, row0, N):
            nblk = max(1, N // 128)
            nblk = min(nblk, int(os.environ.get("KNBLK", "99")))
            bw = min(128, N)
            for blk in range(nblk):
                i = rot("xst")
                P.op("sp", lambda e, blk=blk, i=i: e.dma_start(out=xst[0:bw, i, :],
                                                            in_=src_d[row0 + blk * 128:row0 + blk * 128 + bw, :]),
                     writes=[("xst", i)], dma=True)
                for half in range(2):
                    b = nb()

                    def f(e, i=i, b=b, half=half):
                        for q in range(4):
                            kc = half * 4 + q
                            ins = e.transpose(ps[b][:, q * 128:q * 128 + bw], xst[0:bw, i, kc * 128:(kc + 1) * 128],
                                              identf[0:bw, 0:bw])
                        return ins
                    P.op("pe", f, reads=[("xst", i), "cF"], writes=[("ps", b)])
                    for q in range(4 if not int(os.environ.get("KNOEVAC", "0")) else 0):
                        kc = half * 4 + q
                        eng = "act" if q % 2 == 0 else "dve"
                        if eng == "act":
                            P.op("act", lambda e, b=b, q=q, kc=kc, blk=blk: e.copy(
                                out=xT[:, kc, blk * 128:blk * 128 + bw], in_=ps[b][:, q * 128:q * 128 + bw]),
                                reads=[("ps", b)], writes=[("xT", kc)])
                        else:
                            P.op("dve", lambda e, b=b, q=q, kc=kc, blk=blk: e.tensor_copy(
                                out=xT[:, kc, blk * 128:blk * 128 + bw], in_=ps[b][:, q * 128:q * 128 + bw]),
                                reads=[("ps", b)], writes=[("xT", kc)])

        def store_rows(src_fn, src_reads, dst_d, row0, N, ncol_chunks, dst_col0=0):
            nblk = max(1, N // 128)
            bw = min(128, N)
            for blk in range(nblk):
                i = rot("xst")
                for c0 in range(0, ncol_chunks, 4):
                    b = nb()
                    nq = min(4, ncol_chunks - c0)

                    def f(e, b=b, c0=c0, nq=nq, blk=blk):
                        for q in range(nq):
                            ins = e.transpose(ps[b][0:bw, q * 128:(q + 1) * 128],
                                              src_fn(c0 + q)[:, blk * 128:blk * 128 + bw], identf)
                        return ins
                    P.op("pe", f, reads=list(src_reads) + ["cF"], writes=[("ps", b)])
                    eng = "act" if (c0 // 4) % 2 == 0 else "dve"
                    if eng == "act":
                        P.op("act", lambda e, b=b, c0=c0, nq=nq, i=i: e.copy(
                            out=xst[0:bw, i, c0 * 128:(c0 + nq) * 128], in_=ps[b][0:bw, 0:nq * 128]),
                            reads=[("ps", b)], writes=[("xst", i)])
                    else:
                        P.op("dve", lambda e, b=b, c0=c0, nq=nq, i=i: e.tensor_copy(
                            out=xst[0:bw, i, c0 * 128:(c0 + nq) * 128], in_=ps[b][0:bw, 0:nq * 128]),
                            reads=[("ps", b)], writes=[("xst", i)])
                P.op("act", lambda e, i=i, blk=blk: e.dma_start(
                    out=dst_d[row0 + blk * 128:row0 + blk * 128 + bw, dst_col0:dst_col0 + ncol_chunks * 128],
                    in_=xst[0:bw, i, 0:ncol_chunks * 128]),
                    reads=[("xst", i)], dma=True)

        P.op("pool", lambda e: e.dma_start(out=cA[:, :], in_=cA_d[:, :]), writes=["cA"], dma=True)
        P.op("sp", lambda e: e.dma_start(out=cF[:, :], in_=cF_d[:, :]), writes=["cF"], dma=True)
        dscr = nc.dram_tensor("dscr", [8, 128, 31 * 128], BF16, kind="Internal").ap()
        for j in range(8):
            dbuf = j % 2
            for tap in range(31):
                col = CF_WDW + j * 31 + tap
                if tap % 2 == 0:
                    P.op("dve", lambda e, tap=tap, dbuf=dbuf, col=col: e.tensor_scalar(
                        out=diag[:, dbuf, tap, :], in0=identb, scalar1=cF[:, col:col + 1], scalar2=None, op0=ALU.mult),
                        reads=["cA", "cF"], writes=[("diag", dbuf, tap)])
                else:
                    P.op("act", lambda e, tap=tap, dbuf=dbuf, col=col: e.activation(
                        out=diag[:, dbuf, tap, :], in_=identb, func=AF.Copy, scale=cF[:, col:col + 1]),
                        reads=["cA", "cF"], writes=[("diag", dbuf, tap)])
            P.op("sp", lambda e, j=j, dbuf=dbuf: e.dma_start(out=dscr[j], in_=diag[:, dbuf, :, :].rearrange("p a b -> p (a b)")),
                 reads=[("diag", dbuf, tap) for tap in range(31)], writes=[("dscr", j)], dma=True)
        for _ in range(RING):
            prefetch()
        P.op("dve", lambda e: e.memset(qm[:, :, :, :], 0.0), writes=[("qm", c) for c in range(6)])
        P.op("pool", lambda e: e.memset(kC[:, :, :], 0.0), writes=["kC"])
        P.op("pool", lambda e: e.memset(vC[:, :, :], 0.0), writes=["vC"])
        P.op("dve", lambda e: e.memset(uT[:, :, 0:30], 0.0), writes=[("uT", j) for j in range(8)])

        KSTAGE = int(os.environ.get("KSTAGE", "99"))
        KTILES = int(os.environ.get("KTILES", str(NT)))
        KSAMPLE = int(os.environ.get("KSAMPLE", "1"))
        KSHIFT = int(os.environ.get("KSHIFT", "1"))
        shift_q = []

        def shift_piece(src, dst, L, s_, k, nk):
            n = (L - 1) * 512 // nk
            P.op("act", lambda e: e.dma_start(
                out=dst[s_, k * n:(k + 1) * n].rearrange("(a e) -> a e", a=16),
                in_=src[s_, 1:L, :].rearrange("l c -> (l c)")[k * n:(k + 1) * n].rearrange("(a e) -> a e", a=16)),
                reads=[], writes=[], dma=True)

        for s_ in range(TS):
            for k in range(8):
                shift_q.append((cc_d, wcs_d, 2048, s_, k, 8))
            for k in range(2):
                shift_q.append((cb_d, wbs_d, 512, s_, k, 2))
            shift_q.append((ca_d, was_d, 128, s_, 0, 1))

        def pump_shift():
            if KSHIFT and shift_q:
                shift_piece(*shift_q.pop(0))


        def finish_tile(sample, t0, N):
            if int(os.environ.get("KNOSTORE", "0")):
                return
            store_rows(lambda c: xT[:, c, 0:N], [("xT", kc) for kc in range(8)], ys_d if sample else y_d,
                       0 if sample else t0, N, 8)

        def tile(tt, sample):
            N = TS if sample else T
            t0 = S if sample else tt * T
            par = tt % 2
            last = (tt == NT - 1) and not sample
            if not int(os.environ.get("KNOROPE", "0")):
                P.op("sp", lambda e: e.dma_start(out=rope[:, :, 0:N], in_=rope_d[:, :, t0:t0 + N]),
                     writes=["rope"], dma=True)
            P.tag = "%dA" % tt
            if sample:
                load_xT(xs_d, 0, N)
            else:
                load_xT(x_d, t0, N)
            if KSTAGE <= 1:
                return finish_tile(sample, t0, N)
            rmsnorm_to_h(N, CF_GMIX)
            nblk = max(1, N // 128)
            bw = min(128, N)
            psrc = pss_d if sample else p_d
            prow0 = 0 if sample else t0
            for blk in range(nblk):
                P.op("sp", lambda e, blk=blk: e.dma_start(out=pst[0:bw, :], in_=psrc[prow0 + blk * 128:prow0 + blk * 128 + bw, :]),
                     writes=["pst"], dma=True)
                b = nb()

                def f(e, b=b):
                    for cc in range(2):
                        ins = e.transpose(ps[b][:, cc * 128:cc * 128 + bw], pst[0:bw, cc * 128:(cc + 1) * 128], identf[0:bw, 0:bw])
                    return ins
                P.op("pe", f, reads=["pst", "cF"], writes=[("ps", b)])
                P.op("act", lambda e, b=b, blk=blk: e.copy(
                    out=pT[:, :, blk * 128:blk * 128 + bw], in_=ps[b][:, 0:256].rearrange("p (c n) -> p c n", c=2)[:, :, 0:bw]),
                    reads=[("ps", b)], writes=["pT"])

            hreads = [("hT", kc) for kc in range(8)]
            hfn = lambda k: hT[:, k, 0:N]
            P.tag = "%dB1" % tt
            for j in range(8):
                bb = linear(hfn, hreads, N, fine=(j == 0))
                i = rot("sig")
                P.op("act", lambda e, bb=bb, i=i: e.activation(out=sig[:, i, 0:N], in_=ps[bb][:, 0:N], func=AF.Sigmoid),
                     reads=[("ps", bb)], writes=[("sig", i)])
                ba = linear(hfn, hreads, N)
                if sample:
                    P.op("dve", lambda e, ba=ba, i=i, j=j: e.tensor_tensor(
                        out=utf[:, j, 0:N], in0=ps[ba][:, 0:N], in1=sig[:, i, 0:N], op=ALU.mult),
                        reads=[("ps", ba), ("sig", i)], writes=[("utf", j)])
                else:
                    P.op("dve", lambda e, ba=ba, i=i, j=j: e.tensor_tensor(
                        out=uT[:, j, 30:30 + N], in0=ps[ba][:, 0:N], in1=sig[:, i, 0:N], op=ALU.mult),
                        reads=[("ps", ba), ("sig", i)], writes=[("uT", j)])
                    if last:
                        P.op("dve", lambda e, ba=ba, i=i, j=j: e.tensor_tensor(
                            out=utf[:, j, 0:32], in0=ps[ba][:, N - 32:N], in1=sig[:, i, N - 32:N], op=ALU.mult),
                            reads=[("ps", ba), ("sig", i)], writes=[("utf", j)])
            if KSTAGE <= 2:
                return finish_tile(sample, t0, N)
            if last:
                store_rows(lambda c: utf[:, c, 2:32], [("utf", j) for j in range(8)], convp_d, 0, 30, 8)
            P.tag = "%dB2" % tt
            Cc = rope[:, 0, 0:N]
            Ss = rope[:, 1, 0:N]

            def roped(bz):
                i = rot("zf")
                ib = rot("zb")
                P.op("act", lambda e: e.copy(out=zb[:, ib, 0:N], in_=ps[bz][:, 0:N]),
                     reads=[("ps", bz)], writes=[("zb", ib)])
                br = nb()
                P.op("pe", lambda e: e.matmul(ps[br][:, 0:N], lhsT=rmb, rhs=zb[:, ib, 0:N], start=True, stop=True),
                     reads=[("zb", ib), "cA"], writes=[("ps", br)])
                P.op("dve", lambda e: e.tensor_tensor(out=zf[:, i, 0:N], in0=ps[bz][:, 0:N], in1=Cc, op=ALU.mult),
                     reads=[("ps", bz), "rope"], writes=[("zf", i)])
                it = rot("t1")
                P.op("dve", lambda e: e.tensor_tensor(out=t1[:, it, 0:N], in0=ps[br][:, 0:N], in1=Ss, op=ALU.mult),
                     reads=[("ps", br), "rope"], writes=[("t1", it)])
                P.op("dve", lambda e: e.tensor_tensor(out=zf[:, i, 0:N], in0=zf[:, i, 0:N], in1=t1[:, it, 0:N], op=ALU.add),
                     reads=[("zf", i), ("t1", it)], writes=[("zf", i)])
                return i

            def cache_rows_out(src_ap_fn, src_reads, g, kv, qb_list):
                dst, L = [(wap_d, 128), (wbp_d, 512), (wcp_d, 2048)][g]
                for qb in qb_list:
                    tok0 = t0 + qb * 128
                    r0 = tok0 - (S - L)
                    if r0 < 0:
                        continue
                    ic = rot("cst")
                    b = nb()

                    def f(e, b=b, qb=qb):
                        for cc in range(2):
                            ins = e.transpose(ps[b][:, cc * 128:(cc + 1) * 128],
                                              src_ap_fn(cc)[:, qb * 128:(qb + 1) * 128], identf)
                        return ins
                    P.op("pe", f, reads=list(src_reads) + ["cF"], writes=[("ps", b)])
                    P.op("act", lambda e, b=b, ic=ic: e.copy(out=cst[:, ic, 0:256], in_=ps[b][:, 0:256]),
                         reads=[("ps", b)], writes=[("cst", ic)])
                    P.op("sp", lambda e, ic=ic, r0=r0, dst=dst, kv=kv: e.dma_start(
                        out=dst[r0:r0 + 128, kv * 256:(kv + 1) * 256], in_=cst[:, ic, 0:256]),
                        reads=[("cst", ic)], dma=True)

            for c in range(6):
                bz = linear(hfn, hreads, N)
                i = roped(bz)
                if sample:
                    P.op("act", lambda e, i=i, c=c: e.copy(out=zbq[:, c, 0:N], in_=zf[:, i, 0:N]),
                         reads=[("zf", i)], writes=["zbq"])
                    continue
                P.op("act", lambda e, i=i, c=c: e.copy(out=qm[0:64, 0, c, 0:N], in_=zf[0:64, i, 0:N]),
                     reads=[("zf", i)], writes=[("qm", c)])
                P.op("dve", lambda e, i=i, c=c: e.tensor_copy(out=qm[64:128, 1, c, 0:N], in_=zf[64:128, i, 0:N]),
                     reads=[("zf", i)], writes=[("qm", c)])
            kpair = {}
            for c in range(6):
                bz = linear(hfn, hreads, N)
                i = roped(bz)
                g = c // 2
                if sample:
                    P.op("act", lambda e, i=i, c=c: e.copy(out=kvn[:, c, 0:N], in_=zf[:, i, 0:N]),
                         reads=[("zf", i)], writes=[("kvn", c)])
                else:
                    if g < 2:
                        P.op("act", lambda e, i=i, c=c: e.copy(out=kAB[:, c, par, 0:N], in_=zf[:, i, 0:N]),
                             reads=[("zf", i)], writes=["kAB"])
                    else:
                        P.op("act", lambda e, i=i, c=c: e.copy(out=kC[:, c - 4, t0:t0 + N], in_=zf[:, i, 0:N]),
                             reads=[("zf", i)], writes=["kC"])
                    dst, L = [(wap_d, 128), (wbp_d, 512), (wcp_d, 2048)][g]
                    for qb in range(4):
                        r0 = t0 + qb * 128 - (S - L)
                        if r0 < 0:
                            continue
                        ic = rot("cst")
                        b = nb()
                        P.op("pe", lambda e, b=b, i=i, qb=qb: e.transpose(ps[b][:, 0:128], zf[:, i, qb * 128:(qb + 1) * 128], identf),
                             reads=[("zf", i), "cF"], writes=[("ps", b)])
                        P.op("act", lambda e, b=b, ic=ic: e.copy(out=cst[:, ic, 0:128], in_=ps[b][:, 0:128]),
                             reads=[("ps", b)], writes=[("cst", ic)])
                        P.op("act", lambda e, ic=ic, r0=r0, dst=dst, c=c: e.dma_start(
                            out=dst[r0:r0 + 128, (c % 2) * 128:(c % 2) * 128 + 128], in_=cst[:, ic, 0:128]),
                            reads=[("cst", ic)], dma=True)
            for c in range(6):
                bz = linear(hfn, hreads, N)
                g = c // 2
                cc = c % 2
                i = rot("zf")
                P.op("act", lambda e, bz=bz, i=i: e.copy(out=zf[:, i, 0:N], in_=ps[bz][:, 0:N]),
                     reads=[("ps", bz)], writes=[("zf", i)])
                if sample:
                    P.op("dve", lambda e, i=i, c=c: e.tensor_copy(out=kvn[:, 6 + c, 0:N], in_=zf[:, i, 0:N]),
                         reads=[("zf", i)], writes=[("kvn", 6 + c)])
                    continue
                dst, L = [(wap_d, 128), (wbp_d, 512), (wcp_d, 2048)][g]
                for qb in range(4):
                    r0 = t0 + qb * 128 - (S - L)
                    if r0 < 0 and g != 0:
                        continue
                    b = nb()
                    P.op("pe", lambda e, b=b, i=i, qb=qb: e.transpose(ps[b][:, 0:128], zf[:, i, qb * 128:(qb + 1) * 128], identf),
                         reads=[("zf", i), "cF"], writes=[("ps", b)])
                    if g == 0:
                        P.op("dve", lambda e, b=b, qb=qb, cc=cc: e.tensor_copy(
                            out=vAB[:, par, qb, cc * 128:(cc + 1) * 128], in_=ps[b][:, 0:128]),
                            reads=[("ps", b)], writes=["vAB"])
                    if r0 >= 0:
                        ic = rot("cst")
                        P.op("act", lambda e, b=b, ic=ic: e.copy(out=cst[:, ic, 0:128], in_=ps[b][:, 0:128]),
                             reads=[("ps", b)], writes=[("cst", ic)])
                        P.op("act", lambda e, ic=ic, r0=r0, dst=dst, cc=cc: e.dma_start(
                            out=dst[r0:r0 + 128, 256 + cc * 128:256 + cc * 128 + 128], in_=cst[:, ic, 0:128]),
                            reads=[("cst", ic)], dma=True)
                if g == 1:
                    for r in range(4):
                        b = nb()
                        P.op("pe", lambda e, b=b, i=i, r=r: e.transpose(ps[b][:, 0:128], zf[:, i, r:T:4], identf),
                             reads=[("zf", i), "cF"], writes=[("ps", b)])
                        P.op("dve", lambda e, b=b, r=r, cc=cc: e.tensor_copy(
                            out=vAB[:, par, 4 + r, cc * 128:(cc + 1) * 128], in_=ps[b][:, 0:128]),
                            reads=[("ps", b)], writes=["vAB"])
                if g == 2:
                    for r0_ in range(0, 16, 4):
                        b = nb()

                        def f(e, b=b, i=i, r0_=r0_):
                            for q in range(4):
                                ins = e.transpose(ps[b][0:32, q * 128:(q + 1) * 128], zf[:, i, (r0_ + q):T:16], identf)
                            return ins
                        P.op("pe", f, reads=[("zf", i), "cF"], writes=[("ps", b)])
                        P.op("dve", lambda e, b=b, r0_=r0_, cc=cc: e.tensor_copy(
                            out=vC[32 * tt:32 * tt + 32, r0_:r0_ + 4, cc * 128:(cc + 1) * 128],
                            in_=ps[b][0:32, 0:512].rearrange("p (q c) -> p q c", q=4)),
                            reads=[("ps", b)], writes=["vC"])

            if KSTAGE <= 3:
                return finish_tile(sample, t0, N)
            P.tag = "%dC" % tt
            attn_gen = None
            if not sample and KSTAGE > 4:
                attn_gen = prompt_attention(tt, N)

            def attn_step(n):
                nonlocal attn_gen
                for _ in range(n):
                    if attn_gen is None:
                        return
                    try:
                        next(attn_gen)
                    except StopIteration:
                        attn_gen = None
            if not sample:
                for j in range(8):
                    P.tag = "%dC" % tt
                    dbuf = rot("diag")
                    P.op("sp", lambda e, j=j, dbuf=dbuf: e.dma_start(
                        out=diag[:, dbuf, :, :].rearrange("p a b -> p (a b)"), in_=dscr[j]),
                        reads=[("dscr", j)], writes=[("diag", dbuf, tap) for tap in range(31)], dma=True)
                    b = nb()

                    def f(e, j=j, dbuf=dbuf, b=b):
                        for tap in range(31):
                            ins = e.matmul(ps[b][:, 0:N], lhsT=diag[:, dbuf, tap, :], rhs=uT[:, j, tap:tap + N],
                                           start=(tap == 0), stop=(tap == 30))
                        return ins
                    P.op("pe", f, reads=[("diag", dbuf, tap) for tap in range(31)] + [("uT", j)], writes=[("ps", b)])
                    conv_epilogue(j, b, N)
                    P.tag = "%dD" % tt
                    attn_step(3)
                P.tag = "%dC" % tt
                if not last:
                    P.op("dve", lambda e: e.tensor_copy(out=uT[:, :, 0:30], in_=uT[:, :, T:T + 30]),
                         reads=[("uT", j) for j in range(8)], writes=[("uT", j) for j in range(8)])
            else:
                sample_conv(N)
            P.tag = "%dD" % tt
            attn_step(1000)
            P.tag = "%dC" % tt
            ln_finish(N)

            if KSTAGE <= 4:
                return finish_tile(sample, t0, N)
            P.tag = "%dD" % tt
            if sample:
                sample_attention(N)
            else:
                attn_step(1000)

            if KSTAGE <= 5:
                return finish_tile(sample, t0, N)
            P.tag = "%dE" % tt
            cTreads = [("cT", kc) for kc in range(8)]
            for j in range(8):
                bgA = linear(hfn, hreads, N)
                i = rot("sig")
                P.op("act", lambda e, bgA=bgA, i=i: e.activation(out=sig[:, i, 0:N], in_=ps[bgA][:, 0:N], func=AF.Sigmoid),
                     reads=[("ps", bgA)], writes=[("sig", i)])
                bcv = linear(lambda k: cT[:, k, 0:N], cTreads, N)
                it = rot("t1")
                P.op("dve", lambda e, bcv=bcv, i=i, it=it: e.tensor_tensor(
                    out=t1[:, it, 0:N], in0=ps[bcv][:, 0:N], in1=sig[:, i, 0:N], op=ALU.mult),
                    reads=[("ps", bcv), ("sig", i)], writes=[("t1", it)])
                bgB = linear(hfn, hreads, N)
                i2 = rot("sig")
                P.op("act", lambda e, bgB=bgB, i2=i2: e.activation(out=sig[:, i2, 0:N], in_=ps[bgB][:, 0:N], func=AF.Sigmoid),
                     reads=[("ps", bgB)], writes=[("sig", i2)])
                bat = linear(lambda k: oT[:, k, 0:N], ["oT"], N)
                P.op("dve", lambda e, bat=bat, i2=i2: e.tensor_tensor(
                    out=sig[:, i2, 0:N], in0=ps[bat][:, 0:N], in1=sig[:, i2, 0:N], op=ALU.mult),
                    reads=[("ps", bat), ("sig", i2)], writes=[("sig", i2)])
                P.op("dve", lambda e, i2=i2, it=it, j=j: e.tensor_tensor(
                    out=mg[:, j, 0:N], in0=sig[:, i2, 0:N], in1=t1[:, it, 0:N], op=ALU.add),
                    reads=[("sig", i2), ("t1", it)], writes=[("mg", j)])
            P.tag = "%dF" % tt
            mreads = [("mg", kc) for kc in range(8)]
            for j in range(8):
                b = linear(lambda k: mg[:, k, 0:N], mreads, N)
                P.op("dve", lambda e, b=b, j=j: e.tensor_tensor(out=xT[:, j, 0:N], in0=ps[b][:, 0:N], in1=xT[:, j, 0:N], op=ALU.add),
                     reads=[("ps", b), ("xT", j)], writes=[("xT", j)])
            P.tag = "%dG" % tt
            rmsnorm_to_h(N, CF_GFFN)
            for half in range(2):
                for f_ in range(11):
                    bg = linear(hfn, hreads, N, fine=(half == 0 and f_ == 0))
                    i = rot("sig")
                    P.op("act", lambda e, bg=bg, i=i: e.activation(out=sig[:, i, 0:N], in_=ps[bg][:, 0:N], func=AF.Silu),
                         reads=[("ps", bg)], writes=[("sig", i)])
                    bu = linear(hfn, hreads, N)
                    P.op("dve", lambda e, bu=bu, i=i, f_=f_: e.tensor_tensor(
                        out=actb[:, f_, 0:N], in0=ps[bu][:, 0:N], in1=sig[:, i, 0:N], op=ALU.mult),
                        reads=[("ps", bu), ("sig", i)], writes=[("actb", f_)])
                areads = [("actb", f_) for f_ in range(11)]
                for j in range(8):
                    b = linear(lambda k: actb[:, k, 0:N], areads, N)
                    P.op("dve", lambda e, b=b, j=j: e.tensor_tensor(out=xT[:, j, 0:N], in0=ps[b][:, 0:N], in1=xT[:, j, 0:N], op=ALU.add),
                         reads=[("ps", b), ("xT", j)], writes=[("xT", j)])
            P.tag = "%dH" % tt
            rmsnorm_to_h(N, CF_GPLE)
            for j in range(8):
                bg = linear(hfn, hreads, N, fine=(j == 0))
                i = rot("sig")
                P.op("act", lambda e, bg=bg, i=i: e.activation(out=sig[:, i, 0:N], in_=ps[bg][:, 0:N], func=AF.Sigmoid),
                     reads=[("ps", bg)], writes=[("sig", i)])
                bp = linear(lambda k: pT[:, k, 0:N], ["pT"], N)
                P.op("dve", lambda e, bp=bp, i=i: e.tensor_tensor(out=sig[:, i, 0:N], in0=ps[bp][:, 0:N], in1=sig[:, i, 0:N], op=ALU.mult),
                     reads=[("ps", bp), ("sig", i)], writes=[("sig", i)])
                P.op("dve", lambda e, i=i, j=j: e.tensor_tensor(out=xT[:, j, 0:N], in0=sig[:, i, 0:N], in1=xT[:, j, 0:N], op=ALU.add),
                     reads=[("sig", i), ("xT", j)], writes=[("xT", j)])
            P.tag = "%dI" % tt
            def fin(kc):
                P.op("dve", lambda e, kc=kc: e.scalar_tensor_tensor(
                    out=xT[:, kc, 0:N], in0=xT[:, kc, 0:N], scalar=cF[:, CF_GFIN + kc:CF_GFIN + kc + 1],
                    in1=rs[:, 0, 0:N], op0=ALU.mult, op1=ALU.mult),
                    reads=[("xT", kc), ("rs", 0), "cF"], writes=[("xT", kc)])
            rmsnorm_to_h(N, CF_GFIN, out_fp32=fin)
            store_rows(lambda c: xT[:, c, 0:N], [("xT", kc) for kc in range(8)], ys_d if sample else y_d,
                       0 if sample else t0, N, 8)

        lnb = {}

        def conv_epilogue(j, b, N, src_is_sbuf=None):
            if j == 0:
                lnb["sum"] = reserve()
                lnb["sq"] = reserve()
            src = ps[b][:, 0:N] if src_is_sbuf is None else src_is_sbuf
            rd = [("ps", b)] if src_is_sbuf is None else [("t1", 0)]
            P.op("act", lambda e: e.activation(out=cT[:, j, 0:N], in_=src, func=AF.Identity,
                                              bias=cF[:, CF_BDW + j:CF_BDW + j + 1]),
                 reads=rd + ["cF"], writes=[("cT", j)])
            i = rot("sq")
            P.op("act", lambda e: e.activation(out=sq[:, i, 0:N], in_=src, func=AF.Square,
                                              bias=cF[:, CF_BDW + j:CF_BDW + j + 1]),
                 reads=rd + ["cF"], writes=[("sq", i)])
            bs, bq = lnb["sum"], lnb["sq"]
            P.op("pe", lambda e: e.matmul(ps[bs][:, 0:N], lhsT=onesb, rhs=cT[:, j, 0:N], start=(j == 0), stop=(j == 7)),
                 reads=[("cT", j), "cA"], writes=[("ps", bs)])
            P.op("pe", lambda e: e.matmul(ps[bq][:, 0:N], lhsT=onesb, rhs=sq[:, i, 0:N], start=(j == 0), stop=(j == 7)),
                 reads=[("sq", i), "cA"], writes=[("ps", bq)])

        def ln_finish(N):
            bs, bq = lnb["sum"], lnb["sq"]
            P.op("act", lambda e: e.mul(out=rs[:, 1, 0:N], in_=ps[bs][:, 0:N], mul=1.0 / D),
                 reads=[("ps", bs)], writes=[("rs", 1)])
            P.op("dve", lambda e: e.tensor_tensor(out=rs[:, 2, 0:N], in0=rs[:, 1, 0:N], in1=rs[:, 1, 0:N], op=ALU.mult),
                 reads=[("rs", 1)], writes=[("rs", 2)])
            P.op("dve", lambda e: e.scalar_tensor_tensor(out=rs[:, 2, 0:N], in0=ps[bq][:, 0:N], scalar=1.0 / D,
                                                        in1=rs[:, 2, 0:N], op0=ALU.mult, op1=ALU.subtract),
                 reads=[("ps", bq), ("rs", 2)], writes=[("rs", 2)])
            P.op("act", lambda e: e.activation(out=rs[:, 2, 0:N], in_=rs[:, 2, 0:N], func=AF.Ln, bias=epsc),
                 reads=[("rs", 2), "cF"], writes=[("rs", 2)])
            P.op("act", lambda e: e.activation(out=rs[:, 2, 0:N], in_=rs[:, 2, 0:N], func=AF.Exp, scale=-0.5),
                 reads=[("rs", 2)], writes=[("rs", 2)])
            for j in range(8):
                it = rot("t1")
                P.op("dve", lambda e, j=j, it=it: e.tensor_tensor(out=t1[:, it, 0:N], in0=cT[:, j, 0:N], in1=rs[:, 1, 0:N], op=ALU.subtract),
                     reads=[("cT", j), ("rs", 1)], writes=[("t1", it)])
                P.op("dve", lambda e, j=j, it=it: e.tensor_tensor(out=t1[:, it, 0:N], in0=t1[:, it, 0:N], in1=rs[:, 2, 0:N], op=ALU.mult),
                     reads=[("t1", it), ("rs", 2)], writes=[("t1", it)])
                P.op("act", lambda e, j=j, it=it: e.activation(out=cT[:, j, 0:N], in_=t1[:, it, 0:N], func=AF.Silu,
                                                             scale=cF[:, CF_LNG + j:CF_LNG + j + 1],
                                                             bias=cF[:, CF_LNB + j:CF_LNB + j + 1]),
                     reads=[("t1", it), "cF"], writes=[("cT", j)])
            release(bs)
            release(bq)

        def prompt_attention(tt, N):
            first_write = {0: True, 1: True}
            for g in range(3):
                nsb = 16 if g == 2 else 4
                wsb = 32 if g == 2 else 128
                for cc in range(2):
                    c = 2 * g + cc
                    bO = reserve()
                    bL = reserve()
                    P.op("dve", lambda e, bO=bO: e.memset(ps[bO][:, :], 0.0), writes=[("ps", bO)])
                    P.op("dve", lambda e, bL=bL: e.memset(ps[bL][:, :], 0.0), writes=[("ps", bL)])
                    started = set()
                    pending = [None]
                    for hh in range(2):
                        for ksel in (0, 1):
                            if g == 2 and ksel == 1:
                                continue
                            if g == 1 and ksel == 1 and tt == 0:
                                continue
                            sc = []
                            for sbk in range(nsb):
                                if g == 0:
                                    B = 4 * tt + sbk - ksel
                                    if B < 0:
                                        continue
                                    kt = kAB[:, c, (B // 4) % 2, (B % 4) * 128:(B % 4) * 128 + 128]
                                    qv = qm[:, hh, c, sbk * 128:(sbk + 1) * 128]
                                    vt = vAB[:, (B // 4) % 2, B % 4, cc * 128 + hh * 64:cc * 128 + hh * 64 + 64]
                                elif g == 1:
                                    kp = (tt - ksel) % 2
                                    kt = kAB[:, c, kp, sbk:T:4]
                                    qv = qm[:, hh, c, sbk:T:4]
                                    vt = vAB[:, kp, 4 + sbk, cc * 128 + hh * 64:cc * 128 + hh * 64 + 64]
                                else:
                                    kt = kC[:, cc, sbk:S:16]
                                    qv = qm[:, hh, c, sbk:T:16]
                                    vt = vC[:, sbk, cc * 128 + hh * 64:cc * 128 + hh * 64 + 64]
                                sc.append((sbk, kt, qv, vt))
                            if not sc:
                                continue
                            bsc = nb()

                            def fsc(e, sc=sc, bsc=bsc, wsb=wsb):
                                for (sbk, kt, qv, vt) in sc:
                                    ins = e.matmul(ps[bsc][:, sbk * wsb:(sbk + 1) * wsb], lhsT=kt, rhs=qv, start=True, stop=True)
                                return ins
                            P.op("pe", fsc, reads=["kAB", "kC", ("qm", c)], writes=[("ps", bsc)])
                            ip = rot("pexp")
                            P.op("act", lambda e, bsc=bsc, ip=ip: e.activation(out=pexp[:, ip, 0:512], in_=ps[bsc][:, 0:512],
                                                                             func=AF.Exp, scale=0.125),
                                 reads=[("ps", bsc)], writes=[("pexp", ip)])
                            if g == 2:
                                mcol = CA_MC + 512 * tt
                            else:
                                mcol = CA_MOWN if ksel == 0 else CA_MPREV
                            P.op("dve", lambda e, ip=ip, mcol=mcol: e.tensor_tensor(
                                out=pexp[:, ip, 0:512], in0=pexp[:, ip, 0:512], in1=cA[:, mcol:mcol + 512], op=ALU.mult),
                                reads=[("pexp", ip), "cA"], writes=[("pexp", ip)])
                            pv = []
                            for (sbk, kt, qv, vt) in sc:
                                key = (hh, sbk)
                                st = key not in started
                                started.add(key)
                                if g == 2:
                                    fin_ = True
                                elif g == 1:
                                    fin_ = (ksel == 1) or tt == 0
                                else:
                                    fin_ = (ksel == 1) or (4 * tt + sbk - 1 < 0)
                                pv.append((sbk, vt, st, fin_))

                            def fpv(e, pv=pv, hh=hh, ip=ip, bO=bO, bL=bL, wsb=wsb):
                                for (sbk, vt, st, fin_) in pv:
                                    e.matmul(ps[bO][hh * 64:(hh + 1) * 64, sbk * wsb:(sbk + 1) * wsb], lhsT=vt,
                                             rhs=pexp[:, ip, sbk * wsb:(sbk + 1) * wsb], start=False, stop=fin_,
                                             skip_group_check=True)
                                    ins = e.matmul(ps[bL][hh * 64:(hh + 1) * 64, sbk * wsb:(sbk + 1) * wsb],
                                                   lhsT=onesb[:, 0:64],
                                                   rhs=pexp[:, ip, sbk * wsb:(sbk + 1) * wsb], start=False, stop=fin_,
                                                   skip_group_check=True)
                                return ins
                            if pending[0] is not None:
                                pending[0]()
                            pending[0] = (lambda fpv=fpv, ip=ip, bO=bO, bL=bL: P.op(
                                "pe", fpv, reads=[("pexp", ip), "vAB", "vC", "cA"], writes=[("ps", bO), ("ps", bL)]))
                            yield
                    if pending[0] is not None:
                        pending[0]()
                        pending[0] = None
                    if g == 0:
                        dO = accO[:, cc, 0:N]
                        dL = accL[:, cc, 0:N]
                        P.op("act", lambda e, dO=dO, bO=bO: e.copy(out=dO, in_=ps[bO][:, 0:N]),
                             reads=[("ps", bO)], writes=[("accO", cc)])
                        P.op("dve", lambda e, dL=dL, bL=bL: e.tensor_copy(out=dL, in_=ps[bL][:, 0:N]),
                             reads=[("ps", bL)], writes=[("accL", cc)])
                    else:
                        accumulate_v(bO, bL, cc, g, N, first_write)
                    release(bO)
                    release(bL)
                    yield
            for cc in range(2):
                P.op("act", lambda e, cc=cc: e.activation(out=accL[:, cc, 0:N], in_=accL[:, cc, 0:N], func=AF.Ln),
                     reads=[("accL", cc)], writes=[("accL", cc)])
                P.op("act", lambda e, cc=cc: e.activation(out=accL[:, cc, 0:N], in_=accL[:, cc, 0:N], func=AF.Exp, scale=-1.0),
                     reads=[("accL", cc)], writes=[("accL", cc)])
                P.op("dve", lambda e, cc=cc: e.tensor_tensor(out=oT[:, cc, 0:N], in0=accO[:, cc, 0:N], in1=accL[:, cc, 0:N], op=ALU.mult),
                     reads=[("accL", cc), ("accO", cc)], writes=["oT"])

        def accumulate_v(bO, bL, cc, g, N, first_write):
            r = 4 if g == 1 else 16
            m = N // r
            dO = accO[:, cc, 0:N].rearrange("p (m r) -> p r m", r=r)
            dL = accL[:, cc, 0:N].rearrange("p (m r) -> p r m", r=r)
            sO = ps[bO][:, 0:N].rearrange("p (r m) -> p r m", r=r)
            sL = ps[bL][:, 0:N].rearrange("p (r m) -> p r m", r=r)
            P.op("dve", lambda e: e.tensor_tensor(out=dO, in0=sO, in1=dO, op=ALU.add),
                 reads=[("ps", bO), ("accO", cc)], writes=[("accO", cc)])
            P.op("dve", lambda e: e.tensor_tensor(out=dL, in0=sL, in1=dL, op=ALU.add),
                 reads=[("ps", bL), ("accL", cc)], writes=[("accL", cc)])

        def sample_conv(N):
            P.op("sp", lambda e: e.dma_start(out=wrep, in_=wrep_d[:, :]), writes=["rope"], dma=True)
            for g4 in range(4):
                i = 0
                P.op("sp", lambda e, g4=g4, i=i: e.dma_start(out=stS[:, i, :], in_=st_d[g4 * 120:(g4 + 1) * 120, :]),
                     writes=[("stS", i)], dma=True)
                P.op("dve", lambda e, g4=g4, i=i: e.tensor_tensor(out=prod[:, g4, :], in0=stS[:, i, :], in1=wrep[:, :], op=ALU.mult),
                     reads=[("stS", i), "rope"], writes=["kC"])
            for j in range(8):
                b = nb()

                def f(e, j=j, b=b):
                    for g4 in range(4):
                        ins = e.matmul(ps[b][:, 0:N], lhsT=prod[:, g4, j * 128:(j + 1) * 128],
                                       rhs=cA[0:120, CA_SEL + 16 * g4:CA_SEL + 16 * g4 + 16], start=(g4 == 0), stop=(g4 == 3))
                    return ins
                P.op("pe", f, reads=["kC", "cA"], writes=[("ps", b)])
                P.op("dve", lambda e, j=j, b=b: e.scalar_tensor_tensor(
                    out=t1[:, 0, 0:N], in0=utf[:, j, 0:N], scalar=cF[:, CF_WDW + j * 31 + 30:CF_WDW + j * 31 + 31],
                    in1=ps[b][:, 0:N], op0=ALU.mult, op1=ALU.add),
                    reads=[("utf", j), ("ps", b), "cF"], writes=[("t1", 0)])
                conv_epilogue(j, b, N, src_is_sbuf=t1[:, 0, 0:N])
            P.op("sp", lambda e: e.dma_start(out=convs_d[:, 0:29 * D],
                                            in_=st_d.rearrange("(s j) c -> s (j c)", j=30)[:, D:30 * D]),
                 dma=True)
            store_rows(lambda c: utf[:, c, 0:N], [("utf", j) for j in range(8)], convs_d, 0, N, 8, dst_col0=29 * D)

        def sample_attention(N):
            caches = [(ca_d, 128, 1), (cb_d, 512, 4), (cc_d, 2048, 16)]
            outs = [was_d, wbs_d, wcs_d]
            for g in range(3):
                L = caches[g][1]
                for kv in range(2):
                    store_rows(lambda c, g=g, kv=kv: kvn[:, kv * 6 + 2 * g + c, 0:N],
                               [("kvn", kv * 6 + 2 * g + c) for c in range(2)], outs[g], 0, N, 2,
                               dst_col0=(L - 1) * 512 + kv * 256)
            bO = reserve()
            bL = reserve()
            P.op("dve", lambda e: e.memset(ps[bO][:, :], 0.0), writes=[("ps", bO)])
            P.op("dve", lambda e: e.memset(ps[bL][:, :], 0.0), writes=[("ps", bL)])
            for s in range(N):
                for g in range(3):
                    src, L, dil = caches[g]
                    ik = rot("kt")
                    P.op("sp", lambda e, s=s, src=src, L=L, dil=dil, ik=ik: e.dma_start(
                        out=ktile[:, ik, :], in_=src[s, 0:L:dil, :]), writes=[("kt", ik)], dma=True)
                    iq = rot("qrep")
                    bq = nb()
                    for cc in range(2):
                        c = 2 * g + cc
                        P.op("dve", lambda e, c=c, cc=cc, s=s, iq=iq: e.tensor_tensor(
                            out=qrep[:, iq, cc, :], in0=identb, in1=zbq[:, c, s:s + 1].to_broadcast([128, 128]), op=ALU.mult),
                            reads=["zbq", "cA"], writes=[("qrep", iq, cc)])
                        P.op("pe", lambda e, cc=cc, bq=bq, iq=iq: e.matmul(ps[bq][:, cc * 128:(cc + 1) * 128], lhsT=onesb, rhs=qrep[:, iq, cc, :],
                                                                  start=True, stop=True),
                             reads=[("qrep", iq, cc), "cA"], writes=[("ps", bq)])
                    it = rot("t1")
                    P.op("dve", lambda e, ik=ik, bq=bq, it=it: e.tensor_tensor(
                        out=t1[:, it, 0:256], in0=ktile[:, ik, 0:256], in1=ps[bq][:, 0:256], op=ALU.mult),
                        reads=[("kt", ik), ("ps", bq)], writes=[("t1", it)])
                    col = (s * 3 + g) * 4
                    P.op("dve", lambda e, it=it, col=col: e.tensor_reduce(
                        out=ssc[:, col:col + 4], in_=t1[:, it, 0:256].rearrange("p (h d) -> p h d", h=4),
                        axis=AX.X, op=ALU.add),
                        reads=[("t1", it)], writes=["ssc"])
                    P.op("act", lambda e, col=col: e.activation(out=pss_[:, col:col + 4], in_=ssc[:, col:col + 4], func=AF.Exp, scale=0.125),
                         reads=["ssc"], writes=["pss"])
                    P.op("act", lambda e, ik=ik: e.copy(out=zb[:, 0, 0:256], in_=ktile[:, ik, 256:512]),
                         reads=[("kt", ik)], writes=[("zb", 0)])

                    def fpv(e, s=s, g=g, col=col):
                        ins = None
                        for slot in range(4):
                            cc, hh = slot // 2, slot % 2
                            e.matmul(ps[bO][hh * 64:(hh + 1) * 64, cc * 16 + s:cc * 16 + s + 1],
                                     lhsT=zb[:, 0, slot * 64:(slot + 1) * 64], rhs=pss_[:, col + slot:col + slot + 1],
                                     start=False, stop=(g == 2), skip_group_check=True)
                            ins = e.matmul(ps[bL][hh * 64:(hh + 1) * 64, cc * 16 + s:cc * 16 + s + 1],
                                           lhsT=onesb[:, 0:64], rhs=pss_[:, col + slot:col + slot + 1],
                                           start=False, stop=(g == 2), skip_group_check=True)
                        return ins
                    P.op("pe", fpv, reads=[("zb", 0), "pss", "cA"], writes=[("ps", bO), ("ps", bL)])
            bS = nb()
            P.op("dve", lambda e: e.tensor_tensor(out=zbk[:, :, 0:N], in0=zbq[:, :, 0:N], in1=kvn[:, 0:6, 0:N], op=ALU.mult),
                 reads=["zbq"] + [("kvn", c) for c in range(6)], writes=["zbk"])
            P.op("pe", lambda e: e.matmul(ps[bS][:, 0:6 * N], lhsT=blkb, rhs=zbk[:, :, 0:N].rearrange("p c n -> p (c n)"),
                                          start=True, stop=True),
                 reads=["zbk", "cA"], writes=[("ps", bS)])
            P.op("act", lambda e: e.activation(out=sig[:, 0, 0:6 * N], in_=ps[bS][:, 0:6 * N], func=AF.Exp, scale=0.125),
                 reads=[("ps", bS)], writes=[("sig", 0)])
            P.op("dve", lambda e: e.tensor_tensor(out=sig[:, 1, 0:6 * N], in0=sig[:, 0, 0:6 * N],
                                                 in1=kvn[:, 6:12, 0:N].rearrange("p c n -> p (c n)"), op=ALU.mult),
                 reads=[("sig", 0)] + [("kvn", 6 + c) for c in range(6)], writes=[("sig", 1)])
            for cc in range(2):
                P.op("dve", lambda e, cc=cc: e.tensor_copy(out=accO[:, cc, 0:N], in_=ps[bO][:, cc * 16:cc * 16 + N]),
                     reads=[("ps", bO)], writes=[("accO", cc)])
                P.op("dve", lambda e, cc=cc: e.tensor_copy(out=accL[:, cc, 0:N], in_=ps[bL][:, cc * 16:cc * 16 + N]),
                     reads=[("ps", bL)], writes=[("accL", cc)])
                for g in range(3):
                    c = 2 * g + cc
                    P.op("dve", lambda e, cc=cc, c=c: e.tensor_tensor(out=accO[:, cc, 0:N], in0=accO[:, cc, 0:N],
                                                                     in1=sig[:, 1, c * N:(c + 1) * N], op=ALU.add),
                         reads=[("sig", 1), ("accO", cc)], writes=[("accO", cc)])
                    P.op("dve", lambda e, cc=cc, c=c: e.tensor_tensor(out=accL[:, cc, 0:N], in0=accL[:, cc, 0:N],
                                                                     in1=sig[:, 0, c * N:(c + 1) * N], op=ALU.add),
                         reads=[("sig", 0), ("accL", cc)], writes=[("accL", cc)])
                P.op("dve", lambda e, cc=cc: e.reciprocal(out=accL[:, cc, 0:N], in_=accL[:, cc, 0:N]),
                     reads=[("accL", cc)], writes=[("accL", cc)])
                P.op("dve", lambda e, cc=cc: e.tensor_tensor(out=oT[:, cc, 0:N], in0=accO[:, cc, 0:N], in1=accL[:, cc, 0:N], op=ALU.mult),
                     reads=[("accL", cc), ("accO", cc)], writes=["oT"])
            release(bO)
            release(bL)

        zbq = sb("zbq", [128, 6, TS], BF16)
        zbk = sb("zbk", [128, 6, TS], BF16)

        per = TS // 4
        for tt in range(NT):
            if tt < KTILES and KSTAGE > 0:
                tile(tt, False)
        if KSAMPLE and KSTAGE > 0:
            tile(NT, True)
        while shift_q:
            pump_shift()
        if os.environ.get("KTAGS"):
            P.trace_tags = []
        P.emit()
        if P.trace_tags is not None:
            import json
            json.dump(P.trace_tags, open(os.environ["KTAGS"], "w"))
    return nc


_CACHE = {}


def pack_wstream(w):
    L, offs, tot = slab_offsets()
    out = np.empty((128, tot), np.float32)
    for (name, col0, k0, nk), off in zip(L, offs):
        W = w[name]
        blk = W[k0 * 128:(k0 + nk) * 128, col0:col0 + 128]
        out[:, off:off + nk * 128] = blk.reshape(nk, 128, 128).transpose(1, 0, 2).reshape(128, nk * 128)
    return out


def make_in_maps(x_prompt, x_sample, state_conv, cache_win_a, cache_win_b, cache_win_c, p_prompt, p_sample,
                 w_in, g_mix, w_dw, b_dw, ln_g, ln_b, w_conv_out, w_attn_out, w_o, g_ffn, w_ffn_in, w_ffn_out,
                 g_ple, w_ple_gate, w_ple_proj, g_final, cores=range(NCORES)):
    f = lambda a: np.asarray(a, np.float32)
    w = dict(w_in=f(w_in)[0], w_conv_out=f(w_conv_out)[0], w_attn_out=f(w_attn_out)[0], w_o=f(w_o)[0],
             w_ffn_in=f(w_ffn_in)[0], w_ffn_out=f(w_ffn_out)[0], w_ple_gate=f(w_ple_gate)[0],
             w_ple_proj=f(w_ple_proj)[0])
    wstream = pack_wstream(w)
    ca, rope = build_consts()
    cf = np.zeros((128, CF_N), np.float32)
    cf[:, CF_ID:CF_ID + 128] = np.eye(128, dtype=np.float32)
    cf[:, CF_EPS] = EPS
    cf[:, CF_GMIX:CF_GMIX + 8] = colvec(f(g_mix)[0])
    cf[:, CF_GFFN:CF_GFFN + 8] = colvec(f(g_ffn)[0])
    cf[:, CF_GPLE:CF_GPLE + 8] = colvec(f(g_ple)[0])
    cf[:, CF_GFIN:CF_GFIN + 8] = colvec(f(g_final))
    cf[:, CF_BDW:CF_BDW + 8] = colvec(f(b_dw)[0])
    cf[:, CF_LNG:CF_LNG + 8] = colvec(f(ln_g)[0])
    cf[:, CF_LNB:CF_LNB + 8] = colvec(f(ln_b)[0])
    wd = f(w_dw)[0]
    cf[:, CF_WDW:CF_WDW + 248] = wd.reshape(31, 8, 128).transpose(2, 1, 0).reshape(128, 248)
    wrep = np.ascontiguousarray(np.tile(wd[0:30], (4, 1)))
    xp, xs = f(x_prompt), f(x_sample)
    pp, psm = f(p_prompt)[0], f(p_sample)[0]
    stc = f(state_conv)[0]
    cwa, cwb, cwc = f(cache_win_a)[0], f(cache_win_b)[0], f(cache_win_c)[0]
    in_maps = []
    for c in cores:
        sl = slice(c * TS, (c + 1) * TS)
        in_maps.append(dict(
            x=np.ascontiguousarray(xp[c]), p=np.ascontiguousarray(pp[c]),
            xs=np.ascontiguousarray(xs[sl, 0]), pss=np.ascontiguousarray(psm[sl, 0]),
            state=np.ascontiguousarray(stc[sl].reshape(TS * 30, D)),
            cache_a=np.ascontiguousarray(cwa[sl].reshape(TS, 128, 512)),
            cache_b=np.ascontiguousarray(cwb[sl].reshape(TS, 512, 512)),
            cache_c=np.ascontiguousarray(cwc[sl].reshape(TS, 2048, 512)),
            wstream=wstream, constA=ca, constF=cf, rope=rope, wrep=wrep))
    return in_maps


def kernel(**inputs):
    in_maps = make_in_maps(**inputs)
    if "nc" not in _CACHE:
        _CACHE["nc"] = build_program()
    nc = _CACHE["nc"]
    res = run_bass_kernel_spmd(nc, in_maps, core_ids=list(range(NCORES)))
    R = res.results
    cat = lambda k: np.stack([r[k] for r in R], axis=0)
    y_prompt = cat("y").reshape(8, S, D)
    y_sample = np.concatenate([r["ys"] for r in R], axis=0).reshape(128, 1, D)
    new_conv_prompt = cat("conv_p").reshape(1, 8, 30, D)
    nwa_p = cat("wa_p").reshape(1, 8, 128, 2, 4, 64)
    nwb_p = cat("wb_p").reshape(1, 8, 512, 2, 4, 64)
    nwc_p = cat("wc_p").reshape(1, 8, 2048, 2, 4, 64)
    new_conv_sample = np.concatenate([r["conv_s"] for r in R], axis=0).reshape(1, 128, 30, D)
    nwa_s = np.concatenate([r["wa_s"] for r in R], axis=0).reshape(1, 128, 128, 2, 4, 64)
    nwb_s = np.concatenate([r["wb_s"] for r in R], axis=0).reshape(1, 128, 512, 2, 4, 64)
    nwc_s = np.concatenate([r["wc_s"] for r in R], axis=0).reshape(1, 128, 2048, 2, 4, 64)
    return (y_prompt, y_sample, new_conv_prompt, nwa_p, nwb_p, nwc_p, new_conv_sample, nwa_s, nwb_s, nwc_s)
```

```python
import os
import numpy as np
from contextlib import ExitStack
import concourse.bass as bass
import concourse.mybir as mybir
from concourse.bass_utils import run_bass_kernel_spmd

F32 = mybir.dt.float32
BF16 = mybir.dt.bfloat16
AF = mybir.ActivationFunctionType
ALU = mybir.AluOpType
AX = mybir.AxisListType

ENGINES = ("pe", "act", "dve", "pool", "sp")
COMPUTE = ("pe", "act", "dve", "pool")

D = 1024
KC = 8
S = 2048
T = 512
NT = 4
TS = 16
DFF = 2816
FC = 22
PAST = 8192
EPS = 1e-6
NCORES = 8
RING = 6
SLOTW = 11 * 128


class Op:
    __slots__ = ("eng", "fn", "reads", "writes", "dma", "deps", "sig", "sigval",
                 "dsem", "dval", "idx", "eidx", "waits", "tag")

    def __init__(self, eng, fn, reads, writes, dma):
        self.eng = eng
        self.fn = fn
        self.reads = tuple(reads)
        self.writes = tuple(writes)
        self.dma = dma
        self.deps = []
        self.sig = False
        self.sigval = 0
        self.dsem = None
        self.dval = 0
        self.waits = []


class Prog:
    def __init__(self, nc, n_dma_sems=24):
        self.nc = nc
        self.ops = []
        self.n_dma_sems = n_dma_sems
        self.tag = ""
        self.trace_tags = None

    def op(self, eng, fn, reads=(), writes=(), dma=False):
        pr = [r for r in reads if isinstance(r, tuple) and r[0] == "ps"]
        if pr:
            reads = [r for r in reads if not (isinstance(r, tuple) and r[0] == "ps")]
            writes = list(writes) + pr
        o = Op(eng, fn, reads, writes, dma)
        o.idx = len(self.ops)
        o.tag = self.tag
        self.ops.append(o)
        return o

    def analyze(self):
        last_w = {}
        readers = {}
        for o in self.ops:
            deps = set()
            for r in o.reads:
                w = last_w.get(r)
                if w is not None:
                    deps.add(w)
            for w_ in o.writes:
                w = last_w.get(w_)
                if w is not None:
                    deps.add(w)
                for rd in readers.get(w_, ()):
                    deps.add(rd)
            deps.discard(o.idx)
            o.deps = sorted(deps)
            for r in o.reads:
                readers.setdefault(r, []).append(o.idx)
            for w_ in o.writes:
                last_w[w_] = o.idx
                readers[w_] = []
        ecount = {e: 0 for e in ENGINES}
        for o in self.ops:
            o.eidx = ecount[o.eng]
            ecount[o.eng] += 1
        dma_uses = {}
        dma_rr = {e: 0 for e in ENGINES}
        wm = {e: {c: -1 for c in COMPUTE} for e in ENGINES}
        dwm = {e: {} for e in ENGINES}
        by_eng = {e: [] for e in ENGINES}
        for o in self.ops:
            by_eng[o.eng].append(o)
        for o in self.ops:
            waits_c = {}
            waits_d = {}
            if o.dma:
                k = dma_rr[o.eng] % self.n_dma_sems
                dma_rr[o.eng] += 1
                key = (o.eng, k)
                dma_uses[key] = dma_uses.get(key, 0) + 1
                o.dsem = key
                o.dval = 16 * dma_uses[key]
                if dma_uses[key] > 1:
                    waits_d[key] = o.dval - 16
            for d in o.deps:
                p = self.ops[d]
                if p.dma:
                    waits_d[p.dsem] = max(waits_d.get(p.dsem, 0), p.dval)
                else:
                    if p.eng == o.eng and not o.dma:
                        if o.eng == "pe":
                            continue
                        if o.eidx - p.eidx > 2:
                            continue
                    waits_c[p.eng] = max(waits_c.get(p.eng, -1), p.eidx)
            o.waits = []
            for ce, ei in waits_c.items():
                if ei > wm[o.eng][ce]:
                    wm[o.eng][ce] = ei
                    o.waits.append(("c", ce, ei))
            for key, val in waits_d.items():
                if val > dwm[o.eng].get(key, 0):
                    dwm[o.eng][key] = val
                    o.waits.append(("d", key, val))
        for o in self.ops:
            for w in o.waits:
                if w[0] == "c":
                    by_eng[w[1]][w[2]].sig = True
        for e in COMPUTE:
            c = 0
            for o in by_eng[e]:
                if o.sig:
                    c += 1
                o.sigval = c
        self.by_eng = by_eng

    def emit(self):
        nc = self.nc
        self.analyze()
        by_eng = self.by_eng
        with ExitStack() as es:
            csem = {e: es.enter_context(nc.semaphore("s_" + e)) for e in COMPUTE}
            dsem = {}
            for e in ENGINES:
                if any(o.dma for o in by_eng[e]):
                    for k in range(self.n_dma_sems):
                        dsem[(e, k)] = es.enter_context(nc.semaphore("d_%s_%d" % (e, k)))
            block = es.enter_context(nc.Block())
            dma_final = {}
            for o in self.ops:
                if o.dma:
                    dma_final[o.dsem] = max(dma_final.get(o.dsem, 0), o.dval)

            def run(ename, eng):
                for o in by_eng[ename]:
                    for w in o.waits:
                        if w[0] == "c":
                            p = by_eng[w[1]][w[2]]
                            eng.wait_ge(csem[w[1]], p.sigval)
                        else:
                            eng.wait_ge(dsem[w[1]], w[2])
                    if self.trace_tags is not None and ename in ("pe",):
                        cnt_ = [0]

                        class _Px:
                            def __getattr__(s_, nm, eng=eng, cnt_=cnt_):
                                a = getattr(eng, nm)
                                if nm in ("matmul", "transpose"):
                                    def w(*aa, **kk):
                                        cnt_[0] += 1
                                        return a(*aa, **kk)
                                    return w
                                return a
                        ins = o.fn(_Px())
                        self.trace_tags.append((o.tag, cnt_[0]))
                    else:
                        ins = o.fn(eng)
                    if o.dma:
                        ins.then_inc(dsem[o.dsem], 16)
                    elif o.sig:
                        ins.then_inc(csem[ename], 1)
                if ename == "sp":
                    for key, val in dma_final.items():
                        eng.wait_ge(dsem[key], val)

            @block.tensor
            def _(pe):
                run("pe", pe)

            @block.scalar
            def _(act):
                run("act", act)

            @block.vector
            def _(dve):
                run("dve", dve)

            @block.gpsimd
            def _(pool):
                run("pool", pool)

            @block.sync
            def _(sp):
                run("sp", sp)


def slab_list():
    L = []
    for j in range(8):
        L.append(("w_in", 1024 + 128 * j, 0, 8))
        L.append(("w_in", 128 * j, 0, 8))
    for c in range(6):
        L.append(("w_in", 2048 + 128 * c, 0, 8))
    for c in range(6):
        L.append(("w_in", 2816 + 128 * c, 0, 8))
    for c in range(6):
        L.append(("w_in", 3584 + 128 * c, 0, 8))
    def gates(j):
        L.append(("w_in", 4352 + 128 * j, 0, 8))
        L.append(("w_in", 5376 + 128 * j, 0, 8))
    gates(0)
    for j in range(8):
        if j + 1 < 8:
            gates(j + 1)
        L.append(("w_conv_out", 128 * j, 0, 8))
        L.append(("w_attn_out", 128 * j, 0, 2))
    for j in range(8):
        L.append(("w_o", 128 * j, 0, 8))
    for half in range(2):
        for f in range(11 * half, 11 * half + 11):
            L.append(("w_ffn_in", 128 * f, 0, 8))
            L.append(("w_ffn_in", DFF + 128 * f, 0, 8))
        for j in range(8):
            L.append(("w_ffn_out", 128 * j, 11 * half, 11))
    for j in range(8):
        L.append(("w_ple_gate", 128 * j, 0, 8))
        L.append(("w_ple_proj", 128 * j, 0, 2))
    return L


def slab_offsets():
    L = slab_list()
    offs = []
    o = 0
    for (_, _, _, nk) in L:
        offs.append(o)
        o += nk * 128
    return L, offs, o


CA_ONES, CA_RM, CA_BLK, CA_MOWN, CA_MPREV, CA_MC, CA_SEL, CA_ID = 0, 128, 256, 384, 896, 1408, 3456, 3520
CA_N = 3648
CF_ID, CF_EPS, CF_GMIX, CF_GFFN, CF_GPLE, CF_GFIN, CF_BDW, CF_LNG, CF_LNB, CF_WDW = 0, 128, 129, 137, 145, 153, 161, 169, 177, 185
CF_N = 185 + 248
NPOS = S + TS


def build_consts():
    ca = np.zeros((128, CA_N), np.float32)
    ca[:, CA_ONES:CA_ONES + 128] = 1.0
    m = np.arange(128)
    d = m % 64
    partner = np.where(d < 8, m + 8, np.where(d < 16, m - 8, m))
    rm = np.zeros((128, 128), np.float32)
    rm[partner, m] = 1.0
    ca[:, CA_RM:CA_RM + 128] = rm
    ca[:, CA_BLK:CA_BLK + 128] = (m[:, None] // 64 == m[None, :] // 64).astype(np.float32)
    own = (m[:, None] <= m[None, :]).astype(np.float32)
    prev = (m[:, None] >= m[None, :]).astype(np.float32)
    ca[:, CA_MOWN:CA_MOWN + 512] = np.tile(own, (1, 4))
    ca[:, CA_MPREV:CA_MPREV + 512] = np.tile(prev, (1, 4))
    for tt in range(4):
        mc = (m[:, None] <= (32 * tt + np.arange(32))[None, :]).astype(np.float32)
        ca[:, CA_MC + 512 * tt:CA_MC + 512 * (tt + 1)] = np.tile(mc, (1, 16))
    sel = np.zeros((128, 4, 16), np.float32)
    for g in range(4):
        for sl in range(4):
            sel[sl * 30:(sl + 1) * 30, g, 4 * g + sl] = 1.0
    ca[:, CA_SEL:CA_SEL + 64] = sel.reshape(128, 64)
    ca[:, CA_ID:CA_ID + 128] = np.eye(128, dtype=np.float32)
    half = 8
    inv_freq = (500000.0 ** (-np.arange(half, dtype=np.float32) / half)).astype(np.float32)
    pos = np.concatenate([np.arange(S), np.full(TS, PAST)]).astype(np.float32)
    ang = pos[None, :] * inv_freq[:, None]
    cos = np.cos(ang).astype(np.float32)
    sin = np.sin(ang).astype(np.float32)
    rope = np.zeros((128, 2, NPOS), np.float32)
    rope[:, 0, :] = 1.0
    for p in range(128):
        dd = p % 64
        if dd < 8:
            rope[p, 0] = cos[dd]
            rope[p, 1] = -sin[dd]
        elif dd < 16:
            rope[p, 0] = cos[dd - 8]
            rope[p, 1] = sin[dd - 8]
    return ca, rope


def colvec(v):
    return np.ascontiguousarray(np.asarray(v, np.float32).reshape(8, 128).T)


def build_program(debug=False):
    nc = bass.Bass("TRN2", target_bir_lowering=False)
    slabs, soffs, WTOT = slab_offsets()
    NSL = len(slabs)

    def din(name, shape):
        return nc.dram_tensor(name, list(shape), F32, kind="ExternalInput").ap()

    def dout(name, shape):
        return nc.dram_tensor(name, list(shape), F32, kind="ExternalOutput").ap()

    x_d = din("x", [S, D])
    p_d = din("p", [S, 256])
    xs_d = din("xs", [TS, D])
    pss_d = din("pss", [TS, 256])
    st_d = din("state", [TS * 30, D])
    ca_d = din("cache_a", [TS, 128, 512])
    cb_d = din("cache_b", [TS, 512, 512])
    cc_d = din("cache_c", [TS, 2048, 512])
    ws_d = din("wstream", [128, WTOT])
    cA_d = din("constA", [128, CA_N])
    cF_d = din("constF", [128, CF_N])
    rope_d = din("rope", [128, 2, NPOS])
    wrep_d = din("wrep", [120, D])

    y_d = dout("y", [S, D])
    ys_d = dout("ys", [TS, D])
    convp_d = dout("conv_p", [30, D])
    wap_d = dout("wa_p", [128, 512])
    wbp_d = dout("wb_p", [512, 512])
    wcp_d = dout("wc_p", [2048, 512])
    convs_d = dout("conv_s", [TS, 30 * D])
    was_d = dout("wa_s", [TS, 128 * 512])
    wbs_d = dout("wb_s", [TS, 512 * 512])
    wcs_d = dout("wc_s", [TS, 2048 * 512])
    dbg = {}

    P = Prog(nc)
    with ExitStack() as es:
        def sb(name, shape, dt):
            return es.enter_context(nc.sbuf_tensor("sb_" + name, list(shape), dt))

        xT = sb("xT", [128, 8, T], F32)
        hT = sb("hT", [128, 8, T], BF16)
        UW = 30 + T
        uT = sb("uT", [128, 8, UW], BF16)
        cT = sb("cT", [128, 8, T], BF16)
        qm = sb("qm", [128, 2, 6, T], BF16)
        kAB = sb("kAB", [128, 4, 2, T], BF16)
        kC = sb("kC", [128, 2, S], BF16)
        vAB = sb("vAB", [128, 2, 8, 256], BF16)
        vC = sb("vC", [128, 16, 256], BF16)
        accO = sb("accO", [128, 2, T], F32)
        accL = sb("accL", [128, 2, T], F32)
        oT = sb("oT", [128, 2, T], BF16)
        mg = uT[:, :, 30:30 + T]
        actb = sb("actb", [128, 11, T], BF16)
        ring = sb("ring", [128, RING, SLOTW], BF16)
        diag = sb("diag", [128, 2, 31, 128], BF16)
        rope = sb("rope", [128, 2, T], F32)
        cA = sb("cA", [128, CA_N], BF16)
        cF = sb("cF", [128, CF_N], F32)
        sig = sb("sig", [128, 4, T], F32)
        zf = sb("zf", [128, 2, T], F32)
        zb = sb("zb", [128, 2, T], BF16)
        t1 = sb("t1", [128, 3, T], F32)
        rs = sb("rs", [128, 3, T], F32)
        sq = sb("sq", [128, 2, T], BF16)
        pexp = sb("pexp", [128, 2, T], BF16)
        xst = sb("xst", [128, 2, D], F32)
        pst = sb("pst", [128, 256], F32)
        pT = sb("pT", [128, 2, T], BF16)
        cst = sb("cst", [128, 2, 256], F32)
        stS = sb("stS", [120, 1, D], F32)
        prod = kC[0:120, :, :].rearrange("p a (b c) -> p (a b) c", c=D)
        wrep = rope[0:120, :, :].rearrange("p a b -> p (a b)")
        ktile = sb("ktile", [128, 2, 512], F32)
        ssc = sb("ssc", [128, 192], F32)
        pss_ = sb("pssb", [128, 192], BF16)
        qrep = sb("qrep", [128, 2, 2, 128], BF16)
        utf = sb("utf", [128, 8, 32], F32)
        kvn = sb("kvn", [128, 12, TS], F32)
        ps = [es.enter_context(nc.psum_tensor("ps%d" % i, [128, 512], F32)) for i in range(8)]

        identf = cF[:, CF_ID:CF_ID + 128]
        identb = cA[:, CA_ID:CA_ID + 128]
        onesb = cA[:, CA_ONES:CA_ONES + 128]
        rmb = cA[:, CA_RM:CA_RM + 128]
        blkb = cA[:, CA_BLK:CA_BLK + 128]
        epsc = cF[:, CF_EPS:CF_EPS + 1]

        cnt = {"bank": 0, "slab": 0, "pref": 0, "sig": 0, "zf": 0, "t1": 0, "sq": 0, "pexp": 0,
               "xst": 0, "cst": 0, "diag": 0, "zb": 0, "kt": 0, "qrep": 0, "stS": 0}

        reserved = set()

        def nb():
            while True:
                b = cnt["bank"] % 8
                cnt["bank"] += 1
                if b not in reserved:
                    return b

        def reserve():
            b = nb()
            reserved.add(b)
            return b

        def release(b):
            reserved.discard(b)

        def rot(name, n=2):
            i = cnt[name] % n
            cnt[name] += 1
            return i

        NTILES = NT + 1
        TOTSL = NSL * NTILES

        wscr = nc.dram_tensor("wscr", [128, WTOT], BF16, kind="Internal").ap()

        def prefetch():
            g = cnt["pref"]
            if g >= TOTSL:
                return
            cnt["pref"] += 1
            s = g % NSL
            slot = g % RING
            nk = slabs[s][3]
            off = soffs[s]
            if g < NSL:
                P.op("pool", lambda e, slot=slot, nk=nk, off=off: e.dma_start(
                    out=ring[:, slot, 0:nk * 128], in_=ws_d[:, off:off + nk * 128]),
                    writes=[("ring", slot)], dma=True)
                P.op("sp", lambda e, slot=slot, nk=nk, off=off: e.dma_start(
                    out=wscr[:, off:off + nk * 128], in_=ring[:, slot, 0:nk * 128]),
                    reads=[("ring", slot)], writes=[("wscr", s)], dma=True)
            else:
                P.op("pool", lambda e, slot=slot, nk=nk, off=off: e.dma_start(
                    out=ring[:, slot, 0:nk * 128], in_=wscr[:, off:off + nk * 128]),
                    reads=[("wscr", s)], writes=[("ring", slot)], dma=True)

        def take_slab():
            g = cnt["slab"]
            cnt["slab"] += 1
            return g % RING, slabs[g % NSL][3]

        def linear(rhs_fn, rhs_reads, N, fine=False):
            slot, nk = take_slab()
            b = nb()
            if fine:
                for k in range(nk):
                    P.op("pe", lambda e, slot=slot, nk=nk, b=b, k=k: e.matmul(
                        ps[b][:, 0:N], lhsT=ring[:, slot, k * 128:(k + 1) * 128], rhs=rhs_fn(k),
                        start=(k == 0), stop=(k == nk - 1)),
                        reads=[("ring", slot), rhs_reads[k]], writes=[("ps", b)])
                prefetch()
                if cnt["slab"] % 3 == 0:
                    pump_shift()
                return b

            def f(e, slot=slot, nk=nk, b=b):
                for k in range(nk):
                    ins = e.matmul(ps[b][:, 0:N], lhsT=ring[:, slot, k * 128:(k + 1) * 128],
                                   rhs=rhs_fn(k), start=(k == 0), stop=(k == nk - 1))
                return ins
            P.op("pe", f, reads=[("ring", slot)] + list(rhs_reads), writes=[("ps", b)])
            prefetch()
            if cnt["slab"] % 3 == 0:
                pump_shift()
            return b

        def rmsnorm_to_h(N, gcol0, out_fp32=None):
            bs = nb()
            for kc in range(8):
                i = rot("sq")
                P.op("act", lambda e, kc=kc, i=i: e.activation(out=sq[:, i, 0:N], in_=xT[:, kc, 0:N], func=AF.Square),
                     reads=[("xT", kc)], writes=[("sq", i)])
                P.op("pe", lambda e, kc=kc, i=i, bs=bs: e.matmul(ps[bs][:, 0:N], lhsT=onesb, rhs=sq[:, i, 0:N],
                                                                start=(kc == 0), stop=(kc == 7)),
                     reads=[("sq", i), "cA"], writes=[("ps", bs)])
            P.op("act", lambda e, bs=bs: e.activation(out=rs[:, 0, 0:N], in_=ps[bs][:, 0:N], func=AF.Ln,
                                                     scale=1.0 / D, bias=epsc),
                 reads=[("ps", bs), "cF"], writes=[("rs", 0)])
            P.op("act", lambda e: e.activation(out=rs[:, 0, 0:N], in_=rs[:, 0, 0:N], func=AF.Exp, scale=-0.5),
                 reads=[("rs", 0)], writes=[("rs", 0)])
            for kc in range(8):
                if out_fp32 is None:
                    P.op("dve", lambda e, kc=kc: e.scalar_tensor_tensor(
                        out=hT[:, kc, 0:N], in0=xT[:, kc, 0:N], scalar=cF[:, gcol0 + kc:gcol0 + kc + 1],
                        in1=rs[:, 0, 0:N], op0=ALU.mult, op1=ALU.mult),
                        reads=[("xT", kc), ("rs", 0), "cF"], writes=[("hT", kc)])
                else:
                    out_fp32(kc)

        def load_xT(src_d, row0, N):
            nblk = max(1, N // 128)
            nblk = min(nblk, int(os.environ.get("KNBLK", "99")))
            bw = min(128, N)
            for blk in range(nblk):
                i = rot("xst")
                P.op("sp", lambda e, blk=blk, i=i: e.dma_start(out=xst[0:bw, i, :],
                                                            in_=src_d[row0 + blk * 128:row0 + blk * 128 + bw, :]),
                     writes=[("xst", i)], dma=True)
                for half in range(2):
                    b = nb()

                    def f(e, i=i, b=b, half=half):
                        for q in range(4):
                            kc = half * 4 + q
                            ins = e.transpose(ps[b][:, q * 128:q * 128 + bw], xst[0:bw, i, kc * 128:(kc + 1) * 128],
                                              identf[0:bw, 0:bw])
                        return ins
                    P.op("pe", f, reads=[("xst", i), "cF"], writes=[("ps", b)])
                    for q in range(4 if not int(os.environ.get("KNOEVAC", "0")) else 0):
                        kc = half * 4 + q
                        eng = "act" if q % 2 == 0 else "dve"
                        if eng == "act":
                            P.op("act", lambda e, b=b, q=q, kc=kc, blk=blk: e.copy(
                                out=xT[:, kc, blk * 128:blk * 128 + bw], in_=ps[b][:, q * 128:q * 128 + bw]),
                                reads=[("ps", b)], writes=[("xT", kc)])
                        else:
                            P.op("dve", lambda e, b=b, q=q, kc=kc, blk=blk: e.tensor_copy(
                                out=xT[:, kc, blk * 128:blk * 128 + bw], in_=ps[b][:, q * 128:q * 128 + bw]),
                                reads=[("ps", b)], writes=[("xT", kc)])

        def store_rows(src_fn, src_reads, dst_d, row0, N, ncol_chunks, dst_col0=0):
            nblk = max(1, N // 128)
            bw = min(128, N)
            for blk in range(nblk):
                i = rot("xst")
                for c0 in range(0, ncol_chunks, 4):
                    b = nb()
                    nq = min(4, ncol_chunks - c0)

                    def f(e, b=b, c0=c0, nq=nq, blk=blk):
                        for q in range(nq):
                            ins = e.transpose(ps[b][0:bw, q * 128:(q + 1) * 128],
                                              src_fn(c0 + q)[:, blk * 128:blk * 128 + bw], identf)
                        return ins
                    P.op("pe", f, reads=list(src_reads) + ["cF"], writes=[("ps", b)])
                    eng = "act" if (c0 // 4) % 2 == 0 else "dve"
                    if eng == "act":
                        P.op("act", lambda e, b=b, c0=c0, nq=nq, i=i: e.copy(
                            out=xst[0:bw, i, c0 * 128:(c0 + nq) * 128], in_=ps[b][0:bw, 0:nq * 128]),
                            reads=[("ps", b)], writes=[("xst", i)])
                    else:
                        P.op("dve", lambda e, b=b, c0=c0, nq=nq, i=i: e.tensor_copy(
                            out=xst[0:bw, i, c0 * 128:(c0 + nq) * 128], in_=ps[b][0:bw, 0:nq * 128]),
                            reads=[("ps", b)], writes=[("xst", i)])
                P.op("act", lambda e, i=i, blk=blk: e.dma_start(
                    out=dst_d[row0 + blk * 128:row0 + blk * 128 + bw, dst_col0:dst_col0 + ncol_chunks * 128],
                    in_=xst[0:bw, i, 0:ncol_chunks * 128]),
                    reads=[("xst", i)], dma=True)

        P.op("pool", lambda e: e.dma_start(out=cA[:, :], in_=cA_d[:, :]), writes=["cA"], dma=True)
        P.op("sp", lambda e: e.dma_start(out=cF[:, :], in_=cF_d[:, :]), writes=["cF"], dma=True)
        dscr = nc.dram_tensor("dscr", [8, 128, 31 * 128], BF16, kind="Internal").ap()
        for j in range(8):
            dbuf = j % 2
            for tap in range(31):
                col = CF_WDW + j * 31 + tap
                if tap % 2 == 0:
                    P.op("dve", lambda e, tap=tap, dbuf=dbuf, col=col: e.tensor_scalar(
                        out=diag[:, dbuf, tap, :], in0=identb, scalar1=cF[:, col:col + 1], scalar2=None, op0=ALU.mult),
                        reads=["cA", "cF"], writes=[("diag", dbuf, tap)])
                else:
                    P.op("act", lambda e, tap=tap, dbuf=dbuf, col=col: e.activation(
                        out=diag[:, dbuf, tap, :], in_=identb, func=AF.Copy, scale=cF[:, col:col + 1]),
                        reads=["cA", "cF"], writes=[("diag", dbuf, tap)])
            P.op("sp", lambda e, j=j, dbuf=dbuf: e.dma_start(out=dscr[j], in_=diag[:, dbuf, :, :].rearrange("p a b -> p (a b)")),
                 reads=[("diag", dbuf, tap) for tap in range(31)], writes=[("dscr", j)], dma=True)
        for _ in range(RING):
            prefetch()
        P.op("dve", lambda e: e.memset(qm[:, :, :, :], 0.0), writes=[("qm", c) for c in range(6)])
        P.op("pool", lambda e: e.memset(kC[:, :, :], 0.0), writes=["kC"])
        P.op("pool", lambda e: e.memset(vC[:, :, :], 0.0), writes=["vC"])
        P.op("dve", lambda e: e.memset(uT[:, :, 0:30], 0.0), writes=[("uT", j) for j in range(8)])

        KSTAGE = int(os.environ.get("KSTAGE", "99"))
        KTILES = int(os.environ.get("KTILES", str(NT)))
        KSAMPLE = int(os.environ.get("KSAMPLE", "1"))
        KSHIFT = int(os.environ.get("KSHIFT", "1"))
        shift_q = []

        def shift_piece(src, dst, L, s_, k, nk):
            n = (L - 1) * 512 // nk
            P.op("act", lambda e: e.dma_start(
                out=dst[s_, k * n:(k + 1) * n].rearrange("(a e) -> a e", a=16),
                in_=src[s_, 1:L, :].rearrange("l c -> (l c)")[k * n:(k + 1) * n].rearrange("(a e) -> a e", a=16)),
                reads=[], writes=[], dma=True)

        for s_ in range(TS):
            for k in range(8):
                shift_q.append((cc_d, wcs_d, 2048, s_, k, 8))
            for k in range(2):
                shift_q.append((cb_d, wbs_d, 512, s_, k, 2))
            shift_q.append((ca_d, was_d, 128, s_, 0, 1))

        def pump_shift():
            if KSHIFT and shift_q:
                shift_piece(*shift_q.pop(0))


        def finish_tile(sample, t0, N):
            if int(os.environ.get("KNOSTORE", "0")):
                return
            store_rows(lambda c: xT[:, c, 0:N], [("xT", kc) for kc in range(8)], ys_d if sample else y_d,
                       0 if sample else t0, N, 8)

        def tile(tt, sample):
            N = TS if sample else T
            t0 = S if sample else tt * T
            par = tt % 2
            last = (tt == NT - 1) and not sample
            if not int(os.environ.get("KNOROPE", "0")):
                P.op("sp", lambda e: e.dma_start(out=rope[:, :, 0:N], in_=rope_d[:, :, t0:t0 + N]),
                     writes=["rope"], dma=True)
            P.tag = "%dA" % tt
            if sample:
                load_xT(xs_d, 0, N)
            else:
                load_xT(x_d, t0, N)
            if KSTAGE <= 1:
                return finish_tile(sample, t0, N)
            rmsnorm_to_h(N, CF_GMIX)
            nblk = max(1, N // 128)
            bw = min(128, N)
            psrc = pss_d if sample else p_d
            prow0 = 0 if sample else t0
            for blk in range(nblk):
                P.op("sp", lambda e, blk=blk: e.dma_start(out=pst[0:bw, :], in_=psrc[prow0 + blk * 128:prow0 + blk * 128 + bw, :]),
                     writes=["pst"], dma=True)
                b = nb()

                def f(e, b=b):
                    for cc in range(2):
                        ins = e.transpose(ps[b][:, cc * 128:cc * 128 + bw], pst[0:bw, cc * 128:(cc + 1) * 128], identf[0:bw, 0:bw])
                    return ins
                P.op("pe", f, reads=["pst", "cF"], writes=[("ps", b)])
                P.op("act", lambda e, b=b, blk=blk: e.copy(
                    out=pT[:, :, blk * 128:blk * 128 + bw], in_=ps[b][:, 0:256].rearrange("p (c n) -> p c n", c=2)[:, :, 0:bw]),
                    reads=[("ps", b)], writes=["pT"])

            hreads = [("hT", kc) for kc in range(8)]
            hfn = lambda k: hT[:, k, 0:N]
            P.tag = "%dB1" % tt
            for j in range(8):
                bb = linear(hfn, hreads, N, fine=(j == 0))
                i = rot("sig")
                P.op("act", lambda e, bb=bb, i=i: e.activation(out=sig[:, i, 0:N], in_=ps[bb][:, 0:N], func=AF.Sigmoid),
                     reads=[("ps", bb)], writes=[("sig", i)])
                ba = linear(hfn, hreads, N)
                if sample:
                    P.op("dve", lambda e, ba=ba, i=i, j=j: e.tensor_tensor(
                        out=utf[:, j, 0:N], in0=ps[ba][:, 0:N], in1=sig[:, i, 0:N], op=ALU.mult),
                        reads=[("ps", ba), ("sig", i)], writes=[("utf", j)])
                else:
                    P.op("dve", lambda e, ba=ba, i=i, j=j: e.tensor_tensor(
                        out=uT[:, j, 30:30 + N], in0=ps[ba][:, 0:N], in1=sig[:, i, 0:N], op=ALU.mult),
                        reads=[("ps", ba), ("sig", i)], writes=[("uT", j)])
                    if last:
                        P.op("dve", lambda e, ba=ba, i=i, j=j: e.tensor_tensor(
                            out=utf[:, j, 0:32], in0=ps[ba][:, N - 32:N], in1=sig[:, i, N - 32:N], op=ALU.mult),
                            reads=[("ps", ba), ("sig", i)], writes=[("utf", j)])
            if KSTAGE <= 2:
                return finish_tile(sample, t0, N)
            if last:
                store_rows(lambda c: utf[:, c, 2:32], [("utf", j) for j in range(8)], convp_d, 0, 30, 8)
            P.tag = "%dB2" % tt
            Cc = rope[:, 0, 0:N]
            Ss = rope[:, 1, 0:N]

            def roped(bz):
                i = rot("zf")
                ib = rot("zb")
                P.op("act", lambda e: e.copy(out=zb[:, ib, 0:N], in_=ps[bz][:, 0:N]),
                     reads=[("ps", bz)], writes=[("zb", ib)])
                br = nb()
                P.op("pe", lambda e: e.matmul(ps[br][:, 0:N], lhsT=rmb, rhs=zb[:, ib, 0:N], start=True, stop=True),
                     reads=[("zb", ib), "cA"], writes=[("ps", br)])
                P.op("dve", lambda e: e.tensor_tensor(out=zf[:, i, 0:N], in0=ps[bz][:, 0:N], in1=Cc, op=ALU.mult),
                     reads=[("ps", bz), "rope"], writes=[("zf", i)])
                it = rot("t1")
                P.op("dve", lambda e: e.tensor_tensor(out=t1[:, it, 0:N], in0=ps[br][:, 0:N], in1=Ss, op=ALU.mult),
                     reads=[("ps", br), "rope"], writes=[("t1", it)])
                P.op("dve", lambda e: e.tensor_tensor(out=zf[:, i, 0:N], in0=zf[:, i, 0:N], in1=t1[:, it, 0:N], op=ALU.add),
                     reads=[("zf", i), ("t1", it)], writes=[("zf", i)])
                return i

            def cache_rows_out(src_ap_fn, src_reads, g, kv, qb_list):
                dst, L = [(wap_d, 128), (wbp_d, 512), (wcp_d, 2048)][g]
                for qb in qb_list:
                    tok0 = t0 + qb * 128
                    r0 = tok0 - (S - L)
                    if r0 < 0:
                        continue
                    ic = rot("cst")
                    b = nb()

                    def f(e, b=b, qb=qb):
                        for cc in range(2):
                            ins = e.transpose(ps[b][:, cc * 128:(cc + 1) * 128],
                                              src_ap_fn(cc)[:, qb * 128:(qb + 1) * 128], identf)
                        return ins
                    P.op("pe", f, reads=list(src_reads) + ["cF"], writes=[("ps", b)])
                    P.op("act", lambda e, b=b, ic=ic: e.copy(out=cst[:, ic, 0:256], in_=ps[b][:, 0:256]),
                         reads=[("ps", b)], writes=[("cst", ic)])
                    P.op("sp", lambda e, ic=ic, r0=r0, dst=dst, kv=kv: e.dma_start(
                        out=dst[r0:r0 + 128, kv * 256:(kv + 1) * 256], in_=cst[:, ic, 0:256]),
                        reads=[("cst", ic)], dma=True)

            for c in range(6):
                bz = linear(hfn, hreads, N)
                i = roped(bz)
                if sample:
                    P.op("act", lambda e, i=i, c=c: e.copy(out=zbq[:, c, 0:N], in_=zf[:, i, 0:N]),
                         reads=[("zf", i)], writes=["zbq"])
                    continue
                P.op("act", lambda e, i=i, c=c: e.copy(out=qm[0:64, 0, c, 0:N], in_=zf[0:64, i, 0:N]),
                     reads=[("zf", i)], writes=[("qm", c)])
                P.op("dve", lambda e, i=i, c=c: e.tensor_copy(out=qm[64:128, 1, c, 0:N], in_=zf[64:128, i, 0:N]),
                     reads=[("zf", i)], writes=[("qm", c)])
            kpair = {}
            for c in range(6):
                bz = linear(hfn, hreads, N)
                i = roped(bz)
                g = c // 2
                if sample:
                    P.op("act", lambda e, i=i, c=c: e.copy(out=kvn[:, c, 0:N], in_=zf[:, i, 0:N]),
                         reads=[("zf", i)], writes=[("kvn", c)])
                else:
                    if g < 2:
                        P.op("act", lambda e, i=i, c=c: e.copy(out=kAB[:, c, par, 0:N], in_=zf[:, i, 0:N]),
                             reads=[("zf", i)], writes=["kAB"])
                    else:
                        P.op("act", lambda e, i=i, c=c: e.copy(out=kC[:, c - 4, t0:t0 + N], in_=zf[:, i, 0:N]),
                             reads=[("zf", i)], writes=["kC"])
                    dst, L = [(wap_d, 128), (wbp_d, 512), (wcp_d, 2048)][g]
                    for qb in range(4):
                        r0 = t0 + qb * 128 - (S - L)
                        if r0 < 0:
                            continue
                        ic = rot("cst")
                        b = nb()
                        P.op("pe", lambda e, b=b, i=i, qb=qb: e.transpose(ps[b][:, 0:128], zf[:, i, qb * 128:(qb + 1) * 128], identf),
                             reads=[("zf", i), "cF"], writes=[("ps", b)])
                        P.op("act", lambda e, b=b, ic=ic: e.copy(out=cst[:, ic, 0:128], in_=ps[b][:, 0:128]),
                             reads=[("ps", b)], writes=[("cst", ic)])
                        P.op("act", lambda e, ic=ic, r0=r0, dst=dst, c=c: e.dma_start(
                            out=dst[r0:r0 + 128, (c % 2) * 128:(c % 2) * 128 + 128], in_=cst[:, ic, 0:128]),
                            reads=[("cst", ic)], dma=True)
            for c in range(6):
                bz = linear(hfn, hreads, N)
                g = c // 2
                cc = c % 2
                i = rot("zf")
                P.op("act", lambda e, bz=bz, i=i: e.copy(out=zf[:, i, 0:N], in_=ps[bz][:, 0:N]),
                     reads=[("ps", bz)], writes=[("zf", i)])
                if sample:
                    P.op("dve", lambda e, i=i, c=c: e.tensor_copy(out=kvn[:, 6 + c, 0:N], in_=zf[:, i, 0:N]),
                         reads=[("zf", i)], writes=[("kvn", 6 + c)])
                    continue
                dst, L = [(wap_d, 128), (wbp_d, 512), (wcp_d, 2048)][g]
                for qb in range(4):
                    r0 = t0 + qb * 128 - (S - L)
                    if r0 < 0 and g != 0:
                        continue
                    b = nb()
                    P.op("pe", lambda e, b=b, i=i, qb=qb: e.transpose(ps[b][:, 0:128], zf[:, i, qb * 128:(qb + 1) * 128], identf),
                         reads=[("zf", i), "cF"], writes=[("ps", b)])
                    if g == 0:
                        P.op("dve", lambda e, b=b, qb=qb, cc=cc: e.tensor_copy(
                            out=vAB[:, par, qb, cc * 128:(cc + 1) * 128], in_=ps[b][:, 0:128]),
                            reads=[("ps", b)], writes=["vAB"])
                    if r0 >= 0:
                        ic = rot("cst")
                        P.op("act", lambda e, b=b, ic=ic: e.copy(out=cst[:, ic, 0:128], in_=ps[b][:, 0:128]),
                             reads=[("ps", b)], writes=[("cst", ic)])
                        P.op("act", lambda e, ic=ic, r0=r0, dst=dst, cc=cc: e.dma_start(
                            out=dst[r0:r0 + 128, 256 + cc * 128:256 + cc * 128 + 128], in_=cst[:, ic, 0:128]),
                            reads=[("cst", ic)], dma=True)
                if g == 1:
                    for r in range(4):
                        b = nb()
                        P.op("pe", lambda e, b=b, i=i, r=r: e.transpose(ps[b][:, 0:128], zf[:, i, r:T:4], identf),
                             reads=[("zf", i), "cF"], writes=[("ps", b)])
                        P.op("dve", lambda e, b=b, r=r, cc=cc: e.tensor_copy(
                            out=vAB[:, par, 4 + r, cc * 128:(cc + 1) * 128], in_=ps[b][:, 0:128]),
                            reads=[("ps", b)], writes=["vAB"])
                if g == 2:
                    for r0_ in range(0, 16, 4):
                        b = nb()

                        def f(e, b=b, i=i, r0_=r0_):
                            for q in range(4):
                                ins = e.transpose(ps[b][0:32, q * 128:(q + 1) * 128], zf[:, i, (r0_ + q):T:16], identf)
                            return ins
                        P.op("pe", f, reads=[("zf", i), "cF"], writes=[("ps", b)])
                        P.op("dve", lambda e, b=b, r0_=r0_, cc=cc: e.tensor_copy(
                            out=vC[32 * tt:32 * tt + 32, r0_:r0_ + 4, cc * 128:(cc + 1) * 128],
                            in_=ps[b][0:32, 0:512].rearrange("p (q c) -> p q c", q=4)),
                            reads=[("ps", b)], writes=["vC"])

            if KSTAGE <= 3:
                return finish_tile(sample, t0, N)
            P.tag = "%dC" % tt
            attn_gen = None
            if not sample and KSTAGE > 4:
                attn_gen = prompt_attention(tt, N)

            def attn_step(n):
                nonlocal attn_gen
                for _ in range(n):
                    if attn_gen is None:
                        return
                    try:
                        next(attn_gen)
                    except StopIteration:
                        attn_gen = None
            if not sample:
                for j in range(8):
                    P.tag = "%dC" % tt
                    dbuf = rot("diag")
                    P.op("sp", lambda e, j=j, dbuf=dbuf: e.dma_start(
                        out=diag[:, dbuf, :, :].rearrange("p a b -> p (a b)"), in_=dscr[j]),
                        reads=[("dscr", j)], writes=[("diag", dbuf, tap) for tap in range(31)], dma=True)
                    b = nb()

                    def f(e, j=j, dbuf=dbuf, b=b):
                        for tap in range(31):
                            ins = e.matmul(ps[b][:, 0:N], lhsT=diag[:, dbuf, tap, :], rhs=uT[:, j, tap:tap + N],
                                           start=(tap == 0), stop=(tap == 30))
                        return ins
                    P.op("pe", f, reads=[("diag", dbuf, tap) for tap in range(31)] + [("uT", j)], writes=[("ps", b)])
                    conv_epilogue(j, b, N)
                    P.tag = "%dD" % tt
                    attn_step(3)
                P.tag = "%dC" % tt
                if not last:
                    P.op("dve", lambda e: e.tensor_copy(out=uT[:, :, 0:30], in_=uT[:, :, T:T + 30]),
                         reads=[("uT", j) for j in range(8)], writes=[("uT", j) for j in range(8)])
            else:
                sample_conv(N)
            P.tag = "%dD" % tt
            attn_step(1000)
            P.tag = "%dC" % tt
            ln_finish(N)

            if KSTAGE <= 4:
                return finish_tile(sample, t0, N)
            P.tag = "%dD" % tt
            if sample:
                sample_attention(N)
            else:
                attn_step(1000)

            if KSTAGE <= 5:
                return finish_tile(sample, t0, N)
            P.tag = "%dE" % tt
            cTreads = [("cT", kc) for kc in range(8)]
            gsig = {}

            def gate_pair(j):
                bgA = linear(hfn, hreads, N)
                iA = rot("sig", 4)
                P.op("act", lambda e: e.activation(out=sig[:, iA, 0:N], in_=ps[bgA][:, 0:N], func=AF.Sigmoid),
                     reads=[("ps", bgA)], writes=[("sig", iA)])
                bgB = linear(hfn, hreads, N)
                iB = rot("sig", 4)
                P.op("act", lambda e: e.activation(out=sig[:, iB, 0:N], in_=ps[bgB][:, 0:N], func=AF.Sigmoid),
                     reads=[("ps", bgB)], writes=[("sig", iB)])
                gsig[j] = (iA, iB)
            gate_pair(0)
            for j in range(8):
                if j + 1 < 8:
                    gate_pair(j + 1)
                i, i2 = gsig[j]
                bcv = linear(lambda k: cT[:, k, 0:N], cTreads, N)
                it = rot("t1")
                P.op("dve", lambda e, bcv=bcv, i=i, it=it: e.tensor_tensor(
                    out=t1[:, it, 0:N], in0=ps[bcv][:, 0:N], in1=sig[:, i, 0:N], op=ALU.mult),
                    reads=[("ps", bcv), ("sig", i)], writes=[("t1", it)])
                bat = linear(lambda k: oT[:, k, 0:N], ["oT"], N)
                P.op("dve", lambda e, bat=bat, i2=i2: e.tensor_tensor(
                    out=sig[:, i2, 0:N], in0=ps[bat][:, 0:N], in1=sig[:, i2, 0:N], op=ALU.mult),
                    reads=[("ps", bat), ("sig", i2)], writes=[("sig", i2)])
                P.op("dve", lambda e, i2=i2, it=it, j=j: e.tensor_tensor(
                    out=mg[:, j, 0:N], in0=sig[:, i2, 0:N], in1=t1[:, it, 0:N], op=ALU.add),
                    reads=[("sig", i2), ("t1", it)], writes=[("uT", j)])
            P.tag = "%dF" % tt
            mreads = [("uT", kc) for kc in range(8)]
            for j in range(8):
                b = linear(lambda k: mg[:, k, 0:N], mreads, N)
                P.op("dve", lambda e, b=b, j=j: e.tensor_tensor(out=xT[:, j, 0:N], in0=ps[b][:, 0:N], in1=xT[:, j, 0:N], op=ALU.add),
                     reads=[("ps", b), ("xT", j)], writes=[("xT", j)])
            P.tag = "%dG" % tt
            rmsnorm_to_h(N, CF_GFFN)
            for half in range(2):
                for f_ in range(11):
                    bg = linear(hfn, hreads, N, fine=(half == 0 and f_ == 0))
                    i = rot("sig")
                    P.op("act", lambda e, bg=bg, i=i: e.activation(out=sig[:, i, 0:N], in_=ps[bg][:, 0:N], func=AF.Silu),
                         reads=[("ps", bg)], writes=[("sig", i)])
                    bu = linear(hfn, hreads, N)
                    P.op("dve", lambda e, bu=bu, i=i, f_=f_: e.tensor_tensor(
                        out=actb[:, f_, 0:N], in0=ps[bu][:, 0:N], in1=sig[:, i, 0:N], op=ALU.mult),
                        reads=[("ps", bu), ("sig", i)], writes=[("actb", f_)])
                areads = [("actb", f_) for f_ in range(11)]
                for j in range(8):
                    b = linear(lambda k: actb[:, k, 0:N], areads, N)
                    P.op("dve", lambda e, b=b, j=j: e.tensor_tensor(out=xT[:, j, 0:N], in0=ps[b][:, 0:N], in1=xT[:, j, 0:N], op=ALU.add),
                         reads=[("ps", b), ("xT", j)], writes=[("xT", j)])
            P.tag = "%dH" % tt
            rmsnorm_to_h(N, CF_GPLE)
            for j in range(8):
                bg = linear(hfn, hreads, N, fine=(j == 0))
                i = rot("sig")
                P.op("act", lambda e, bg=bg, i=i: e.activation(out=sig[:, i, 0:N], in_=ps[bg][:, 0:N], func=AF.Sigmoid),
                     reads=[("ps", bg)], writes=[("sig", i)])
                bp = linear(lambda k: pT[:, k, 0:N], ["pT"], N)
                P.op("dve", lambda e, bp=bp, i=i: e.tensor_tensor(out=sig[:, i, 0:N], in0=ps[bp][:, 0:N], in1=sig[:, i, 0:N], op=ALU.mult),
                     reads=[("ps", bp), ("sig", i)], writes=[("sig", i)])
                P.op("dve", lambda e, i=i, j=j: e.tensor_tensor(out=xT[:, j, 0:N], in0=sig[:, i, 0:N], in1=xT[:, j, 0:N], op=ALU.add),
                     reads=[("sig", i), ("xT", j)], writes=[("xT", j)])
            P.tag = "%dI" % tt
            def fin(kc):
                P.op("dve", lambda e, kc=kc: e.scalar_tensor_tensor(
                    out=xT[:, kc, 0:N], in0=xT[:, kc, 0:N], scalar=cF[:, CF_GFIN + kc:CF_GFIN + kc + 1],
                    in1=rs[:, 0, 0:N], op0=ALU.mult, op1=ALU.mult),
                    reads=[("xT", kc), ("rs", 0), "cF"], writes=[("xT", kc)])
            rmsnorm_to_h(N, CF_GFIN, out_fp32=fin)
            store_rows(lambda c: xT[:, c, 0:N], [("xT", kc) for kc in range(8)], ys_d if sample else y_d,
                       0 if sample else t0, N, 8)

        lnb = {}

        def conv_epilogue(j, b, N, src_is_sbuf=None):
            if j == 0:
                lnb["sum"] = reserve()
                lnb["sq"] = reserve()
            src = ps[b][:, 0:N] if src_is_sbuf is None else src_is_sbuf
            rd = [("ps", b)] if src_is_sbuf is None else [("t1", 0)]
            P.op("act", lambda e: e.activation(out=cT[:, j, 0:N], in_=src, func=AF.Identity,
                                              bias=cF[:, CF_BDW + j:CF_BDW + j + 1]),
                 reads=rd + ["cF"], writes=[("cT", j)])
            i = rot("sq")
            P.op("act", lambda e: e.activation(out=sq[:, i, 0:N], in_=src, func=AF.Square,
                                              bias=cF[:, CF_BDW + j:CF_BDW + j + 1]),
                 reads=rd + ["cF"], writes=[("sq", i)])
            bs, bq = lnb["sum"], lnb["sq"]
            P.op("pe", lambda e: e.matmul(ps[bs][:, 0:N], lhsT=onesb, rhs=cT[:, j, 0:N], start=(j == 0), stop=(j == 7)),
                 reads=[("cT", j), "cA"], writes=[("ps", bs)])
            P.op("pe", lambda e: e.matmul(ps[bq][:, 0:N], lhsT=onesb, rhs=sq[:, i, 0:N], start=(j == 0), stop=(j == 7)),
                 reads=[("sq", i), "cA"], writes=[("ps", bq)])

        def ln_finish(N):
            bs, bq = lnb["sum"], lnb["sq"]
            P.op("act", lambda e: e.mul(out=rs[:, 1, 0:N], in_=ps[bs][:, 0:N], mul=1.0 / D),
                 reads=[("ps", bs)], writes=[("rs", 1)])
            P.op("dve", lambda e: e.tensor_tensor(out=rs[:, 2, 0:N], in0=rs[:, 1, 0:N], in1=rs[:, 1, 0:N], op=ALU.mult),
                 reads=[("rs", 1)], writes=[("rs", 2)])
            P.op("dve", lambda e: e.scalar_tensor_tensor(out=rs[:, 2, 0:N], in0=ps[bq][:, 0:N], scalar=1.0 / D,
                                                        in1=rs[:, 2, 0:N], op0=ALU.mult, op1=ALU.subtract),
                 reads=[("ps", bq), ("rs", 2)], writes=[("rs", 2)])
            P.op("act", lambda e: e.activation(out=rs[:, 2, 0:N], in_=rs[:, 2, 0:N], func=AF.Ln, bias=epsc),
                 reads=[("rs", 2), "cF"], writes=[("rs", 2)])
            P.op("act", lambda e: e.activation(out=rs[:, 2, 0:N], in_=rs[:, 2, 0:N], func=AF.Exp, scale=-0.5),
                 reads=[("rs", 2)], writes=[("rs", 2)])
            for j in range(8):
                it = rot("t1", 3)
                le = "dve"
                P.op(le, lambda e, j=j, it=it: e.tensor_tensor(out=t1[:, it, 0:N], in0=cT[:, j, 0:N], in1=rs[:, 1, 0:N], op=ALU.subtract),
                     reads=[("cT", j), ("rs", 1)], writes=[("t1", it)])
                P.op(le, lambda e, j=j, it=it: e.tensor_tensor(out=t1[:, it, 0:N], in0=t1[:, it, 0:N], in1=rs[:, 2, 0:N], op=ALU.mult),
                     reads=[("t1", it), ("rs", 2)], writes=[("t1", it)])
                P.op("act", lambda e, j=j, it=it: e.activation(out=cT[:, j, 0:N], in_=t1[:, it, 0:N], func=AF.Silu,
                                                             scale=cF[:, CF_LNG + j:CF_LNG + j + 1],
                                                             bias=cF[:, CF_LNB + j:CF_LNB + j + 1]),
                     reads=[("t1", it), "cF"], writes=[("cT", j)])
            release(bs)
            release(bq)

        def prompt_attention(tt, N):
            first_write = {0: True, 1: True}
            for g in range(3):
                nsb = 16 if g == 2 else 4
                wsb = 32 if g == 2 else 128
                for cc in range(2):
                    c = 2 * g + cc
                    bO = reserve()
                    bL = reserve()
                    P.op("dve", lambda e, bO=bO: e.memset(ps[bO][:, :], 0.0), writes=[("ps", bO)])
                    P.op("dve", lambda e, bL=bL: e.memset(ps[bL][:, :], 0.0), writes=[("ps", bL)])
                    started = set()
                    pending = [None]
                    for hh in range(2):
                        for ksel in (0, 1):
                            if g == 2 and ksel == 1:
                                continue
                            if g == 1 and ksel == 1 and tt == 0:
                                continue
                            sc = []
                            for sbk in range(nsb):
                                if g == 0:
                                    B = 4 * tt + sbk - ksel
                                    if B < 0:
                                        continue
                                    kt = kAB[:, c, (B // 4) % 2, (B % 4) * 128:(B % 4) * 128 + 128]
                                    qv = qm[:, hh, c, sbk * 128:(sbk + 1) * 128]
                                    vt = vAB[:, (B // 4) % 2, B % 4, cc * 128 + hh * 64:cc * 128 + hh * 64 + 64]
                                elif g == 1:
                                    kp = (tt - ksel) % 2
                                    kt = kAB[:, c, kp, sbk:T:4]
                                    qv = qm[:, hh, c, sbk:T:4]
                                    vt = vAB[:, kp, 4 + sbk, cc * 128 + hh * 64:cc * 128 + hh * 64 + 64]
                                else:
                                    kt = kC[:, cc, sbk:S:16]
                                    qv = qm[:, hh, c, sbk:T:16]
                                    vt = vC[:, sbk, cc * 128 + hh * 64:cc * 128 + hh * 64 + 64]
                                sc.append((sbk, kt, qv, vt))
                            if not sc:
                                continue
                            bsc = nb()

                            def fsc(e, sc=sc, bsc=bsc, wsb=wsb):
                                for (sbk, kt, qv, vt) in sc:
                                    ins = e.matmul(ps[bsc][:, sbk * wsb:(sbk + 1) * wsb], lhsT=kt, rhs=qv, start=True, stop=True)
                                return ins
                            P.op("pe", fsc, reads=["kAB", "kC", ("qm", c)], writes=[("ps", bsc)])
                            ip = rot("pexp")
                            P.op("act", lambda e, bsc=bsc, ip=ip: e.activation(out=pexp[:, ip, 0:512], in_=ps[bsc][:, 0:512],
                                                                             func=AF.Exp, scale=0.125),
                                 reads=[("ps", bsc)], writes=[("pexp", ip)])
                            if g == 2:
                                mcol = CA_MC + 512 * tt
                            else:
                                mcol = CA_MOWN if ksel == 0 else CA_MPREV
                            P.op("dve", lambda e, ip=ip, mcol=mcol: e.tensor_tensor(
                                out=pexp[:, ip, 0:512], in0=pexp[:, ip, 0:512], in1=cA[:, mcol:mcol + 512], op=ALU.mult),
                                reads=[("pexp", ip), "cA"], writes=[("pexp", ip)])
                            pv = []
                            for (sbk, kt, qv, vt) in sc:
                                key = (hh, sbk)
                                st = key not in started
                                started.add(key)
                                if g == 2:
                                    fin_ = True
                                elif g == 1:
                                    fin_ = (ksel == 1) or tt == 0
                                else:
                                    fin_ = (ksel == 1) or (4 * tt + sbk - 1 < 0)
                                pv.append((sbk, vt, st, fin_))

                            def fpv(e, pv=pv, hh=hh, ip=ip, bO=bO, bL=bL, wsb=wsb):
                                for (sbk, vt, st, fin_) in pv:
                                    e.matmul(ps[bO][hh * 64:(hh + 1) * 64, sbk * wsb:(sbk + 1) * wsb], lhsT=vt,
                                             rhs=pexp[:, ip, sbk * wsb:(sbk + 1) * wsb], start=False, stop=fin_,
                                             skip_group_check=True)
                                    ins = e.matmul(ps[bL][hh * 64:(hh + 1) * 64, sbk * wsb:(sbk + 1) * wsb],
                                                   lhsT=onesb[:, 0:64],
                                                   rhs=pexp[:, ip, sbk * wsb:(sbk + 1) * wsb], start=False, stop=fin_,
                                                   skip_group_check=True)
                                return ins
                            if pending[0] is not None:
                                pending[0]()
                            pending[0] = (lambda fpv=fpv, ip=ip, bO=bO, bL=bL: P.op(
                                "pe", fpv, reads=[("pexp", ip), "vAB", "vC", "cA"], writes=[("ps", bO), ("ps", bL)]))
                            yield
                    if pending[0] is not None:
                        pending[0]()
                        pending[0] = None
                    if g == 0:
                        dO = accO[:, cc, 0:N]
                        dL = accL[:, cc, 0:N]
                        P.op("act", lambda e, dO=dO, bO=bO: e.copy(out=dO, in_=ps[bO][:, 0:N]),
                             reads=[("ps", bO)], writes=[("accO", cc)])
                        P.op("dve", lambda e, dL=dL, bL=bL: e.tensor_copy(out=dL, in_=ps[bL][:, 0:N]),
                             reads=[("ps", bL)], writes=[("accL", cc)])
                    else:
                        accumulate_v(bO, bL, cc, g, N, first_write)
                    release(bO)
                    release(bL)
                    yield
            for cc in range(2):
                P.op("act", lambda e, cc=cc: e.activation(out=accL[:, cc, 0:N], in_=accL[:, cc, 0:N], func=AF.Ln),
                     reads=[("accL", cc)], writes=[("accL", cc)])
                P.op("act", lambda e, cc=cc: e.activation(out=accL[:, cc, 0:N], in_=accL[:, cc, 0:N], func=AF.Exp, scale=-1.0),
                     reads=[("accL", cc)], writes=[("accL", cc)])
                P.op("dve", lambda e, cc=cc: e.tensor_tensor(out=oT[:, cc, 0:N], in0=accO[:, cc, 0:N], in1=accL[:, cc, 0:N], op=ALU.mult),
                     reads=[("accL", cc), ("accO", cc)], writes=["oT"])

        def accumulate_v(bO, bL, cc, g, N, first_write):
            r = 4 if g == 1 else 16
            m = N // r
            dO = accO[:, cc, 0:N].rearrange("p (m r) -> p r m", r=r)
            dL = accL[:, cc, 0:N].rearrange("p (m r) -> p r m", r=r)
            sO = ps[bO][:, 0:N].rearrange("p (r m) -> p r m", r=r)
            sL = ps[bL][:, 0:N].rearrange("p (r m) -> p r m", r=r)
            P.op("dve", lambda e: e.tensor_tensor(out=dO, in0=sO, in1=dO, op=ALU.add),
                 reads=[("ps", bO), ("accO", cc)], writes=[("accO", cc)])
            P.op("dve", lambda e: e.tensor_tensor(out=dL, in0=sL, in1=dL, op=ALU.add),
                 reads=[("ps", bL), ("accL", cc)], writes=[("accL", cc)])

        def sample_conv(N):
            P.op("sp", lambda e: e.dma_start(out=wrep, in_=wrep_d[:, :]), writes=["rope"], dma=True)
            for g4 in range(4):
                i = 0
                P.op("sp", lambda e, g4=g4, i=i: e.dma_start(out=stS[:, i, :], in_=st_d[g4 * 120:(g4 + 1) * 120, :]),
                     writes=[("stS", i)], dma=True)
                P.op("dve", lambda e, g4=g4, i=i: e.tensor_tensor(out=prod[:, g4, :], in0=stS[:, i, :], in1=wrep[:, :], op=ALU.mult),
                     reads=[("stS", i), "rope"], writes=["kC"])
            for j in range(8):
                b = nb()

                def f(e, j=j, b=b):
                    for g4 in range(4):
                        ins = e.matmul(ps[b][:, 0:N], lhsT=prod[:, g4, j * 128:(j + 1) * 128],
                                       rhs=cA[0:120, CA_SEL + 16 * g4:CA_SEL + 16 * g4 + 16], start=(g4 == 0), stop=(g4 == 3))
                    return ins
                P.op("pe", f, reads=["kC", "cA"], writes=[("ps", b)])
                P.op("dve", lambda e, j=j, b=b: e.scalar_tensor_tensor(
                    out=t1[:, 0, 0:N], in0=utf[:, j, 0:N], scalar=cF[:, CF_WDW + j * 31 + 30:CF_WDW + j * 31 + 31],
                    in1=ps[b][:, 0:N], op0=ALU.mult, op1=ALU.add),
                    reads=[("utf", j), ("ps", b), "cF"], writes=[("t1", 0)])
                conv_epilogue(j, b, N, src_is_sbuf=t1[:, 0, 0:N])
            P.op("sp", lambda e: e.dma_start(out=convs_d[:, 0:29 * D],
                                            in_=st_d.rearrange("(s j) c -> s (j c)", j=30)[:, D:30 * D]),
                 dma=True)
            store_rows(lambda c: utf[:, c, 0:N], [("utf", j) for j in range(8)], convs_d, 0, N, 8, dst_col0=29 * D)

        def sample_attention(N):
            caches = [(ca_d, 128, 1), (cb_d, 512, 4), (cc_d, 2048, 16)]
            outs = [was_d, wbs_d, wcs_d]
            for g in range(3):
                L = caches[g][1]
                for kv in range(2):
                    store_rows(lambda c, g=g, kv=kv: kvn[:, kv * 6 + 2 * g + c, 0:N],
                               [("kvn", kv * 6 + 2 * g + c) for c in range(2)], outs[g], 0, N, 2,
                               dst_col0=(L - 1) * 512 + kv * 256)
            bO = reserve()
            bL = reserve()
            P.op("dve", lambda e: e.memset(ps[bO][:, :], 0.0), writes=[("ps", bO)])
            P.op("dve", lambda e: e.memset(ps[bL][:, :], 0.0), writes=[("ps", bL)])
            for s in range(N):
                for g in range(3):
                    src, L, dil = caches[g]
                    ik = rot("kt")
                    P.op("sp", lambda e, s=s, src=src, L=L, dil=dil, ik=ik: e.dma_start(
                        out=ktile[:, ik, :], in_=src[s, 0:L:dil, :]), writes=[("kt", ik)], dma=True)
                    iq = rot("qrep")
                    bq = nb()
                    for cc in range(2):
                        c = 2 * g + cc
                        P.op("dve", lambda e, c=c, cc=cc, s=s, iq=iq: e.tensor_tensor(
                            out=qrep[:, iq, cc, :], in0=identb, in1=zbq[:, c, s:s + 1].to_broadcast([128, 128]), op=ALU.mult),
                            reads=["zbq", "cA"], writes=[("qrep", iq, cc)])
                        P.op("pe", lambda e, cc=cc, bq=bq, iq=iq: e.matmul(ps[bq][:, cc * 128:(cc + 1) * 128], lhsT=onesb, rhs=qrep[:, iq, cc, :],
                                                                  start=True, stop=True),
                             reads=[("qrep", iq, cc), "cA"], writes=[("ps", bq)])
                    it = rot("t1")
                    P.op("dve", lambda e, ik=ik, bq=bq, it=it: e.tensor_tensor(
                        out=t1[:, it, 0:256], in0=ktile[:, ik, 0:256], in1=ps[bq][:, 0:256], op=ALU.mult),
                        reads=[("kt", ik), ("ps", bq)], writes=[("t1", it)])
                    col = (s * 3 + g) * 4
                    P.op("dve", lambda e, it=it, col=col: e.tensor_reduce(
                        out=ssc[:, col:col + 4], in_=t1[:, it, 0:256].rearrange("p (h d) -> p h d", h=4),
                        axis=AX.X, op=ALU.add),
                        reads=[("t1", it)], writes=["ssc"])
                    P.op("act", lambda e, col=col: e.activation(out=pss_[:, col:col + 4], in_=ssc[:, col:col + 4], func=AF.Exp, scale=0.125),
                         reads=["ssc"], writes=["pss"])
                    P.op("act", lambda e, ik=ik: e.copy(out=zb[:, 0, 0:256], in_=ktile[:, ik, 256:512]),
                         reads=[("kt", ik)], writes=[("zb", 0)])

                    def fpv(e, s=s, g=g, col=col):
                        ins = None
                        for slot in range(4):
                            cc, hh = slot // 2, slot % 2
                            e.matmul(ps[bO][hh * 64:(hh + 1) * 64, cc * 16 + s:cc * 16 + s + 1],
                                     lhsT=zb[:, 0, slot * 64:(slot + 1) * 64], rhs=pss_[:, col + slot:col + slot + 1],
                                     start=False, stop=(g == 2), skip_group_check=True)
                            ins = e.matmul(ps[bL][hh * 64:(hh + 1) * 64, cc * 16 + s:cc * 16 + s + 1],
                                           lhsT=onesb[:, 0:64], rhs=pss_[:, col + slot:col + slot + 1],
                                           start=False, stop=(g == 2), skip_group_check=True)
                        return ins
                    P.op("pe", fpv, reads=[("zb", 0), "pss", "cA"], writes=[("ps", bO), ("ps", bL)])
            bS = nb()
            P.op("dve", lambda e: e.tensor_tensor(out=zbk[:, :, 0:N], in0=zbq[:, :, 0:N], in1=kvn[:, 0:6, 0:N], op=ALU.mult),
                 reads=["zbq"] + [("kvn", c) for c in range(6)], writes=["zbk"])
            P.op("pe", lambda e: e.matmul(ps[bS][:, 0:6 * N], lhsT=blkb, rhs=zbk[:, :, 0:N].rearrange("p c n -> p (c n)"),
                                          start=True, stop=True),
                 reads=["zbk", "cA"], writes=[("ps", bS)])
            P.op("act", lambda e: e.activation(out=sig[:, 0, 0:6 * N], in_=ps[bS][:, 0:6 * N], func=AF.Exp, scale=0.125),
                 reads=[("ps", bS)], writes=[("sig", 0)])
            P.op("dve", lambda e: e.tensor_tensor(out=sig[:, 1, 0:6 * N], in0=sig[:, 0, 0:6 * N],
                                                 in1=kvn[:, 6:12, 0:N].rearrange("p c n -> p (c n)"), op=ALU.mult),
                 reads=[("sig", 0)] + [("kvn", 6 + c) for c in range(6)], writes=[("sig", 1)])
            for cc in range(2):
                P.op("dve", lambda e, cc=cc: e.tensor_copy(out=accO[:, cc, 0:N], in_=ps[bO][:, cc * 16:cc * 16 + N]),
                     reads=[("ps", bO)], writes=[("accO", cc)])
                P.op("dve", lambda e, cc=cc: e.tensor_copy(out=accL[:, cc, 0:N], in_=ps[bL][:, cc * 16:cc * 16 + N]),
                     reads=[("ps", bL)], writes=[("accL", cc)])
                for g in range(3):
                    c = 2 * g + cc
                    P.op("dve", lambda e, cc=cc, c=c: e.tensor_tensor(out=accO[:, cc, 0:N], in0=accO[:, cc, 0:N],
                                                                     in1=sig[:, 1, c * N:(c + 1) * N], op=ALU.add),
                         reads=[("sig", 1), ("accO", cc)], writes=[("accO", cc)])
                    P.op("dve", lambda e, cc=cc, c=c: e.tensor_tensor(out=accL[:, cc, 0:N], in0=accL[:, cc, 0:N],
                                                                     in1=sig[:, 0, c * N:(c + 1) * N], op=ALU.add),
                         reads=[("sig", 0), ("accL", cc)], writes=[("accL", cc)])
                P.op("dve", lambda e, cc=cc: e.reciprocal(out=accL[:, cc, 0:N], in_=accL[:, cc, 0:N]),
                     reads=[("accL", cc)], writes=[("accL", cc)])
                P.op("dve", lambda e, cc=cc: e.tensor_tensor(out=oT[:, cc, 0:N], in0=accO[:, cc, 0:N], in1=accL[:, cc, 0:N], op=ALU.mult),
                     reads=[("accL", cc), ("accO", cc)], writes=["oT"])
            release(bO)
            release(bL)

        zbq = sb("zbq", [128, 6, TS], BF16)
        zbk = sb("zbk", [128, 6, TS], BF16)

        per = TS // 4
        for tt in range(NT):
            if tt < KTILES and KSTAGE > 0:
                tile(tt, False)
        if KSAMPLE and KSTAGE > 0:
            tile(NT, True)
        while shift_q:
            pump_shift()
        if os.environ.get("KTAGS"):
            P.trace_tags = []
        P.emit()
        if P.trace_tags is not None:
            import json
            json.dump(P.trace_tags, open(os.environ["KTAGS"], "w"))
    return nc


_CACHE = {}


def pack_wstream(w):
    L, offs, tot = slab_offsets()
    out = np.empty((128, tot), np.float32)
    for (name, col0, k0, nk), off in zip(L, offs):
        W = w[name]
        blk = W[k0 * 128:(k0 + nk) * 128, col0:col0 + 128]
        out[:, off:off + nk * 128] = blk.reshape(nk, 128, 128).transpose(1, 0, 2).reshape(128, nk * 128)
    return out


def make_in_maps(x_prompt, x_sample, state_conv, cache_win_a, cache_win_b, cache_win_c, p_prompt, p_sample,
                 w_in, g_mix, w_dw, b_dw, ln_g, ln_b, w_conv_out, w_attn_out, w_o, g_ffn, w_ffn_in, w_ffn_out,
                 g_ple, w_ple_gate, w_ple_proj, g_final, cores=range(NCORES)):
    f = lambda a: np.asarray(a, np.float32)
    w = dict(w_in=f(w_in)[0], w_conv_out=f(w_conv_out)[0], w_attn_out=f(w_attn_out)[0], w_o=f(w_o)[0],
             w_ffn_in=f(w_ffn_in)[0], w_ffn_out=f(w_ffn_out)[0], w_ple_gate=f(w_ple_gate)[0],
             w_ple_proj=f(w_ple_proj)[0])
    wstream = pack_wstream(w)
    ca, rope = build_consts()
    cf = np.zeros((128, CF_N), np.float32)
    cf[:, CF_ID:CF_ID + 128] = np.eye(128, dtype=np.float32)
    cf[:, CF_EPS] = EPS
    cf[:, CF_GMIX:CF_GMIX + 8] = colvec(f(g_mix)[0])
    cf[:, CF_GFFN:CF_GFFN + 8] = colvec(f(g_ffn)[0])
    cf[:, CF_GPLE:CF_GPLE + 8] = colvec(f(g_ple)[0])
    cf[:, CF_GFIN:CF_GFIN + 8] = colvec(f(g_final))
    cf[:, CF_BDW:CF_BDW + 8] = colvec(f(b_dw)[0])
    cf[:, CF_LNG:CF_LNG + 8] = colvec(f(ln_g)[0])
    cf[:, CF_LNB:CF_LNB + 8] = colvec(f(ln_b)[0])
    wd = f(w_dw)[0]
    cf[:, CF_WDW:CF_WDW + 248] = wd.reshape(31, 8, 128).transpose(2, 1, 0).reshape(128, 248)
    wrep = np.ascontiguousarray(np.tile(wd[0:30], (4, 1)))
    xp, xs = f(x_prompt), f(x_sample)
    pp, psm = f(p_prompt)[0], f(p_sample)[0]
    stc = f(state_conv)[0]
    cwa, cwb, cwc = f(cache_win_a)[0], f(cache_win_b)[0], f(cache_win_c)[0]
    in_maps = []
    for c in cores:
        sl = slice(c * TS, (c + 1) * TS)
        in_maps.append(dict(
            x=np.ascontiguousarray(xp[c]), p=np.ascontiguousarray(pp[c]),
            xs=np.ascontiguousarray(xs[sl, 0]), pss=np.ascontiguousarray(psm[sl, 0]),
            state=np.ascontiguousarray(stc[sl].reshape(TS * 30, D)),
            cache_a=np.ascontiguousarray(cwa[sl].reshape(TS, 128, 512)),
            cache_b=np.ascontiguousarray(cwb[sl].reshape(TS, 512, 512)),
            cache_c=np.ascontiguousarray(cwc[sl].reshape(TS, 2048, 512)),
            wstream=wstream, constA=ca, constF=cf, rope=rope, wrep=wrep))
    return in_maps


def kernel(**inputs):
    in_maps = make_in_maps(**inputs)
    if "nc" not in _CACHE:
        _CACHE["nc"] = build_program()
    nc = _CACHE["nc"]
    res = run_bass_kernel_spmd(nc, in_maps, core_ids=list(range(NCORES)))
    R = res.results
    cat = lambda k: np.stack([r[k] for r in R], axis=0)
    y_prompt = cat("y").reshape(8, S, D)
    y_sample = np.concatenate([r["ys"] for r in R], axis=0).reshape(128, 1, D)
    new_conv_prompt = cat("conv_p").reshape(1, 8, 30, D)
    nwa_p = cat("wa_p").reshape(1, 8, 128, 2, 4, 64)
    nwb_p = cat("wb_p").reshape(1, 8, 512, 2, 4, 64)
    nwc_p = cat("wc_p").reshape(1, 8, 2048, 2, 4, 64)
    new_conv_sample = np.concatenate([r["conv_s"] for r in R], axis=0).reshape(1, 128, 30, D)
    nwa_s = np.concatenate([r["wa_s"] for r in R], axis=0).reshape(1, 128, 128, 2, 4, 64)
    nwb_s = np.concatenate([r["wb_s"] for r in R], axis=0).reshape(1, 128, 512, 2, 4, 64)
    nwc_s = np.concatenate([r["wc_s"] for r in R], axis=0).reshape(1, 128, 2048, 2, 4, 64)
    return (y_prompt, y_sample, new_conv_prompt, nwa_p, nwb_p, nwc_p, new_conv_sample, nwa_s, nwb_s, nwc_s)
```

```python
import os
import numpy as np
from contextlib import ExitStack
import concourse.bass as bass
import concourse.mybir as mybir
from concourse.bass_utils import run_bass_kernel_spmd

F32 = mybir.dt.float32
BF16 = mybir.dt.bfloat16
AF = mybir.ActivationFunctionType
ALU = mybir.AluOpType
AX = mybir.AxisListType

ENGINES = ("pe", "act", "dve", "pool", "sp")
COMPUTE = ("pe", "act", "dve", "pool")

D = 1024
KC = 8
S = 2048
T = 512
NT = 4
TS = 16
DFF = 2816
FC = 22
PAST = 8192
EPS = 1e-6
NCORES = 8
RING = 6
SLOTW = 11 * 128


class Op:
    __slots__ = ("eng", "fn", "reads", "writes", "dma", "deps", "sig", "sigval",
                 "dsem", "dval", "idx", "eidx", "waits", "tag")

    def __init__(self, eng, fn, reads, writes, dma):
        self.eng = eng
        self.fn = fn
        self.reads = tuple(reads)
        self.writes = tuple(writes)
        self.dma = dma
        self.deps = []
        self.sig = False
        self.sigval = 0
        self.dsem = None
        self.dval = 0
        self.waits = []


class Prog:
    def __init__(self, nc, n_dma_sems=24):
        self.nc = nc
        self.ops = []
        self.n_dma_sems = n_dma_sems
        self.tag = ""
        self.trace_tags = None

    def op(self, eng, fn, reads=(), writes=(), dma=False):
        pr = [r for r in reads if isinstance(r, tuple) and r[0] == "ps"]
        if pr:
            reads = [r for r in reads if not (isinstance(r, tuple) and r[0] == "ps")]
            writes = list(writes) + pr
        o = Op(eng, fn, reads, writes, dma)
        o.idx = len(self.ops)
        o.tag = self.tag
        self.ops.append(o)
        return o

    def analyze(self):
        last_w = {}
        readers = {}
        for o in self.ops:
            deps = set()
            for r in o.reads:
                w = last_w.get(r)
                if w is not None:
                    deps.add(w)
            for w_ in o.writes:
                w = last_w.get(w_)
                if w is not None:
                    deps.add(w)
                for rd in readers.get(w_, ()):
                    deps.add(rd)
            deps.discard(o.idx)
            o.deps = sorted(deps)
            for r in o.reads:
                readers.setdefault(r, []).append(o.idx)
            for w_ in o.writes:
                last_w[w_] = o.idx
                readers[w_] = []
        ecount = {e: 0 for e in ENGINES}
        for o in self.ops:
            o.eidx = ecount[o.eng]
            ecount[o.eng] += 1
        dma_uses = {}
        dma_rr = {e: 0 for e in ENGINES}
        wm = {e: {c: -1 for c in COMPUTE} for e in ENGINES}
        dwm = {e: {} for e in ENGINES}
        by_eng = {e: [] for e in ENGINES}
        for o in self.ops:
            by_eng[o.eng].append(o)
        for o in self.ops:
            waits_c = {}
            waits_d = {}
            if o.dma:
                k = dma_rr[o.eng] % self.n_dma_sems
                dma_rr[o.eng] += 1
                key = (o.eng, k)
                dma_uses[key] = dma_uses.get(key, 0) + 1
                o.dsem = key
                o.dval = 16 * dma_uses[key]
                if dma_uses[key] > 1:
                    waits_d[key] = o.dval - 16
            for d in o.deps:
                p = self.ops[d]
                if p.dma:
                    waits_d[p.dsem] = max(waits_d.get(p.dsem, 0), p.dval)
                else:
                    if p.eng == o.eng and not o.dma:
                        if o.eng == "pe":
                            continue
                        if o.eidx - p.eidx > 2:
                            continue
                    waits_c[p.eng] = max(waits_c.get(p.eng, -1), p.eidx)
            o.waits = []
            for ce, ei in waits_c.items():
                if ei > wm[o.eng][ce]:
                    wm[o.eng][ce] = ei
                    o.waits.append(("c", ce, ei))
            for key, val in waits_d.items():
                if val > dwm[o.eng].get(key, 0):
                    dwm[o.eng][key] = val
                    o.waits.append(("d", key, val))
        for o in self.ops:
            for w in o.waits:
                if w[0] == "c":
                    by_eng[w[1]][w[2]].sig = True
        for e in COMPUTE:
            c = 0
            for o in by_eng[e]:
                if o.sig:
                    c += 1
                o.sigval = c
        self.by_eng = by_eng

    def emit(self):
        nc = self.nc
        self.analyze()
        by_eng = self.by_eng
        with ExitStack() as es:
            csem = {e: es.enter_context(nc.semaphore("s_" + e)) for e in COMPUTE}
            dsem = {}
            for e in ENGINES:
                if any(o.dma for o in by_eng[e]):
                    for k in range(self.n_dma_sems):
                        dsem[(e, k)] = es.enter_context(nc.semaphore("d_%s_%d" % (e, k)))
            block = es.enter_context(nc.Block())
            dma_final = {}
            for o in self.ops:
                if o.dma:
                    dma_final[o.dsem] = max(dma_final.get(o.dsem, 0), o.dval)

            def run(ename, eng):
                for o in by_eng[ename]:
                    for w in o.waits:
                        if w[0] == "c":
                            p = by_eng[w[1]][w[2]]
                            eng.wait_ge(csem[w[1]], p.sigval)
                        else:
                            eng.wait_ge(dsem[w[1]], w[2])
                    if self.trace_tags is not None and ename in ("pe",):
                        cnt_ = [0]

                        class _Px:
                            def __getattr__(s_, nm, eng=eng, cnt_=cnt_):
                                a = getattr(eng, nm)
                                if nm in ("matmul", "transpose"):
                                    def w(*aa, **kk):
                                        cnt_[0] += 1
                                        return a(*aa, **kk)
                                    return w
                                return a
                        ins = o.fn(_Px())
                        self.trace_tags.append((o.tag, cnt_[0]))
                    else:
                        ins = o.fn(eng)
                    if o.dma:
                        ins.then_inc(dsem[o.dsem], 16)
                    elif o.sig:
                        ins.then_inc(csem[ename], 1)
                if ename == "sp":
                    for key, val in dma_final.items():
                        eng.wait_ge(dsem[key], val)

            @block.tensor
            def _(pe):
                run("pe", pe)

            @block.scalar
            def _(act):
                run("act", act)

            @block.vector
            def _(dve):
                run("dve", dve)

            @block.gpsimd
            def _(pool):
                run("pool", pool)

            @block.sync
            def _(sp):
                run("sp", sp)


def slab_list():
    L = []
    for j in range(8):
        L.append(("w_in", 1024 + 128 * j, 0, 8))
        L.append(("w_in", 128 * j, 0, 8))
    for c in range(6):
        L.append(("w_in", 2048 + 128 * c, 0, 8))
    for c in range(6):
        L.append(("w_in", 2816 + 128 * c, 0, 8))
    for c in range(6):
        L.append(("w_in", 3584 + 128 * c, 0, 8))
    def gates(j):
        L.append(("w_in", 4352 + 128 * j, 0, 8))
        L.append(("w_in", 5376 + 128 * j, 0, 8))
    gates(0)
    for j in range(8):
        if j + 1 < 8:
            gates(j + 1)
        L.append(("w_conv_out", 128 * j, 0, 8))
        L.append(("w_attn_out", 128 * j, 0, 2))
    for j in range(8):
        L.append(("w_o", 128 * j, 0, 8))
    for half in range(2):
        for f in range(11 * half, 11 * half + 11):
            L.append(("w_ffn_in", 128 * f, 0, 8))
            L.append(("w_ffn_in", DFF + 128 * f, 0, 8))
        for j in range(8):
            L.append(("w_ffn_out", 128 * j, 11 * half, 11))
    for j in range(8):
        L.append(("w_ple_gate", 128 * j, 0, 8))
        L.append(("w_ple_proj", 128 * j, 0, 2))
    return L


def slab_offsets():
    L = slab_list()
    offs = []
    o = 0
    for (_, _, _, nk) in L:
        offs.append(o)
        o += nk * 128
    return L, offs, o


CA_ONES, CA_RM, CA_BLK, CA_MOWN, CA_MPREV, CA_MC, CA_SEL, CA_ID = 0, 128, 256, 384, 896, 1408, 3456, 3520
CA_N = 3648
CF_ID, CF_EPS, CF_GMIX, CF_GFFN, CF_GPLE, CF_GFIN, CF_BDW, CF_LNG, CF_LNB, CF_WDW = 0, 128, 129, 137, 145, 153, 161, 169, 177, 185
CF_N = 185 + 248
NPOS = S + TS


def build_consts():
    ca = np.zeros((128, CA_N), np.float32)
    ca[:, CA_ONES:CA_ONES + 128] = 1.0
    m = np.arange(128)
    d = m % 64
    partner = np.where(d < 8, m + 8, np.where(d < 16, m - 8, m))
    rm = np.zeros((128, 128), np.float32)
    rm[partner, m] = 1.0
    ca[:, CA_RM:CA_RM + 128] = rm
    ca[:, CA_BLK:CA_BLK + 128] = (m[:, None] // 64 == m[None, :] // 64).astype(np.float32)
    own = (m[:, None] <= m[None, :]).astype(np.float32)
    prev = (m[:, None] >= m[None, :]).astype(np.float32)
    ca[:, CA_MOWN:CA_MOWN + 512] = np.tile(own, (1, 4))
    ca[:, CA_MPREV:CA_MPREV + 512] = np.tile(prev, (1, 4))
    for tt in range(4):
        mc = (m[:, None] <= (32 * tt + np.arange(32))[None, :]).astype(np.float32)
        ca[:, CA_MC + 512 * tt:CA_MC + 512 * (tt + 1)] = np.tile(mc, (1, 16))
    sel = np.zeros((128, 4, 16), np.float32)
    for g in range(4):
        for sl in range(4):
            sel[sl * 30:(sl + 1) * 30, g, 4 * g + sl] = 1.0
    ca[:, CA_SEL:CA_SEL + 64] = sel.reshape(128, 64)
    ca[:, CA_ID:CA_ID + 128] = np.eye(128, dtype=np.float32)
    half = 8
    inv_freq = (500000.0 ** (-np.arange(half, dtype=np.float32) / half)).astype(np.float32)
    pos = np.concatenate([np.arange(S), np.full(TS, PAST)]).astype(np.float32)
    ang = pos[None, :] * inv_freq[:, None]
    cos = np.cos(ang).astype(np.float32)
    sin = np.sin(ang).astype(np.float32)
    rope = np.zeros((128, 2, NPOS), np.float32)
    rope[:, 0, :] = 1.0
    for p in range(128):
        dd = p % 64
        if dd < 8:
            rope[p, 0] = cos[dd]
            rope[p, 1] = -sin[dd]
        elif dd < 16:
            rope[p, 0] = cos[dd - 8]
            rope[p, 1] = sin[dd - 8]
    return ca, rope


def colvec(v):
    return np.ascontiguousarray(np.asarray(v, np.float32).reshape(8, 128).T)


def build_program(debug=False):
    nc = bass.Bass("TRN2", target_bir_lowering=False)
    slabs, soffs, WTOT = slab_offsets()
    NSL = len(slabs)

    def din(name, shape):
        return nc.dram_tensor(name, list(shape), F32, kind="ExternalInput").ap()

    def dout(name, shape):
        return nc.dram_tensor(name, list(shape), F32, kind="ExternalOutput").ap()

    x_d = din("x", [S, D])
    p_d = din("p", [S, 256])
    xs_d = din("xs", [TS, D])
    pss_d = din("pss", [TS, 256])
    st_d = din("state", [TS * 30, D])
    ca_d = din("cache_a", [TS, 128, 512])
    cb_d = din("cache_b", [TS, 512, 512])
    cc_d = din("cache_c", [TS, 2048, 512])
    ws_d = din("wstream", [128, WTOT])
    cA_d = din("constA", [128, CA_N])
    cF_d = din("constF", [128, CF_N])
    rope_d = din("rope", [128, 2, NPOS])
    wrep_d = din("wrep", [120, D])

    y_d = dout("y", [S, D])
    ys_d = dout("ys", [TS, D])
    convp_d = dout("conv_p", [30, D])
    wap_d = dout("wa_p", [128, 512])
    wbp_d = dout("wb_p", [512, 512])
    wcp_d = dout("wc_p", [2048, 512])
    convs_d = dout("conv_s", [TS, 30 * D])
    was_d = dout("wa_s", [TS, 128 * 512])
    wbs_d = dout("wb_s", [TS, 512 * 512])
    wcs_d = dout("wc_s", [TS, 2048 * 512])
    dbg = {}

    P = Prog(nc)
    with ExitStack() as es:
        def sb(name, shape, dt):
            return es.enter_context(nc.sbuf_tensor("sb_" + name, list(shape), dt))

        xT = sb("xT", [128, 8, T], F32)
        hT = sb("hT", [128, 8, T], BF16)
        UW = 30 + T
        uT = sb("uT", [128, 8, UW], BF16)
        cT = sb("cT", [128, 8, T], BF16)
        qm = sb("qm", [128, 2, 6, T], BF16)
        kAB = sb("kAB", [128, 4, 2, T], BF16)
        kC = sb("kC", [128, 2, S], BF16)
        vAB = sb("vAB", [128, 2, 8, 256], BF16)
        vC = sb("vC", [128, 16, 256], BF16)
        accO = sb("accO", [128, 2, T], F32)
        accL = sb("accL", [128, 2, T], F32)
        oT = sb("oT", [128, 2, T], BF16)
        mg = uT[:, :, 30:30 + T]
        actb = sb("actb", [128, 11, T], BF16)
        ring = sb("ring", [128, RING, SLOTW], BF16)
        diag = sb("diag", [128, 2, 31, 128], BF16)
        rope = sb("rope", [128, 2, T], F32)
        cA = sb("cA", [128, CA_N], BF16)
        cF = sb("cF", [128, CF_N], F32)
        sig = sb("sig", [128, 4, T], F32)
        zf = sb("zf", [128, 2, T], F32)
        zb = sb("zb", [128, 2, T], BF16)
        t1 = sb("t1", [128, 3, T], F32)
        rs = sb("rs", [128, 3, T], F32)
        sq = sb("sq", [128, 2, T], BF16)
        pexp = sb("pexp", [128, 2, T], BF16)
        xst = sb("xst", [128, 2, D], F32)
        pst = sb("pst", [128, 256], F32)
        pT = sb("pT", [128, 2, T], BF16)
        cst = sb("cst", [128, 2, 256], F32)
        stS = sb("stS", [120, 1, D], F32)
        prod = kC[0:120, :, :].rearrange("p a (b c) -> p (a b) c", c=D)
        wrep = rope[0:120, :, :].rearrange("p a b -> p (a b)")
        ktile = sb("ktile", [128, 2, 512], F32)
        ssc = sb("ssc", [128, 192], F32)
        pss_ = sb("pssb", [128, 192], BF16)
        qrep = sb("qrep", [128, 2, 2, 128], BF16)
        utf = sb("utf", [128, 8, 32], F32)
        kvn = sb("kvn", [128, 12, TS], F32)
        ps = [es.enter_context(nc.psum_tensor("ps%d" % i, [128, 512], F32)) for i in range(8)]

        identf = cF[:, CF_ID:CF_ID + 128]
        identb = cA[:, CA_ID:CA_ID + 128]
        onesb = cA[:, CA_ONES:CA_ONES + 128]
        rmb = cA[:, CA_RM:CA_RM + 128]
        blkb = cA[:, CA_BLK:CA_BLK + 128]
        epsc = cF[:, CF_EPS:CF_EPS + 1]

        cnt = {"bank": 0, "slab": 0, "pref": 0, "sig": 0, "zf": 0, "t1": 0, "sq": 0, "pexp": 0,
               "xst": 0, "cst": 0, "diag": 0, "zb": 0, "kt": 0, "qrep": 0, "stS": 0}

        reserved = set()

        def nb():
            while True:
                b = cnt["bank"] % 8
                cnt["bank"] += 1
                if b not in reserved:
                    return b

        def reserve():
            b = nb()
            reserved.add(b)
            return b

        def release(b):
            reserved.discard(b)

        def rot(name, n=2):
            i = cnt[name] % n
            cnt[name] += 1
            return i

        NTILES = NT + 1
        TOTSL = NSL * NTILES

        wscr = nc.dram_tensor("wscr", [128, WTOT], BF16, kind="Internal").ap()

        def prefetch():
            g = cnt["pref"]
            if g >= TOTSL:
                return
            cnt["pref"] += 1
            s = g % NSL
            slot = g % RING
            nk = slabs[s][3]
            off = soffs[s]
            if g < NSL:
                P.op("pool", lambda e, slot=slot, nk=nk, off=off: e.dma_start(
                    out=ring[:, slot, 0:nk * 128], in_=ws_d[:, off:off + nk * 128]),
                    writes=[("ring", slot)], dma=True)
                P.op("sp", lambda e, slot=slot, nk=nk, off=off: e.dma_start(
                    out=wscr[:, off:off + nk * 128], in_=ring[:, slot, 0:nk * 128]),
                    reads=[("ring", slot)], writes=[("wscr", s)], dma=True)
            else:
                P.op("pool", lambda e, slot=slot, nk=nk, off=off: e.dma_start(
                    out=ring[:, slot, 0:nk * 128], in_=wscr[:, off:off + nk * 128]),
                    reads=[("wscr", s)], writes=[("ring", slot)], dma=True)

        def take_slab():
            g = cnt["slab"]
            cnt["slab"] += 1
            return g % RING, slabs[g % NSL][3]

        def linear(rhs_fn, rhs_reads, N, fine=False):
            slot, nk = take_slab()
            b = nb()
            if fine:
                for k in range(nk):
                    P.op("pe", lambda e, slot=slot, nk=nk, b=b, k=k: e.matmul(
                        ps[b][:, 0:N], lhsT=ring[:, slot, k * 128:(k + 1) * 128], rhs=rhs_fn(k),
                        start=(k == 0), stop=(k == nk - 1)),
                        reads=[("ring", slot), rhs_reads[k]], writes=[("ps", b)])
                prefetch()
                if cnt["slab"] % 3 == 0:
                    pump_shift()
                return b

            def f(e, slot=slot, nk=nk, b=b):
                for k in range(nk):
                    ins = e.matmul(ps[b][:, 0:N], lhsT=ring[:, slot, k * 128:(k + 1) * 128],
                                   rhs=rhs_fn(k), start=(k == 0), stop=(k == nk - 1))
                return ins
            P.op("pe", f, reads=[("ring", slot)] + list(rhs_reads), writes=[("ps", b)])
            prefetch()
            if cnt["slab"] % 3 == 0:
                pump_shift()
            return b

        def rmsnorm_to_h(N, gcol0, out_fp32=None):
            bs = nb()
            for kc in range(8):
                i = rot("sq")
                P.op("act", lambda e, kc=kc, i=i: e.activation(out=sq[:, i, 0:N], in_=xT[:, kc, 0:N], func=AF.Square),
                     reads=[("xT", kc)], writes=[("sq", i)])
                P.op("pe", lambda e, kc=kc, i=i, bs=bs: e.matmul(ps[bs][:, 0:N], lhsT=onesb, rhs=sq[:, i, 0:N],
                                                                start=(kc == 0), stop=(kc == 7)),
                     reads=[("sq", i), "cA"], writes=[("ps", bs)])
            P.op("act", lambda e, bs=bs: e.activation(out=rs[:, 0, 0:N], in_=ps[bs][:, 0:N], func=AF.Ln,
                                                     scale=1.0 / D, bias=epsc),
                 reads=[("ps", bs), "cF"], writes=[("rs", 0)])
            P.op("act", lambda e: e.activation(out=rs[:, 0, 0:N], in_=rs[:, 0, 0:N], func=AF.Exp, scale=-0.5),
                 reads=[("rs", 0)], writes=[("rs", 0)])
            for kc in range(8):
                if out_fp32 is None:
                    P.op("dve", lambda e, kc=kc: e.scalar_tensor_tensor(
                        out=hT[:, kc, 0:N], in0=xT[:, kc, 0:N], scalar=cF[:, gcol0 + kc:gcol0 + kc + 1],
                        in1=rs[:, 0, 0:N], op0=ALU.mult, op1=ALU.mult),
                        reads=[("xT", kc), ("rs", 0), "cF"], writes=[("hT", kc)])
                else:
                    out_fp32(kc)

        def load_xT(src_d, row0, N):
            nblk = max(1, N // 128)
            nblk = min(nblk, int(os.environ.get("KNBLK", "99")))
            bw = min(128, N)
            for blk in range(nblk):
                i = rot("xst")
                P.op("sp", lambda e, blk=blk, i=i: e.dma_start(out=xst[0:bw, i, :],
                                                            in_=src_d[row0 + blk * 128:row0 + blk * 128 + bw, :]),
                     writes=[("xst", i)], dma=True)
                for half in range(2):
                    b = nb()

                    def f(e, i=i, b=b, half=half):
                        for q in range(4):
                            kc = half * 4 + q
                            ins = e.transpose(ps[b][:, q * 128:q * 128 + bw], xst[0:bw, i, kc * 128:(kc + 1) * 128],
                                              identf[0:bw, 0:bw])
                        return ins
                    P.op("pe", f, reads=[("xst", i), "cF"], writes=[("ps", b)])
                    for q in range(4 if not int(os.environ.get("KNOEVAC", "0")) else 0):
                        kc = half * 4 + q
                        eng = "act" if q % 2 == 0 else "dve"
                        if eng == "act":
                            P.op("act", lambda e, b=b, q=q, kc=kc, blk=blk: e.copy(
                                out=xT[:, kc, blk * 128:blk * 128 + bw], in_=ps[b][:, q * 128:q * 128 + bw]),
                                reads=[("ps", b)], writes=[("xT", kc)])
                        else:
                            P.op("dve", lambda e, b=b, q=q, kc=kc, blk=blk: e.tensor_copy(
                                out=xT[:, kc, blk * 128:blk * 128 + bw], in_=ps[b][:, q * 128:q * 128 + bw]),
                                reads=[("ps", b)], writes=[("xT", kc)])

        def store_rows(src_fn, src_reads, dst_d, row0, N, ncol_chunks, dst_col0=0):
            nblk = max(1, N // 128)
            bw = min(128, N)
            for blk in range(nblk):
                i = rot("xst")
                for c0 in range(0, ncol_chunks, 4):
                    b = nb()
                    nq = min(4, ncol_chunks - c0)

                    def f(e, b=b, c0=c0, nq=nq, blk=blk):
                        for q in range(nq):
                            ins = e.transpose(ps[b][0:bw, q * 128:(q + 1) * 128],
                                              src_fn(c0 + q)[:, blk * 128:blk * 128 + bw], identf)
                        return ins
                    P.op("pe", f, reads=list(src_reads) + ["cF"], writes=[("ps", b)])
                    eng = "act" if (c0 // 4) % 2 == 0 else "dve"
                    if eng == "act":
                        P.op("act", lambda e, b=b, c0=c0, nq=nq, i=i: e.copy(
                            out=xst[0:bw, i, c0 * 128:(c0 + nq) * 128], in_=ps[b][0:bw, 0:nq * 128]),
                            reads=[("ps", b)], writes=[("xst", i)])
                    else:
                        P.op("dve", lambda e, b=b, c0=c0, nq=nq, i=i: e.tensor_copy(
                            out=xst[0:bw, i, c0 * 128:(c0 + nq) * 128], in_=ps[b][0:bw, 0:nq * 128]),
                            reads=[("ps", b)], writes=[("xst", i)])
                P.op("act", lambda e, i=i, blk=blk: e.dma_start(
                    out=dst_d[row0 + blk * 128:row0 + blk * 128 + bw, dst_col0:dst_col0 + ncol_chunks * 128],
                    in_=xst[0:bw, i, 0:ncol_chunks * 128]),
                    reads=[("xst", i)], dma=True)

        P.op("pool", lambda e: e.dma_start(out=cA[:, :], in_=cA_d[:, :]), writes=["cA"], dma=True)
        P.op("sp", lambda e: e.dma_start(out=cF[:, :], in_=cF_d[:, :]), writes=["cF"], dma=True)
        dscr = nc.dram_tensor("dscr", [8, 128, 31 * 128], BF16, kind="Internal").ap()
        for j in range(8):
            dbuf = j % 2
            for tap in range(31):
                col = CF_WDW + j * 31 + tap
                if tap % 2 == 0:
                    P.op("dve", lambda e, tap=tap, dbuf=dbuf, col=col: e.tensor_scalar(
                        out=diag[:, dbuf, tap, :], in0=identb, scalar1=cF[:, col:col + 1], scalar2=None, op0=ALU.mult),
                        reads=["cA", "cF"], writes=[("diag", dbuf, tap)])
                else:
                    P.op("act", lambda e, tap=tap, dbuf=dbuf, col=col: e.activation(
                        out=diag[:, dbuf, tap, :], in_=identb, func=AF.Copy, scale=cF[:, col:col + 1]),
                        reads=["cA", "cF"], writes=[("diag", dbuf, tap)])
            P.op("sp", lambda e, j=j, dbuf=dbuf: e.dma_start(out=dscr[j], in_=diag[:, dbuf, :, :].rearrange("p a b -> p (a b)")),
                 reads=[("diag", dbuf, tap) for tap in range(31)], writes=[("dscr", j)], dma=True)
        for _ in range(RING):
            prefetch()
        P.op("dve", lambda e: e.memset(qm[:, :, :, :], 0.0), writes=[("qm", c) for c in range(6)])
        P.op("pool", lambda e: e.memset(kC[:, :, :], 0.0), writes=["kC"])
        P.op("pool", lambda e: e.memset(vC[:, :, :], 0.0), writes=["vC"])
        P.op("dve", lambda e: e.memset(uT[:, :, 0:30], 0.0), writes=[("uT", j) for j in range(8)])

        KSTAGE = int(os.environ.get("KSTAGE", "99"))
        KTILES = int(os.environ.get("KTILES", str(NT)))
        KSAMPLE = int(os.environ.get("KSAMPLE", "1"))
        KSHIFT = int(os.environ.get("KSHIFT", "1"))
        shift_q = []

        def shift_piece(src, dst, L, s_, k, nk):
            n = (L - 1) * 512 // nk
            P.op("act", lambda e: e.dma_start(
                out=dst[s_, k * n:(k + 1) * n].rearrange("(a e) -> a e", a=16),
                in_=src[s_, 1:L, :].rearrange("l c -> (l c)")[k * n:(k + 1) * n].rearrange("(a e) -> a e", a=16)),
                reads=[], writes=[], dma=True)

        for s_ in range(TS):
            for k in range(8):
                shift_q.append((cc_d, wcs_d, 2048, s_, k, 8))
            for k in range(2):
                shift_q.append((cb_d, wbs_d, 512, s_, k, 2))
            shift_q.append((ca_d, was_d, 128, s_, 0, 1))

        def pump_shift():
            if KSHIFT and shift_q:
                shift_piece(*shift_q.pop(0))


        def finish_tile(sample, t0, N):
            if int(os.environ.get("KNOSTORE", "0")):
                return
            store_rows(lambda c: xT[:, c, 0:N], [("xT", kc) for kc in range(8)], ys_d if sample else y_d,
                       0 if sample else t0, N, 8)

        def tile(tt, sample):
            N = TS if sample else T
            t0 = S if sample else tt * T
            par = tt % 2
            last = (tt == NT - 1) and not sample
            if sample:
                sample_conv_prefetch()
            if not int(os.environ.get("KNOROPE", "0")):
                P.op("sp", lambda e: e.dma_start(out=rope[:, :, 0:N], in_=rope_d[:, :, t0:t0 + N]),
                     writes=["rope"], dma=True)
            P.tag = "%dA" % tt
            if sample:
                load_xT(xs_d, 0, N)
            else:
                load_xT(x_d, t0, N)
            if KSTAGE <= 1:
                return finish_tile(sample, t0, N)
            rmsnorm_to_h(N, CF_GMIX)
            nblk = max(1, N // 128)
            bw = min(128, N)
            psrc = pss_d if sample else p_d
            prow0 = 0 if sample else t0
            for blk in range(nblk):
                P.op("sp", lambda e, blk=blk: e.dma_start(out=pst[0:bw, :], in_=psrc[prow0 + blk * 128:prow0 + blk * 128 + bw, :]),
                     writes=["pst"], dma=True)
                b = nb()

                def f(e, b=b):
                    for cc in range(2):
                        ins = e.transpose(ps[b][:, cc * 128:cc * 128 + bw], pst[0:bw, cc * 128:(cc + 1) * 128], identf[0:bw, 0:bw])
                    return ins
                P.op("pe", f, reads=["pst", "cF"], writes=[("ps", b)])
                P.op("act", lambda e, b=b, blk=blk: e.copy(
                    out=pT[:, :, blk * 128:blk * 128 + bw], in_=ps[b][:, 0:256].rearrange("p (c n) -> p c n", c=2)[:, :, 0:bw]),
                    reads=[("ps", b)], writes=["pT"])

            hreads = [("hT", kc) for kc in range(8)]
            hfn = lambda k: hT[:, k, 0:N]
            P.tag = "%dB1" % tt
            for j in range(8):
                bb = linear(hfn, hreads, N, fine=(j == 0))
                i = rot("sig")
                P.op("act", lambda e, bb=bb, i=i: e.activation(out=sig[:, i, 0:N], in_=ps[bb][:, 0:N], func=AF.Sigmoid),
                     reads=[("ps", bb)], writes=[("sig", i)])
                ba = linear(hfn, hreads, N)
                if sample:
                    P.op("dve", lambda e, ba=ba, i=i, j=j: e.tensor_tensor(
                        out=utf[:, j, 0:N], in0=ps[ba][:, 0:N], in1=sig[:, i, 0:N], op=ALU.mult),
                        reads=[("ps", ba), ("sig", i)], writes=[("utf", j)])
                else:
                    P.op("dve", lambda e, ba=ba, i=i, j=j: e.tensor_tensor(
                        out=uT[:, j, 30:30 + N], in0=ps[ba][:, 0:N], in1=sig[:, i, 0:N], op=ALU.mult),
                        reads=[("ps", ba), ("sig", i)], writes=[("uT", j)])
                    if last:
                        P.op("dve", lambda e, ba=ba, i=i, j=j: e.tensor_tensor(
                            out=utf[:, j, 0:32], in0=ps[ba][:, N - 32:N], in1=sig[:, i, N - 32:N], op=ALU.mult),
                            reads=[("ps", ba), ("sig", i)], writes=[("utf", j)])
            if KSTAGE <= 2:
                return finish_tile(sample, t0, N)
            if last:
                store_rows(lambda c: utf[:, c, 2:32], [("utf", j) for j in range(8)], convp_d, 0, 30, 8)
            P.tag = "%dB2" % tt
            Cc = rope[:, 0, 0:N]
            Ss = rope[:, 1, 0:N]

            def roped(bz):
                i = rot("zf")
                ib = rot("zb")
                P.op("act", lambda e: e.copy(out=zb[:, ib, 0:N], in_=ps[bz][:, 0:N]),
                     reads=[("ps", bz)], writes=[("zb", ib)])
                br = nb()
                P.op("pe", lambda e: e.matmul(ps[br][:, 0:N], lhsT=rmb, rhs=zb[:, ib, 0:N], start=True, stop=True),
                     reads=[("zb", ib), "cA"], writes=[("ps", br)])
                P.op("dve", lambda e: e.tensor_tensor(out=zf[:, i, 0:N], in0=ps[bz][:, 0:N], in1=Cc, op=ALU.mult),
                     reads=[("ps", bz), "rope"], writes=[("zf", i)])
                it = rot("t1")
                P.op("dve", lambda e: e.tensor_tensor(out=t1[:, it, 0:N], in0=ps[br][:, 0:N], in1=Ss, op=ALU.mult),
                     reads=[("ps", br), "rope"], writes=[("t1", it)])
                P.op("dve", lambda e: e.tensor_tensor(out=zf[:, i, 0:N], in0=zf[:, i, 0:N], in1=t1[:, it, 0:N], op=ALU.add),
                     reads=[("zf", i), ("t1", it)], writes=[("zf", i)])
                return i

            def cache_rows_out(src_ap_fn, src_reads, g, kv, qb_list):
                dst, L = [(wap_d, 128), (wbp_d, 512), (wcp_d, 2048)][g]
                for qb in qb_list:
                    tok0 = t0 + qb * 128
                    r0 = tok0 - (S - L)
                    if r0 < 0:
                        continue
                    ic = rot("cst")
                    b = nb()

                    def f(e, b=b, qb=qb):
                        for cc in range(2):
                            ins = e.transpose(ps[b][:, cc * 128:(cc + 1) * 128],
                                              src_ap_fn(cc)[:, qb * 128:(qb + 1) * 128], identf)
                        return ins
                    P.op("pe", f, reads=list(src_reads) + ["cF"], writes=[("ps", b)])
                    P.op("act", lambda e, b=b, ic=ic: e.copy(out=cst[:, ic, 0:256], in_=ps[b][:, 0:256]),
                         reads=[("ps", b)], writes=[("cst", ic)])
                    P.op("sp", lambda e, ic=ic, r0=r0, dst=dst, kv=kv: e.dma_start(
                        out=dst[r0:r0 + 128, kv * 256:(kv + 1) * 256], in_=cst[:, ic, 0:256]),
                        reads=[("cst", ic)], dma=True)

            for c in range(6):
                bz = linear(hfn, hreads, N)
                i = roped(bz)
                if sample:
                    P.op("act", lambda e, i=i, c=c: e.copy(out=zbq[:, c, 0:N], in_=zf[:, i, 0:N]),
                         reads=[("zf", i)], writes=["zbq"])
                    continue
                P.op("act", lambda e, i=i, c=c: e.copy(out=qm[0:64, 0, c, 0:N], in_=zf[0:64, i, 0:N]),
                     reads=[("zf", i)], writes=[("qm", c)])
                P.op("dve", lambda e, i=i, c=c: e.tensor_copy(out=qm[64:128, 1, c, 0:N], in_=zf[64:128, i, 0:N]),
                     reads=[("zf", i)], writes=[("qm", c)])
            kpair = {}
            for c in range(6):
                bz = linear(hfn, hreads, N)
                i = roped(bz)
                g = c // 2
                if sample:
                    P.op("act", lambda e, i=i, c=c: e.copy(out=kvn[:, c, 0:N], in_=zf[:, i, 0:N]),
                         reads=[("zf", i)], writes=[("kvn", c)])
                else:
                    if g < 2:
                        P.op("act", lambda e, i=i, c=c: e.copy(out=kAB[:, c, par, 0:N], in_=zf[:, i, 0:N]),
                             reads=[("zf", i)], writes=["kAB"])
                    else:
                        P.op("act", lambda e, i=i, c=c: e.copy(out=kC[:, c - 4, t0:t0 + N], in_=zf[:, i, 0:N]),
                             reads=[("zf", i)], writes=["kC"])
                    dst, L = [(wap_d, 128), (wbp_d, 512), (wcp_d, 2048)][g]
                    for qb in range(4):
                        r0 = t0 + qb * 128 - (S - L)
                        if r0 < 0:
                            continue
                        ic = rot("cst")
                        b = nb()
                        P.op("pe", lambda e, b=b, i=i, qb=qb: e.transpose(ps[b][:, 0:128], zf[:, i, qb * 128:(qb + 1) * 128], identf),
                             reads=[("zf", i), "cF"], writes=[("ps", b)])
                        P.op("act", lambda e, b=b, ic=ic: e.copy(out=cst[:, ic, 0:128], in_=ps[b][:, 0:128]),
                             reads=[("ps", b)], writes=[("cst", ic)])
                        P.op("act", lambda e, ic=ic, r0=r0, dst=dst, c=c: e.dma_start(
                            out=dst[r0:r0 + 128, (c % 2) * 128:(c % 2) * 128 + 128], in_=cst[:, ic, 0:128]),
                            reads=[("cst", ic)], dma=True)
            for c in range(6):
                bz = linear(hfn, hreads, N)
                g = c // 2
                cc = c % 2
                i = rot("zf")
                P.op("act", lambda e, bz=bz, i=i: e.copy(out=zf[:, i, 0:N], in_=ps[bz][:, 0:N]),
                     reads=[("ps", bz)], writes=[("zf", i)])
                if sample:
                    P.op("dve", lambda e, i=i, c=c: e.tensor_copy(out=kvn[:, 6 + c, 0:N], in_=zf[:, i, 0:N]),
                         reads=[("zf", i)], writes=[("kvn", 6 + c)])
                    continue
                dst, L = [(wap_d, 128), (wbp_d, 512), (wcp_d, 2048)][g]
                for qb in range(4):
                    r0 = t0 + qb * 128 - (S - L)
                    if r0 < 0 and g != 0:
                        continue
                    b = nb()
                    P.op("pe", lambda e, b=b, i=i, qb=qb: e.transpose(ps[b][:, 0:128], zf[:, i, qb * 128:(qb + 1) * 128], identf),
                         reads=[("zf", i), "cF"], writes=[("ps", b)])
                    if g == 0:
                        P.op("dve", lambda e, b=b, qb=qb, cc=cc: e.tensor_copy(
                            out=vAB[:, par, qb, cc * 128:(cc + 1) * 128], in_=ps[b][:, 0:128]),
                            reads=[("ps", b)], writes=["vAB"])
                    if r0 >= 0:
                        ic = rot("cst")
                        P.op("act", lambda e, b=b, ic=ic: e.copy(out=cst[:, ic, 0:128], in_=ps[b][:, 0:128]),
                             reads=[("ps", b)], writes=[("cst", ic)])
                        P.op("act", lambda e, ic=ic, r0=r0, dst=dst, cc=cc: e.dma_start(
                            out=dst[r0:r0 + 128, 256 + cc * 128:256 + cc * 128 + 128], in_=cst[:, ic, 0:128]),
                            reads=[("cst", ic)], dma=True)
                if g == 1:
                    for r in range(4):
                        b = nb()
                        P.op("pe", lambda e, b=b, i=i, r=r: e.transpose(ps[b][:, 0:128], zf[:, i, r:T:4], identf),
                             reads=[("zf", i), "cF"], writes=[("ps", b)])
                        P.op("dve", lambda e, b=b, r=r, cc=cc: e.tensor_copy(
                            out=vAB[:, par, 4 + r, cc * 128:(cc + 1) * 128], in_=ps[b][:, 0:128]),
                            reads=[("ps", b)], writes=["vAB"])
                if g == 2:
                    for r0_ in range(0, 16, 4):
                        b = nb()

                        def f(e, b=b, i=i, r0_=r0_):
                            for q in range(4):
                                ins = e.transpose(ps[b][0:32, q * 128:(q + 1) * 128], zf[:, i, (r0_ + q):T:16], identf)
                            return ins
                        P.op("pe", f, reads=[("zf", i), "cF"], writes=[("ps", b)])
                        P.op("dve", lambda e, b=b, r0_=r0_, cc=cc: e.tensor_copy(
                            out=vC[32 * tt:32 * tt + 32, r0_:r0_ + 4, cc * 128:(cc + 1) * 128],
                            in_=ps[b][0:32, 0:512].rearrange("p (q c) -> p q c", q=4)),
                            reads=[("ps", b)], writes=["vC"])

            if KSTAGE <= 3:
                return finish_tile(sample, t0, N)
            P.tag = "%dC" % tt
            attn_gen = None
            if not sample and KSTAGE > 4:
                attn_gen = prompt_attention(tt, N)

            def attn_step(n):
                nonlocal attn_gen
                for _ in range(n):
                    if attn_gen is None:
                        return
                    try:
                        next(attn_gen)
                    except StopIteration:
                        attn_gen = None
            if not sample:
                for j in range(8):
                    P.tag = "%dC" % tt
                    dbuf = rot("diag")
                    P.op("sp", lambda e, j=j, dbuf=dbuf: e.dma_start(
                        out=diag[:, dbuf, :, :].rearrange("p a b -> p (a b)"), in_=dscr[j]),
                        reads=[("dscr", j)], writes=[("diag", dbuf, tap) for tap in range(31)], dma=True)
                    b = nb()

                    def f(e, j=j, dbuf=dbuf, b=b):
                        for tap in range(31):
                            ins = e.matmul(ps[b][:, 0:N], lhsT=diag[:, dbuf, tap, :], rhs=uT[:, j, tap:tap + N],
                                           start=(tap == 0), stop=(tap == 30))
                        return ins
                    P.op("pe", f, reads=[("diag", dbuf, tap) for tap in range(31)] + [("uT", j)], writes=[("ps", b)])
                    conv_epilogue(j, b, N)
                    P.tag = "%dD" % tt
                    attn_step(3)
                P.tag = "%dC" % tt
                if not last:
                    P.op("dve", lambda e: e.tensor_copy(out=uT[:, :, 0:30], in_=uT[:, :, T:T + 30]),
                         reads=[("uT", j) for j in range(8)], writes=[("uT", j) for j in range(8)])
            else:
                sample_conv(N)
            P.tag = "%dD" % tt
            attn_step(1000)
            P.tag = "%dC" % tt
            ln_finish(N)

            if KSTAGE <= 4:
                return finish_tile(sample, t0, N)
            P.tag = "%dD" % tt
            if sample:
                sample_attention(N)
            else:
                attn_step(1000)

            if KSTAGE <= 5:
                return finish_tile(sample, t0, N)
            P.tag = "%dE" % tt
            cTreads = [("cT", kc) for kc in range(8)]
            gsig = {}

            def gate_pair(j):
                bgA = linear(hfn, hreads, N)
                iA = rot("sig", 4)
                P.op("act", lambda e: e.activation(out=sig[:, iA, 0:N], in_=ps[bgA][:, 0:N], func=AF.Sigmoid),
                     reads=[("ps", bgA)], writes=[("sig", iA)])
                bgB = linear(hfn, hreads, N)
                iB = rot("sig", 4)
                P.op("act", lambda e: e.activation(out=sig[:, iB, 0:N], in_=ps[bgB][:, 0:N], func=AF.Sigmoid),
                     reads=[("ps", bgB)], writes=[("sig", iB)])
                gsig[j] = (iA, iB)
            gate_pair(0)
            for j in range(8):
                if j + 1 < 8:
                    gate_pair(j + 1)
                i, i2 = gsig[j]
                bcv = linear(lambda k: cT[:, k, 0:N], cTreads, N)
                it = rot("t1")
                P.op("dve", lambda e, bcv=bcv, i=i, it=it: e.tensor_tensor(
                    out=t1[:, it, 0:N], in0=ps[bcv][:, 0:N], in1=sig[:, i, 0:N], op=ALU.mult),
                    reads=[("ps", bcv), ("sig", i)], writes=[("t1", it)])
                bat = linear(lambda k: oT[:, k, 0:N], ["oT"], N)
                P.op("dve", lambda e, bat=bat, i2=i2: e.tensor_tensor(
                    out=sig[:, i2, 0:N], in0=ps[bat][:, 0:N], in1=sig[:, i2, 0:N], op=ALU.mult),
                    reads=[("ps", bat), ("sig", i2)], writes=[("sig", i2)])
                P.op("dve", lambda e, i2=i2, it=it, j=j: e.tensor_tensor(
                    out=mg[:, j, 0:N], in0=sig[:, i2, 0:N], in1=t1[:, it, 0:N], op=ALU.add),
                    reads=[("sig", i2), ("t1", it)], writes=[("uT", j)])
            P.tag = "%dF" % tt
            mreads = [("uT", kc) for kc in range(8)]
            for j in range(8):
                b = linear(lambda k: mg[:, k, 0:N], mreads, N)
                P.op("dve", lambda e, b=b, j=j: e.tensor_tensor(out=xT[:, j, 0:N], in0=ps[b][:, 0:N], in1=xT[:, j, 0:N], op=ALU.add),
                     reads=[("ps", b), ("xT", j)], writes=[("xT", j)])
            P.tag = "%dG" % tt
            rmsnorm_to_h(N, CF_GFFN)
            for half in range(2):
                for f_ in range(11):
                    bg = linear(hfn, hreads, N, fine=(half == 0 and f_ == 0))
                    i = rot("sig")
                    P.op("act", lambda e, bg=bg, i=i: e.activation(out=sig[:, i, 0:N], in_=ps[bg][:, 0:N], func=AF.Silu),
                         reads=[("ps", bg)], writes=[("sig", i)])
                    bu = linear(hfn, hreads, N)
                    P.op("dve", lambda e, bu=bu, i=i, f_=f_: e.tensor_tensor(
                        out=actb[:, f_, 0:N], in0=ps[bu][:, 0:N], in1=sig[:, i, 0:N], op=ALU.mult),
                        reads=[("ps", bu), ("sig", i)], writes=[("actb", f_)])
                areads = [("actb", f_) for f_ in range(11)]
                for j in range(8):
                    b = linear(lambda k: actb[:, k, 0:N], areads, N)
                    P.op("dve", lambda e, b=b, j=j: e.tensor_tensor(out=xT[:, j, 0:N], in0=ps[b][:, 0:N], in1=xT[:, j, 0:N], op=ALU.add),
                         reads=[("ps", b), ("xT", j)], writes=[("xT", j)])
            P.tag = "%dH" % tt
            rmsnorm_to_h(N, CF_GPLE)
            for j in range(8):
                bg = linear(hfn, hreads, N, fine=(j == 0))
                i = rot("sig")
                P.op("act", lambda e, bg=bg, i=i: e.activation(out=sig[:, i, 0:N], in_=ps[bg][:, 0:N], func=AF.Sigmoid),
                     reads=[("ps", bg)], writes=[("sig", i)])
                bp = linear(lambda k: pT[:, k, 0:N], ["pT"], N)
                P.op("dve", lambda e, bp=bp, i=i: e.tensor_tensor(out=sig[:, i, 0:N], in0=ps[bp][:, 0:N], in1=sig[:, i, 0:N], op=ALU.mult),
                     reads=[("ps", bp), ("sig", i)], writes=[("sig", i)])
                P.op("dve", lambda e, i=i, j=j: e.tensor_tensor(out=xT[:, j, 0:N], in0=sig[:, i, 0:N], in1=xT[:, j, 0:N], op=ALU.add),
                     reads=[("sig", i), ("xT", j)], writes=[("xT", j)])
            P.tag = "%dI" % tt
            def fin(kc):
                P.op("dve", lambda e, kc=kc: e.scalar_tensor_tensor(
                    out=xT[:, kc, 0:N], in0=xT[:, kc, 0:N], scalar=cF[:, CF_GFIN + kc:CF_GFIN + kc + 1],
                    in1=rs[:, 0, 0:N], op0=ALU.mult, op1=ALU.mult),
                    reads=[("xT", kc), ("rs", 0), "cF"], writes=[("xT", kc)])
            rmsnorm_to_h(N, CF_GFIN, out_fp32=fin)
            store_rows(lambda c: xT[:, c, 0:N], [("xT", kc) for kc in range(8)], ys_d if sample else y_d,
                       0 if sample else t0, N, 8)

        lnb = {}

        def conv_epilogue(j, b, N, src_is_sbuf=None):
            if j == 0:
                lnb["sum"] = reserve()
                lnb["sq"] = reserve()
            src = ps[b][:, 0:N] if src_is_sbuf is None else src_is_sbuf
            rd = [("ps", b)] if src_is_sbuf is None else [("t1", 0)]
            P.op("act", lambda e: e.activation(out=cT[:, j, 0:N], in_=src, func=AF.Identity,
                                              bias=cF[:, CF_BDW + j:CF_BDW + j + 1]),
                 reads=rd + ["cF"], writes=[("cT", j)])
            i = rot("sq")
            P.op("act", lambda e: e.activation(out=sq[:, i, 0:N], in_=src, func=AF.Square,
                                              bias=cF[:, CF_BDW + j:CF_BDW + j + 1]),
                 reads=rd + ["cF"], writes=[("sq", i)])
            bs, bq = lnb["sum"], lnb["sq"]
            P.op("pe", lambda e: e.matmul(ps[bs][:, 0:N], lhsT=onesb, rhs=cT[:, j, 0:N], start=(j == 0), stop=(j == 7)),
                 reads=[("cT", j), "cA"], writes=[("ps", bs)])
            P.op("pe", lambda e: e.matmul(ps[bq][:, 0:N], lhsT=onesb, rhs=sq[:, i, 0:N], start=(j == 0), stop=(j == 7)),
                 reads=[("sq", i), "cA"], writes=[("ps", bq)])

        def ln_finish(N):
            bs, bq = lnb["sum"], lnb["sq"]
            P.op("act", lambda e: e.mul(out=rs[:, 1, 0:N], in_=ps[bs][:, 0:N], mul=1.0 / D),
                 reads=[("ps", bs)], writes=[("rs", 1)])
            P.op("dve", lambda e: e.tensor_tensor(out=rs[:, 2, 0:N], in0=rs[:, 1, 0:N], in1=rs[:, 1, 0:N], op=ALU.mult),
                 reads=[("rs", 1)], writes=[("rs", 2)])
            P.op("dve", lambda e: e.scalar_tensor_tensor(out=rs[:, 2, 0:N], in0=ps[bq][:, 0:N], scalar=1.0 / D,
                                                        in1=rs[:, 2, 0:N], op0=ALU.mult, op1=ALU.subtract),
                 reads=[("ps", bq), ("rs", 2)], writes=[("rs", 2)])
            P.op("act", lambda e: e.activation(out=rs[:, 2, 0:N], in_=rs[:, 2, 0:N], func=AF.Ln, bias=epsc),
                 reads=[("rs", 2), "cF"], writes=[("rs", 2)])
            P.op("act", lambda e: e.activation(out=rs[:, 2, 0:N], in_=rs[:, 2, 0:N], func=AF.Exp, scale=-0.5),
                 reads=[("rs", 2)], writes=[("rs", 2)])
            for j in range(8):
                it = rot("t1", 3)
                le = "dve"
                P.op(le, lambda e, j=j, it=it: e.tensor_tensor(out=t1[:, it, 0:N], in0=cT[:, j, 0:N], in1=rs[:, 1, 0:N], op=ALU.subtract),
                     reads=[("cT", j), ("rs", 1)], writes=[("t1", it)])
                P.op(le, lambda e, j=j, it=it: e.tensor_tensor(out=t1[:, it, 0:N], in0=t1[:, it, 0:N], in1=rs[:, 2, 0:N], op=ALU.mult),
                     reads=[("t1", it), ("rs", 2)], writes=[("t1", it)])
                P.op("act", lambda e, j=j, it=it: e.activation(out=cT[:, j, 0:N], in_=t1[:, it, 0:N], func=AF.Silu,
                                                             scale=cF[:, CF_LNG + j:CF_LNG + j + 1],
                                                             bias=cF[:, CF_LNB + j:CF_LNB + j + 1]),
                     reads=[("t1", it), "cF"], writes=[("cT", j)])
            release(bs)
            release(bq)

        def prompt_attention(tt, N):
            first_write = {0: True, 1: True}
            for g in range(3):
                nsb = 16 if g == 2 else 4
                wsb = 32 if g == 2 else 128
                for cc in range(2):
                    c = 2 * g + cc
                    bO = reserve()
                    bL = reserve()
                    P.op("dve", lambda e, bO=bO: e.memset(ps[bO][:, :], 0.0), writes=[("ps", bO)])
                    P.op("dve", lambda e, bL=bL: e.memset(ps[bL][:, :], 0.0), writes=[("ps", bL)])
                    started = set()
                    pending = [None]
                    for hh in range(2):
                        for ksel in (0, 1):
                            if g == 2 and ksel == 1:
                                continue
                            if g == 1 and ksel == 1 and tt == 0:
                                continue
                            sc = []
                            for sbk in range(nsb):
                                if g == 0:
                                    B = 4 * tt + sbk - ksel
                                    if B < 0:
                                        continue
                                    kt = kAB[:, c, (B // 4) % 2, (B % 4) * 128:(B % 4) * 128 + 128]
                                    qv = qm[:, hh, c, sbk * 128:(sbk + 1) * 128]
                                    vt = vAB[:, (B // 4) % 2, B % 4, cc * 128 + hh * 64:cc * 128 + hh * 64 + 64]
                                elif g == 1:
                                    kp = (tt - ksel) % 2
                                    kt = kAB[:, c, kp, sbk:T:4]
                                    qv = qm[:, hh, c, sbk:T:4]
                                    vt = vAB[:, kp, 4 + sbk, cc * 128 + hh * 64:cc * 128 + hh * 64 + 64]
                                else:
                                    kt = kC[:, cc, sbk:S:16]
                                    qv = qm[:, hh, c, sbk:T:16]
                                    vt = vC[:, sbk, cc * 128 + hh * 64:cc * 128 + hh * 64 + 64]
                                sc.append((sbk, kt, qv, vt))
                            if not sc:
                                continue
                            bsc = nb()

                            def fsc(e, sc=sc, bsc=bsc, wsb=wsb):
                                for (sbk, kt, qv, vt) in sc:
                                    ins = e.matmul(ps[bsc][:, sbk * wsb:(sbk + 1) * wsb], lhsT=kt, rhs=qv, start=True, stop=True)
                                return ins
                            P.op("pe", fsc, reads=["kAB", "kC", ("qm", c)], writes=[("ps", bsc)])
                            ip = rot("pexp")
                            P.op("act", lambda e, bsc=bsc, ip=ip: e.activation(out=pexp[:, ip, 0:512], in_=ps[bsc][:, 0:512],
                                                                             func=AF.Exp, scale=0.125),
                                 reads=[("ps", bsc)], writes=[("pexp", ip)])
                            if g == 2:
                                mcol = CA_MC + 512 * tt
                            else:
                                mcol = CA_MOWN if ksel == 0 else CA_MPREV
                            P.op("dve", lambda e, ip=ip, mcol=mcol: e.tensor_tensor(
                                out=pexp[:, ip, 0:512], in0=pexp[:, ip, 0:512], in1=cA[:, mcol:mcol + 512], op=ALU.mult),
                                reads=[("pexp", ip), "cA"], writes=[("pexp", ip)])
                            pv = []
                            for (sbk, kt, qv, vt) in sc:
                                key = (hh, sbk)
                                st = key not in started
                                started.add(key)
                                if g == 2:
                                    fin_ = True
                                elif g == 1:
                                    fin_ = (ksel == 1) or tt == 0
                                else:
                                    fin_ = (ksel == 1) or (4 * tt + sbk - 1 < 0)
                                pv.append((sbk, vt, st, fin_))

                            def fpv(e, pv=pv, hh=hh, ip=ip, bO=bO, bL=bL, wsb=wsb):
                                for (sbk, vt, st, fin_) in pv:
                                    e.matmul(ps[bO][hh * 64:(hh + 1) * 64, sbk * wsb:(sbk + 1) * wsb], lhsT=vt,
                                             rhs=pexp[:, ip, sbk * wsb:(sbk + 1) * wsb], start=False, stop=fin_,
                                             skip_group_check=True)
                                    ins = e.matmul(ps[bL][hh * 64:(hh + 1) * 64, sbk * wsb:(sbk + 1) * wsb],
                                                   lhsT=onesb[:, 0:64],
                                                   rhs=pexp[:, ip, sbk * wsb:(sbk + 1) * wsb], start=False, stop=fin_,
                                                   skip_group_check=True)
                                return ins
                            if pending[0] is not None:
                                pending[0]()
                            pending[0] = (lambda fpv=fpv, ip=ip, bO=bO, bL=bL: P.op(
                                "pe", fpv, reads=[("pexp", ip), "vAB", "vC", "cA"], writes=[("ps", bO), ("ps", bL)]))
                            yield
                    if pending[0] is not None:
                        pending[0]()
                        pending[0] = None
                    if g == 0:
                        dO = accO[:, cc, 0:N]
                        dL = accL[:, cc, 0:N]
                        P.op("act", lambda e, dO=dO, bO=bO: e.copy(out=dO, in_=ps[bO][:, 0:N]),
                             reads=[("ps", bO)], writes=[("accO", cc)])
                        P.op("dve", lambda e, dL=dL, bL=bL: e.tensor_copy(out=dL, in_=ps[bL][:, 0:N]),
                             reads=[("ps", bL)], writes=[("accL", cc)])
                    else:
                        accumulate_v(bO, bL, cc, g, N, first_write)
                    release(bO)
                    release(bL)
                    yield
            for cc in range(2):
                P.op("act", lambda e, cc=cc: e.activation(out=accL[:, cc, 0:N], in_=accL[:, cc, 0:N], func=AF.Ln),
                     reads=[("accL", cc)], writes=[("accL", cc)])
                P.op("act", lambda e, cc=cc: e.activation(out=accL[:, cc, 0:N], in_=accL[:, cc, 0:N], func=AF.Exp, scale=-1.0),
                     reads=[("accL", cc)], writes=[("accL", cc)])
                P.op("dve", lambda e, cc=cc: e.tensor_tensor(out=oT[:, cc, 0:N], in0=accO[:, cc, 0:N], in1=accL[:, cc, 0:N], op=ALU.mult),
                     reads=[("accL", cc), ("accO", cc)], writes=["oT"])

        def accumulate_v(bO, bL, cc, g, N, first_write):
            r = 4 if g == 1 else 16
            m = N // r
            dO = accO[:, cc, 0:N].rearrange("p (m r) -> p r m", r=r)
            dL = accL[:, cc, 0:N].rearrange("p (m r) -> p r m", r=r)
            sO = ps[bO][:, 0:N].rearrange("p (r m) -> p r m", r=r)
            sL = ps[bL][:, 0:N].rearrange("p (r m) -> p r m", r=r)
            P.op("dve", lambda e: e.tensor_tensor(out=dO, in0=sO, in1=dO, op=ALU.add),
                 reads=[("ps", bO), ("accO", cc)], writes=[("accO", cc)])
            P.op("dve", lambda e: e.tensor_tensor(out=dL, in0=sL, in1=dL, op=ALU.add),
                 reads=[("ps", bL), ("accL", cc)], writes=[("accL", cc)])

        def sample_conv_prefetch():
            P.op("sp", lambda e: e.dma_start(out=wrep, in_=wrep_d[:, :]), writes=["rope"], dma=True)
            for g4 in range(4):
                i = 0
                P.op("sp", lambda e, g4=g4, i=i: e.dma_start(out=stS[:, i, :], in_=st_d[g4 * 120:(g4 + 1) * 120, :]),
                     writes=[("stS", i)], dma=True)
                P.op("dve", lambda e, g4=g4, i=i: e.tensor_tensor(out=prod[:, g4, :], in0=stS[:, i, :], in1=wrep[:, :], op=ALU.mult),
                     reads=[("stS", i), "rope"], writes=["kC"])

        def sample_conv(N):
            for j in range(8):
                b = nb()

                def f(e, j=j, b=b):
                    for g4 in range(4):
                        ins = e.matmul(ps[b][:, 0:N], lhsT=prod[:, g4, j * 128:(j + 1) * 128],
                                       rhs=cA[0:120, CA_SEL + 16 * g4:CA_SEL + 16 * g4 + 16], start=(g4 == 0), stop=(g4 == 3))
                    return ins
                P.op("pe", f, reads=["kC", "cA"], writes=[("ps", b)])
                P.op("dve", lambda e, j=j, b=b: e.scalar_tensor_tensor(
                    out=t1[:, 0, 0:N], in0=utf[:, j, 0:N], scalar=cF[:, CF_WDW + j * 31 + 30:CF_WDW + j * 31 + 31],
                    in1=ps[b][:, 0:N], op0=ALU.mult, op1=ALU.add),
                    reads=[("utf", j), ("ps", b), "cF"], writes=[("t1", 0)])
                conv_epilogue(j, b, N, src_is_sbuf=t1[:, 0, 0:N])
            P.op("sp", lambda e: e.dma_start(out=convs_d[:, 0:29 * D],
                                            in_=st_d.rearrange("(s j) c -> s (j c)", j=30)[:, D:30 * D]),
                 dma=True)
            store_rows(lambda c: utf[:, c, 0:N], [("utf", j) for j in range(8)], convs_d, 0, N, 8, dst_col0=29 * D)

        def sample_attention(N):
            caches = [(ca_d, 128, 1), (cb_d, 512, 4), (cc_d, 2048, 16)]
            outs = [was_d, wbs_d, wcs_d]
            for g in range(3):
                L = caches[g][1]
                for kv in range(2):
                    store_rows(lambda c, g=g, kv=kv: kvn[:, kv * 6 + 2 * g + c, 0:N],
                               [("kvn", kv * 6 + 2 * g + c) for c in range(2)], outs[g], 0, N, 2,
                               dst_col0=(L - 1) * 512 + kv * 256)
            bO = reserve()
            bL = reserve()
            P.op("dve", lambda e: e.memset(ps[bO][:, :], 0.0), writes=[("ps", bO)])
            P.op("dve", lambda e: e.memset(ps[bL][:, :], 0.0), writes=[("ps", bL)])
            for s in range(N):
                for g in range(3):
                    src, L, dil = caches[g]
                    ik = rot("kt")
                    P.op("sp", lambda e, s=s, src=src, L=L, dil=dil, ik=ik: e.dma_start(
                        out=ktile[:, ik, :], in_=src[s, 0:L:dil, :]), writes=[("kt", ik)], dma=True)
                    iq = rot("qrep")
                    bq = nb()
                    for cc in range(2):
                        c = 2 * g + cc
                        P.op("dve", lambda e, c=c, cc=cc, s=s, iq=iq: e.tensor_tensor(
                            out=qrep[:, iq, cc, :], in0=identb, in1=zbq[:, c, s:s + 1].to_broadcast([128, 128]), op=ALU.mult),
                            reads=["zbq", "cA"], writes=[("qrep", iq, cc)])
                        P.op("pe", lambda e, cc=cc, bq=bq, iq=iq: e.matmul(ps[bq][:, cc * 128:(cc + 1) * 128], lhsT=onesb, rhs=qrep[:, iq, cc, :],
                                                                  start=True, stop=True),
                             reads=[("qrep", iq, cc), "cA"], writes=[("ps", bq)])
                    it = rot("t1")
                    P.op("dve", lambda e, ik=ik, bq=bq, it=it: e.tensor_tensor(
                        out=t1[:, it, 0:256], in0=ktile[:, ik, 0:256], in1=ps[bq][:, 0:256], op=ALU.mult),
                        reads=[("kt", ik), ("ps", bq)], writes=[("t1", it)])
                    col = (s * 3 + g) * 4
                    P.op("dve", lambda e, it=it, col=col: e.tensor_reduce(
                        out=ssc[:, col:col + 4], in_=t1[:, it, 0:256].rearrange("p (h d) -> p h d", h=4),
                        axis=AX.X, op=ALU.add),
                        reads=[("t1", it)], writes=[("ssc", col)])
                    P.op("act", lambda e, col=col: e.activation(out=pss_[:, col:col + 4], in_=ssc[:, col:col + 4], func=AF.Exp, scale=0.125),
                         reads=[("ssc", col)], writes=[("pss", col)])
                    iv = rot("zb")
                    P.op("act", lambda e, ik=ik, iv=iv: e.copy(out=zb[:, iv, 0:256], in_=ktile[:, ik, 256:512]),
                         reads=[("kt", ik)], writes=[("zb", iv)])

                    def fpv(e, s=s, g=g, col=col, iv=iv):
                        ins = None
                        for slot in range(4):
                            cc, hh = slot // 2, slot % 2
                            e.matmul(ps[bO][hh * 64:(hh + 1) * 64, cc * 16 + s:cc * 16 + s + 1],
                                     lhsT=zb[:, iv, slot * 64:(slot + 1) * 64], rhs=pss_[:, col + slot:col + slot + 1],
                                     start=False, stop=(g == 2), skip_group_check=True)
                            ins = e.matmul(ps[bL][hh * 64:(hh + 1) * 64, cc * 16 + s:cc * 16 + s + 1],
                                           lhsT=onesb[:, 0:64], rhs=pss_[:, col + slot:col + slot + 1],
                                           start=False, stop=(g == 2), skip_group_check=True)
                        return ins
                    P.op("pe", fpv, reads=[("zb", iv), ("pss", col), "cA"], writes=[("ps", bO), ("ps", bL)])
            bS = nb()
            P.op("dve", lambda e: e.tensor_tensor(out=zbk[:, :, 0:N], in0=zbq[:, :, 0:N], in1=kvn[:, 0:6, 0:N], op=ALU.mult),
                 reads=["zbq"] + [("kvn", c) for c in range(6)], writes=["zbk"])
            P.op("pe", lambda e: e.matmul(ps[bS][:, 0:6 * N], lhsT=blkb, rhs=zbk[:, :, 0:N].rearrange("p c n -> p (c n)"),
                                          start=True, stop=True),
                 reads=["zbk", "cA"], writes=[("ps", bS)])
            P.op("act", lambda e: e.activation(out=sig[:, 0, 0:6 * N], in_=ps[bS][:, 0:6 * N], func=AF.Exp, scale=0.125),
                 reads=[("ps", bS)], writes=[("sig", 0)])
            P.op("dve", lambda e: e.tensor_tensor(out=sig[:, 1, 0:6 * N], in0=sig[:, 0, 0:6 * N],
                                                 in1=kvn[:, 6:12, 0:N].rearrange("p c n -> p (c n)"), op=ALU.mult),
                 reads=[("sig", 0)] + [("kvn", 6 + c) for c in range(6)], writes=[("sig", 1)])
            for cc in range(2):
                P.op("dve", lambda e, cc=cc: e.tensor_copy(out=accO[:, cc, 0:N], in_=ps[bO][:, cc * 16:cc * 16 + N]),
                     reads=[("ps", bO)], writes=[("accO", cc)])
                P.op("dve", lambda e, cc=cc: e.tensor_copy(out=accL[:, cc, 0:N], in_=ps[bL][:, cc * 16:cc * 16 + N]),
                     reads=[("ps", bL)], writes=[("accL", cc)])
                for g in range(3):
                    c = 2 * g + cc
                    P.op("dve", lambda e, cc=cc, c=c: e.tensor_tensor(out=accO[:, cc, 0:N], in0=accO[:, cc, 0:N],
                                                                     in1=sig[:, 1, c * N:(c + 1) * N], op=ALU.add),
                         reads=[("sig", 1), ("accO", cc)], writes=[("accO", cc)])
                    P.op("dve", lambda e, cc=cc, c=c: e.tensor_tensor(out=accL[:, cc, 0:N], in0=accL[:, cc, 0:N],
                                                                     in1=sig[:, 0, c * N:(c + 1) * N], op=ALU.add),
                         reads=[("sig", 0), ("accL", cc)], writes=[("accL", cc)])
                P.op("dve", lambda e, cc=cc: e.reciprocal(out=accL[:, cc, 0:N], in_=accL[:, cc, 0:N]),
                     reads=[("accL", cc)], writes=[("accL", cc)])
                P.op("dve", lambda e, cc=cc: e.tensor_tensor(out=oT[:, cc, 0:N], in0=accO[:, cc, 0:N], in1=accL[:, cc, 0:N], op=ALU.mult),
                     reads=[("accL", cc), ("accO", cc)], writes=["oT"])
            release(bO)
            release(bL)

        zbq = sb("zbq", [128, 6, TS], BF16)
        zbk = sb("zbk", [128, 6, TS], BF16)

        per = TS // 4
        for tt in range(NT):
            if tt < KTILES and KSTAGE > 0:
                tile(tt, False)
        if KSAMPLE and KSTAGE > 0:
            tile(NT, True)
        while shift_q:
            pump_shift()
        if os.environ.get("KTAGS"):
            P.trace_tags = []
        P.emit()
        if P.trace_tags is not None:
            import json
            json.dump(P.trace_tags, open(os.environ["KTAGS"], "w"))
    return nc


_CACHE = {}


def pack_wstream(w):
    L, offs, tot = slab_offsets()
    out = np.empty((128, tot), np.float32)
    for (name, col0, k0, nk), off in zip(L, offs):
        W = w[name]
        blk = W[k0 * 128:(k0 + nk) * 128, col0:col0 + 128]
        out[:, off:off + nk * 128] = blk.reshape(nk, 128, 128).transpose(1, 0, 2).reshape(128, nk * 128)
    return out


def make_in_maps(x_prompt, x_sample, state_conv, cache_win_a, cache_win_b, cache_win_c, p_prompt, p_sample,
                 w_in, g_mix, w_dw, b_dw, ln_g, ln_b, w_conv_out, w_attn_out, w_o, g_ffn, w_ffn_in, w_ffn_out,
                 g_ple, w_ple_gate, w_ple_proj, g_final, cores=range(NCORES)):
    f = lambda a: np.asarray(a, np.float32)
    w = dict(w_in=f(w_in)[0], w_conv_out=f(w_conv_out)[0], w_attn_out=f(w_attn_out)[0], w_o=f(w_o)[0],
             w_ffn_in=f(w_ffn_in)[0], w_ffn_out=f(w_ffn_out)[0], w_ple_gate=f(w_ple_gate)[0],
             w_ple_proj=f(w_ple_proj)[0])
    wstream = pack_wstream(w)
    ca, rope = build_consts()
    cf = np.zeros((128, CF_N), np.float32)
    cf[:, CF_ID:CF_ID + 128] = np.eye(128, dtype=np.float32)
    cf[:, CF_EPS] = EPS
    cf[:, CF_GMIX:CF_GMIX + 8] = colvec(f(g_mix)[0])
    cf[:, CF_GFFN:CF_GFFN + 8] = colvec(f(g_ffn)[0])
    cf[:, CF_GPLE:CF_GPLE + 8] = colvec(f(g_ple)[0])
    cf[:, CF_GFIN:CF_GFIN + 8] = colvec(f(g_final))
    cf[:, CF_BDW:CF_BDW + 8] = colvec(f(b_dw)[0])
    cf[:, CF_LNG:CF_LNG + 8] = colvec(f(ln_g)[0])
    cf[:, CF_LNB:CF_LNB + 8] = colvec(f(ln_b)[0])
    wd = f(w_dw)[0]
    cf[:, CF_WDW:CF_WDW + 248] = wd.reshape(31, 8, 128).transpose(2, 1, 0).reshape(128, 248)
    wrep = np.ascontiguousarray(np.tile(wd[0:30], (4, 1)))
    xp, xs = f(x_prompt), f(x_sample)
    pp, psm = f(p_prompt)[0], f(p_sample)[0]
    stc = f(state_conv)[0]
    cwa, cwb, cwc = f(cache_win_a)[0], f(cache_win_b)[0], f(cache_win_c)[0]
    in_maps = []
    for c in cores:
        sl = slice(c * TS, (c + 1) * TS)
        in_maps.append(dict(
            x=np.ascontiguousarray(xp[c]), p=np.ascontiguousarray(pp[c]),
            xs=np.ascontiguousarray(xs[sl, 0]), pss=np.ascontiguousarray(psm[sl, 0]),
            state=np.ascontiguousarray(stc[sl].reshape(TS * 30, D)),
            cache_a=np.ascontiguousarray(cwa[sl].reshape(TS, 128, 512)),
            cache_b=np.ascontiguousarray(cwb[sl].reshape(TS, 512, 512)),
            cache_c=np.ascontiguousarray(cwc[sl].reshape(TS, 2048, 512)),
            wstream=wstream, constA=ca, constF=cf, rope=rope, wrep=wrep))
    return in_maps


def kernel(**inputs):
    in_maps = make_in_maps(**inputs)
    if "nc" not in _CACHE:
        _CACHE["nc"] = build_program()
    nc = _CACHE["nc"]
    res = run_bass_kernel_spmd(nc, in_maps, core_ids=list(range(NCORES)))
    R = res.results
    cat = lambda k: np.stack([r[k] for r in R], axis=0)
    y_prompt = cat("y").reshape(8, S, D)
    y_sample = np.concatenate([r["ys"] for r in R], axis=0).reshape(128, 1, D)
    new_conv_prompt = cat("conv_p").reshape(1, 8, 30, D)
    nwa_p = cat("wa_p").reshape(1, 8, 128, 2, 4, 64)
    nwb_p = cat("wb_p").reshape(1, 8, 512, 2, 4, 64)
    nwc_p = cat("wc_p").reshape(1, 8, 2048, 2, 4, 64)
    new_conv_sample = np.concatenate([r["conv_s"] for r in R], axis=0).reshape(1, 128, 30, D)
    nwa_s = np.concatenate([r["wa_s"] for r in R], axis=0).reshape(1, 128, 128, 2, 4, 64)
    nwb_s = np.concatenate([r["wb_s"] for r in R], axis=0).reshape(1, 128, 512, 2, 4, 64)
    nwc_s = np.concatenate([r["wc_s"] for r in R], axis=0).reshape(1, 128, 2048, 2, 4, 64)
    return (y_prompt, y_sample, new_conv_prompt, nwa_p, nwb_p, nwc_p, new_conv_sample, nwa_s, nwb_s, nwc_s)
```

```python
import os
import numpy as np
from contextlib import ExitStack
import concourse.bass as bass
import concourse.mybir as mybir
from concourse.bass_utils import run_bass_kernel_spmd

F32 = mybir.dt.float32
BF16 = mybir.dt.bfloat16
AF = mybir.ActivationFunctionType
ALU = mybir.AluOpType
AX = mybir.AxisListType

ENGINES = ("pe", "act", "dve", "pool", "sp")
COMPUTE = ("pe", "act", "dve", "pool")

D = 1024
KC = 8
S = 2048
T = 512
NT = 4
TS = 16
DFF = 2816
FC = 22
PAST = 8192
EPS = 1e-6
NCORES = 8
RING = 6
SLOTW = 11 * 128


class Op:
    __slots__ = ("eng", "fn", "reads", "writes", "dma", "deps", "sig", "sigval",
                 "dsem", "dval", "idx", "eidx", "waits", "tag")

    def __init__(self, eng, fn, reads, writes, dma):
        self.eng = eng
        self.fn = fn
        self.reads = tuple(reads)
        self.writes = tuple(writes)
        self.dma = dma
        self.deps = []
        self.sig = False
        self.sigval = 0
        self.dsem = None
        self.dval = 0
        self.waits = []


class Prog:
    def __init__(self, nc, n_dma_sems=24):
        self.nc = nc
        self.ops = []
        self.n_dma_sems = n_dma_sems
        self.tag = ""
        self.trace_tags = None

    def op(self, eng, fn, reads=(), writes=(), dma=False):
        pr = [r for r in reads if isinstance(r, tuple) and r[0] == "ps"]
        if pr:
            reads = [r for r in reads if not (isinstance(r, tuple) and r[0] == "ps")]
            writes = list(writes) + pr
        o = Op(eng, fn, reads, writes, dma)
        o.idx = len(self.ops)
        o.tag = self.tag
        self.ops.append(o)
        return o

    def analyze(self):
        last_w = {}
        readers = {}
        for o in self.ops:
            deps = set()
            for r in o.reads:
                w = last_w.get(r)
                if w is not None:
                    deps.add(w)
            for w_ in o.writes:
                w = last_w.get(w_)
                if w is not None:
                    deps.add(w)
                for rd in readers.get(w_, ()):
                    deps.add(rd)
            deps.discard(o.idx)
            o.deps = sorted(deps)
            for r in o.reads:
                readers.setdefault(r, []).append(o.idx)
            for w_ in o.writes:
                last_w[w_] = o.idx
                readers[w_] = []
        ecount = {e: 0 for e in ENGINES}
        for o in self.ops:
            o.eidx = ecount[o.eng]
            ecount[o.eng] += 1
        dma_uses = {}
        dma_rr = {e: 0 for e in ENGINES}
        wm = {e: {c: -1 for c in COMPUTE} for e in ENGINES}
        dwm = {e: {} for e in ENGINES}
        by_eng = {e: [] for e in ENGINES}
        for o in self.ops:
            by_eng[o.eng].append(o)
        for o in self.ops:
            waits_c = {}
            waits_d = {}
            if o.dma:
                k = dma_rr[o.eng] % self.n_dma_sems
                dma_rr[o.eng] += 1
                key = (o.eng, k)
                dma_uses[key] = dma_uses.get(key, 0) + 1
                o.dsem = key
                o.dval = 16 * dma_uses[key]
                if dma_uses[key] > 1:
                    waits_d[key] = o.dval - 16
            for d in o.deps:
                p = self.ops[d]
                if p.dma:
                    waits_d[p.dsem] = max(waits_d.get(p.dsem, 0), p.dval)
                else:
                    if p.eng == o.eng and not o.dma:
                        if o.eng == "pe":
                            continue
                        if o.eidx - p.eidx > 2:
                            continue
                    waits_c[p.eng] = max(waits_c.get(p.eng, -1), p.eidx)
            o.waits = []
            for ce, ei in waits_c.items():
                if ei > wm[o.eng][ce]:
                    wm[o.eng][ce] = ei
                    o.waits.append(("c", ce, ei))
            for key, val in waits_d.items():
                if val > dwm[o.eng].get(key, 0):
                    dwm[o.eng][key] = val
                    o.waits.append(("d", key, val))
        for o in self.ops:
            for w in o.waits:
                if w[0] == "c":
                    by_eng[w[1]][w[2]].sig = True
        for e in COMPUTE:
            c = 0
            for o in by_eng[e]:
                if o.sig:
                    c += 1
                o.sigval = c
        self.by_eng = by_eng

    def emit(self):
        nc = self.nc
        self.analyze()
        by_eng = self.by_eng
        with ExitStack() as es:
            csem = {e: es.enter_context(nc.semaphore("s_" + e)) for e in COMPUTE}
            dsem = {}
            for e in ENGINES:
                if any(o.dma for o in by_eng[e]):
                    for k in range(self.n_dma_sems):
                        dsem[(e, k)] = es.enter_context(nc.semaphore("d_%s_%d" % (e, k)))
            block = es.enter_context(nc.Block())
            dma_final = {}
            for o in self.ops:
                if o.dma:
                    dma_final[o.dsem] = max(dma_final.get(o.dsem, 0), o.dval)

            def run(ename, eng):
                for o in by_eng[ename]:
                    for w in o.waits:
                        if w[0] == "c":
                            p = by_eng[w[1]][w[2]]
                            eng.wait_ge(csem[w[1]], p.sigval)
                        else:
                            eng.wait_ge(dsem[w[1]], w[2])
                    if self.trace_tags is not None and ename in ("pe",):
                        cnt_ = [0]

                        class _Px:
                            def __getattr__(s_, nm, eng=eng, cnt_=cnt_):
                                a = getattr(eng, nm)
                                if nm in ("matmul", "transpose"):
                                    def w(*aa, **kk):
                                        cnt_[0] += 1
                                        return a(*aa, **kk)
                                    return w
                                return a
                        ins = o.fn(_Px())
                        self.trace_tags.append((o.tag, cnt_[0]))
                    else:
                        ins = o.fn(eng)
                    if o.dma:
                        ins.then_inc(dsem[o.dsem], 16)
                    elif o.sig:
                        ins.then_inc(csem[ename], 1)
                if ename == "sp":
                    for key, val in dma_final.items():
                        eng.wait_ge(dsem[key], val)

            @block.tensor
            def _(pe):
                run("pe", pe)

            @block.scalar
            def _(act):
                run("act", act)

            @block.vector
            def _(dve):
                run("dve", dve)

            @block.gpsimd
            def _(pool):
                run("pool", pool)

            @block.sync
            def _(sp):
                run("sp", sp)


def slab_list():
    L = []
    for j in range(8):
        L.append(("w_in", 1024 + 128 * j, 0, 8))
        L.append(("w_in", 128 * j, 0, 8))
    for c in range(6):
        L.append(("w_in", 2048 + 128 * c, 0, 8))
    for c in range(6):
        L.append(("w_in", 2816 + 128 * c, 0, 8))
    for c in range(6):
        L.append(("w_in", 3584 + 128 * c, 0, 8))
    def gates(j):
        L.append(("w_in", 4352 + 128 * j, 0, 8))
        L.append(("w_in", 5376 + 128 * j, 0, 8))
    gates(0)
    for j in range(8):
        if j + 1 < 8:
            gates(j + 1)
        L.append(("w_conv_out", 128 * j, 0, 8))
        L.append(("w_attn_out", 128 * j, 0, 2))
    for j in range(8):
        L.append(("w_o", 128 * j, 0, 8))
    for half in range(2):
        for f in range(11 * half, 11 * half + 11):
            L.append(("w_ffn_in", 128 * f, 0, 8))
            L.append(("w_ffn_in", DFF + 128 * f, 0, 8))
        for j in range(8):
            L.append(("w_ffn_out", 128 * j, 11 * half, 11))
    for j in range(8):
        L.append(("w_ple_gate", 128 * j, 0, 8))
        L.append(("w_ple_proj", 128 * j, 0, 2))
    return L


def slab_offsets():
    L = slab_list()
    offs = []
    o = 0
    for (_, _, _, nk) in L:
        offs.append(o)
        o += nk * 128
    return L, offs, o


CA_ONES, CA_RM, CA_BLK, CA_MOWN, CA_MPREV, CA_MC, CA_SEL, CA_ID = 0, 128, 256, 384, 896, 1408, 3456, 3520
CA_N = 3648
CF_ID, CF_EPS, CF_GMIX, CF_GFFN, CF_GPLE, CF_GFIN, CF_BDW, CF_LNG, CF_LNB, CF_WDW = 0, 128, 129, 137, 145, 153, 161, 169, 177, 185
CF_N = 185 + 248
NPOS = S + TS


def build_consts():
    ca = np.zeros((128, CA_N), np.float32)
    ca[:, CA_ONES:CA_ONES + 128] = 1.0
    m = np.arange(128)
    d = m % 64
    partner = np.where(d < 8, m + 8, np.where(d < 16, m - 8, m))
    rm = np.zeros((128, 128), np.float32)
    rm[partner, m] = 1.0
    ca[:, CA_RM:CA_RM + 128] = rm
    ca[:, CA_BLK:CA_BLK + 128] = (m[:, None] // 64 == m[None, :] // 64).astype(np.float32)
    own = (m[:, None] <= m[None, :]).astype(np.float32)
    prev = (m[:, None] >= m[None, :]).astype(np.float32)
    ca[:, CA_MOWN:CA_MOWN + 512] = np.tile(own, (1, 4))
    ca[:, CA_MPREV:CA_MPREV + 512] = np.tile(prev, (1, 4))
    for tt in range(4):
        mc = (m[:, None] <= (32 * tt + np.arange(32))[None, :]).astype(np.float32)
        ca[:, CA_MC + 512 * tt:CA_MC + 512 * (tt + 1)] = np.tile(mc, (1, 16))
    sel = np.zeros((128, 4, 16), np.float32)
    for g in range(4):
        for sl in range(4):
            sel[sl * 30:(sl + 1) * 30, g, 4 * g + sl] = 1.0
    ca[:, CA_SEL:CA_SEL + 64] = sel.reshape(128, 64)
    ca[:, CA_ID:CA_ID + 128] = np.eye(128, dtype=np.float32)
    half = 8
    inv_freq = (500000.0 ** (-np.arange(half, dtype=np.float32) / half)).astype(np.float32)
    pos = np.concatenate([np.arange(S), np.full(TS, PAST)]).astype(np.float32)
    ang = pos[None, :] * inv_freq[:, None]
    cos = np.cos(ang).astype(np.float32)
    sin = np.sin(ang).astype(np.float32)
    rope = np.zeros((128, 2, NPOS), np.float32)
    rope[:, 0, :] = 1.0
    for p in range(128):
        dd = p % 64
        if dd < 8:
            rope[p, 0] = cos[dd]
            rope[p, 1] = -sin[dd]
        elif dd < 16:
            rope[p, 0] = cos[dd - 8]
            rope[p, 1] = sin[dd - 8]
    return ca, rope


def colvec(v):
    return np.ascontiguousarray(np.asarray(v, np.float32).reshape(8, 128).T)


def build_program(debug=False):
    nc = bass.Bass("TRN2", target_bir_lowering=False)
    slabs, soffs, WTOT = slab_offsets()
    NSL = len(slabs)

    def din(name, shape):
        return nc.dram_tensor(name, list(shape), F32, kind="ExternalInput").ap()

    def dout(name, shape):
        return nc.dram_tensor(name, list(shape), F32, kind="ExternalOutput").ap()

    x_d = din("x", [S, D])
    p_d = din("p", [S, 256])
    xs_d = din("xs", [TS, D])
    pss_d = din("pss", [TS, 256])
    st_d = din("state", [TS * 30, D])
    ca_d = din("cache_a", [TS, 128, 512])
    cb_d = din("cache_b", [TS, 512, 512])
    cc_d = din("cache_c", [TS, 2048, 512])
    ws_d = din("wstream", [128, WTOT])
    cA_d = din("constA", [128, CA_N])
    cF_d = din("constF", [128, CF_N])
    rope_d = din("rope", [128, 2, NPOS])
    wrep_d = din("wrep", [120, D])

    y_d = dout("y", [S, D])
    ys_d = dout("ys", [TS, D])
    convp_d = dout("conv_p", [30, D])
    wap_d = dout("wa_p", [128, 512])
    wbp_d = dout("wb_p", [512, 512])
    wcp_d = dout("wc_p", [2048, 512])
    convs_d = dout("conv_s", [TS, 30 * D])
    was_d = dout("wa_s", [TS, 128 * 512])
    wbs_d = dout("wb_s", [TS, 512 * 512])
    wcs_d = dout("wc_s", [TS, 2048 * 512])
    dbg = {}

    P = Prog(nc)
    with ExitStack() as es:
        def sb(name, shape, dt):
            return es.enter_context(nc.sbuf_tensor("sb_" + name, list(shape), dt))

        xT = sb("xT", [128, 8, T], F32)
        hT = sb("hT", [128, 8, T], BF16)
        UW = 30 + T
        uT = sb("uT", [128, 8, UW], BF16)
        cT = sb("cT", [128, 8, T], BF16)
        qm = sb("qm", [128, 2, 6, T], BF16)
        kAB = sb("kAB", [128, 4, 2, T], BF16)
        kC = sb("kC", [128, 2, S], BF16)
        vAB = sb("vAB", [128, 2, 8, 256], BF16)
        vC = sb("vC", [128, 16, 256], BF16)
        accO = sb("accO", [128, 2, T], F32)
        accL = sb("accL", [128, 2, T], F32)
        oT = sb("oT", [128, 2, T], BF16)
        mg = uT[:, :, 30:30 + T]
        actb = sb("actb", [128, 11, T], BF16)
        ring = sb("ring", [128, RING, SLOTW], BF16)
        diag = sb("diag", [128, 2, 31, 128], BF16)
        rope = sb("rope", [128, 2, T], F32)
        cA = sb("cA", [128, CA_N], BF16)
        cF = sb("cF", [128, CF_N], F32)
        sig = sb("sig", [128, 4, T], F32)
        zf = sb("zf", [128, 2, T], F32)
        zb = sb("zb", [128, 2, T], BF16)
        t1 = sb("t1", [128, 3, T], F32)
        rs = sb("rs", [128, 3, T], F32)
        sq = sb("sq", [128, 2, T], BF16)
        pexp = sb("pexp", [128, 2, T], BF16)
        xst = sb("xst", [128, 2, D], F32)
        pst = sb("pst", [128, 256], F32)
        pT = sb("pT", [128, 2, T], BF16)
        cst = sb("cst", [128, 2, 256], F32)
        stS = sb("stS", [120, 1, D], F32)
        prod = kC[0:120, :, :].rearrange("p a (b c) -> p (a b) c", c=D)
        wrep = rope[0:120, :, :].rearrange("p a b -> p (a b)")
        ktile = sb("ktile", [128, 2, 512], F32)
        ssc = sb("ssc", [128, 192], F32)
        pss_ = sb("pssb", [128, 192], BF16)
        qrep = sb("qrep", [128, 2, 2, 128], BF16)
        utf = sb("utf", [128, 8, 32], F32)
        kvn = sb("kvn", [128, 12, TS], F32)
        ps = [es.enter_context(nc.psum_tensor("ps%d" % i, [128, 512], F32)) for i in range(8)]

        identf = cF[:, CF_ID:CF_ID + 128]
        identb = cA[:, CA_ID:CA_ID + 128]
        onesb = cA[:, CA_ONES:CA_ONES + 128]
        rmb = cA[:, CA_RM:CA_RM + 128]
        blkb = cA[:, CA_BLK:CA_BLK + 128]
        epsc = cF[:, CF_EPS:CF_EPS + 1]

        cnt = {"bank": 0, "slab": 0, "pref": 0, "sig": 0, "zf": 0, "t1": 0, "sq": 0, "pexp": 0,
               "xst": 0, "cst": 0, "diag": 0, "zb": 0, "kt": 0, "qrep": 0, "stS": 0}

        reserved = set()

        def nb():
            while True:
                b = cnt["bank"] % 8
                cnt["bank"] += 1
                if b not in reserved:
                    return b

        def reserve():
            b = nb()
            reserved.add(b)
            return b

        def release(b):
            reserved.discard(b)

        def rot(name, n=2):
            i = cnt[name] % n
            cnt[name] += 1
            return i

        NTILES = NT + 1
        TOTSL = NSL * NTILES

        wscr = nc.dram_tensor("wscr", [128, WTOT], BF16, kind="Internal").ap()

        def prefetch():
            g = cnt["pref"]
            if g >= TOTSL:
                return
            cnt["pref"] += 1
            s = g % NSL
            slot = g % RING
            nk = slabs[s][3]
            off = soffs[s]
            if g < NSL:
                P.op("pool", lambda e, slot=slot, nk=nk, off=off: e.dma_start(
                    out=ring[:, slot, 0:nk * 128], in_=ws_d[:, off:off + nk * 128]),
                    writes=[("ring", slot)], dma=True)
                P.op("sp", lambda e, slot=slot, nk=nk, off=off: e.dma_start(
                    out=wscr[:, off:off + nk * 128], in_=ring[:, slot, 0:nk * 128]),
                    reads=[("ring", slot)], writes=[("wscr", s)], dma=True)
            else:
                P.op("pool", lambda e, slot=slot, nk=nk, off=off: e.dma_start(
                    out=ring[:, slot, 0:nk * 128], in_=wscr[:, off:off + nk * 128]),
                    reads=[("wscr", s)], writes=[("ring", slot)], dma=True)

        def take_slab():
            g = cnt["slab"]
            cnt["slab"] += 1
            return g % RING, slabs[g % NSL][3]

        def linear(rhs_fn, rhs_reads, N, fine=False):
            slot, nk = take_slab()
            b = nb()
            if fine:
                for k in range(nk):
                    P.op("pe", lambda e, slot=slot, nk=nk, b=b, k=k: e.matmul(
                        ps[b][:, 0:N], lhsT=ring[:, slot, k * 128:(k + 1) * 128], rhs=rhs_fn(k),
                        start=(k == 0), stop=(k == nk - 1)),
                        reads=[("ring", slot), rhs_reads[k]], writes=[("ps", b)])
                prefetch()
                if cnt["slab"] % 3 == 0:
                    pump_shift()
                return b

            def f(e, slot=slot, nk=nk, b=b):
                for k in range(nk):
                    ins = e.matmul(ps[b][:, 0:N], lhsT=ring[:, slot, k * 128:(k + 1) * 128],
                                   rhs=rhs_fn(k), start=(k == 0), stop=(k == nk - 1))
                return ins
            P.op("pe", f, reads=[("ring", slot)] + list(rhs_reads), writes=[("ps", b)])
            prefetch()
            if cnt["slab"] % 3 == 0:
                pump_shift()
            return b

        def rmsnorm_to_h(N, gcol0, out_fp32=None):
            bs = nb()
            for kc in range(8):
                i = rot("sq")
                P.op("act", lambda e, kc=kc, i=i: e.activation(out=sq[:, i, 0:N], in_=xT[:, kc, 0:N], func=AF.Square),
                     reads=[("xT", kc)], writes=[("sq", i)])
                P.op("pe", lambda e, kc=kc, i=i, bs=bs: e.matmul(ps[bs][:, 0:N], lhsT=onesb, rhs=sq[:, i, 0:N],
                                                                start=(kc == 0), stop=(kc == 7)),
                     reads=[("sq", i), "cA"], writes=[("ps", bs)])
            P.op("act", lambda e, bs=bs: e.activation(out=rs[:, 0, 0:N], in_=ps[bs][:, 0:N], func=AF.Ln,
                                                     scale=1.0 / D, bias=epsc),
                 reads=[("ps", bs), "cF"], writes=[("rs", 0)])
            P.op("act", lambda e: e.activation(out=rs[:, 0, 0:N], in_=rs[:, 0, 0:N], func=AF.Exp, scale=-0.5),
                 reads=[("rs", 0)], writes=[("rs", 0)])
            for kc in range(8):
                if out_fp32 is None:
                    P.op("dve", lambda e, kc=kc: e.scalar_tensor_tensor(
                        out=hT[:, kc, 0:N], in0=xT[:, kc, 0:N], scalar=cF[:, gcol0 + kc:gcol0 + kc + 1],
                        in1=rs[:, 0, 0:N], op0=ALU.mult, op1=ALU.mult),
                        reads=[("xT", kc), ("rs", 0), "cF"], writes=[("hT", kc)])
                else:
                    out_fp32(kc)

        def load_xT(src_d, row0, N):
            nblk = max(1, N // 128)
            nblk = min(nblk, int(os.environ.get("KNBLK", "99")))
            bw = min(128, N)
            for blk in range(nblk):
                i = rot("xst")
                P.op("sp", lambda e, blk=blk, i=i: e.dma_start(out=xst[0:bw, i, :],
                                                            in_=src_d[row0 + blk * 128:row0 + blk * 128 + bw, :]),
                     writes=[("xst", i)], dma=True)
                for half in range(2):
                    b = nb()

                    def f(e, i=i, b=b, half=half):
                        for q in range(4):
                            kc = half * 4 + q
                            ins = e.transpose(ps[b][:, q * 128:q * 128 + bw], xst[0:bw, i, kc * 128:(kc + 1) * 128],
                                              identf[0:bw, 0:bw])
                        return ins
                    P.op("pe", f, reads=[("xst", i), "cF"], writes=[("ps", b)])
                    for q in range(4 if not int(os.environ.get("KNOEVAC", "0")) else 0):
                        kc = half * 4 + q
                        eng = "act" if q % 2 == 0 else "dve"
                        if eng == "act":
                            P.op("act", lambda e, b=b, q=q, kc=kc, blk=blk: e.copy(
                                out=xT[:, kc, blk * 128:blk * 128 + bw], in_=ps[b][:, q * 128:q * 128 + bw]),
                                reads=[("ps", b)], writes=[("xT", kc)])
                        else:
                            P.op("dve", lambda e, b=b, q=q, kc=kc, blk=blk: e.tensor_copy(
                                out=xT[:, kc, blk * 128:blk * 128 + bw], in_=ps[b][:, q * 128:q * 128 + bw]),
                                reads=[("ps", b)], writes=[("xT", kc)])

        def store_rows(src_fn, src_reads, dst_d, row0, N, ncol_chunks, dst_col0=0):
            nblk = max(1, N // 128)
            bw = min(128, N)
            for blk in range(nblk):
                i = rot("xst")
                for c0 in range(0, ncol_chunks, 4):
                    b = nb()
                    nq = min(4, ncol_chunks - c0)

                    def f(e, b=b, c0=c0, nq=nq, blk=blk):
                        for q in range(nq):
                            ins = e.transpose(ps[b][0:bw, q * 128:(q + 1) * 128],
                                              src_fn(c0 + q)[:, blk * 128:blk * 128 + bw], identf)
                        return ins
                    P.op("pe", f, reads=list(src_reads) + ["cF"], writes=[("ps", b)])
                    eng = "act" if (c0 // 4) % 2 == 0 else "dve"
                    if eng == "act":
                        P.op("act", lambda e, b=b, c0=c0, nq=nq, i=i: e.copy(
                            out=xst[0:bw, i, c0 * 128:(c0 + nq) * 128], in_=ps[b][0:bw, 0:nq * 128]),
                            reads=[("ps", b)], writes=[("xst", i)])
                    else:
                        P.op("dve", lambda e, b=b, c0=c0, nq=nq, i=i: e.tensor_copy(
                            out=xst[0:bw, i, c0 * 128:(c0 + nq) * 128], in_=ps[b][0:bw, 0:nq * 128]),
                            reads=[("ps", b)], writes=[("xst", i)])
                P.op("act", lambda e, i=i, blk=blk: e.dma_start(
                    out=dst_d[row0 + blk * 128:row0 + blk * 128 + bw, dst_col0:dst_col0 + ncol_chunks * 128],
                    in_=xst[0:bw, i, 0:ncol_chunks * 128]),
                    reads=[("xst", i)], dma=True)

        P.op("pool", lambda e: e.dma_start(out=cA[:, :], in_=cA_d[:, :]), writes=["cA"], dma=True)
        P.op("sp", lambda e: e.dma_start(out=cF[:, :], in_=cF_d[:, :]), writes=["cF"], dma=True)
        dscr = nc.dram_tensor("dscr", [8, 128, 31 * 128], BF16, kind="Internal").ap()
        for j in range(8):
            dbuf = j % 2
            for tap in range(31):
                col = CF_WDW + j * 31 + tap
                if tap % 2 == 0:
                    P.op("dve", lambda e, tap=tap, dbuf=dbuf, col=col: e.tensor_scalar(
                        out=diag[:, dbuf, tap, :], in0=identb, scalar1=cF[:, col:col + 1], scalar2=None, op0=ALU.mult),
                        reads=["cA", "cF"], writes=[("diag", dbuf, tap)])
                else:
                    P.op("act", lambda e, tap=tap, dbuf=dbuf, col=col: e.activation(
                        out=diag[:, dbuf, tap, :], in_=identb, func=AF.Copy, scale=cF[:, col:col + 1]),
                        reads=["cA", "cF"], writes=[("diag", dbuf, tap)])
            P.op("sp", lambda e, j=j, dbuf=dbuf: e.dma_start(out=dscr[j], in_=diag[:, dbuf, :, :].rearrange("p a b -> p (a b)")),
                 reads=[("diag", dbuf, tap) for tap in range(31)], writes=[("dscr", j)], dma=True)
        for _ in range(RING):
            prefetch()
        P.op("dve", lambda e: e.memset(qm[:, :, :, :], 0.0), writes=[("qm", c) for c in range(6)])
        P.op("pool", lambda e: e.memset(kC[:, :, :], 0.0), writes=["kC"])
        P.op("pool", lambda e: e.memset(vC[:, :, :], 0.0), writes=["vC"])
        P.op("dve", lambda e: e.memset(uT[:, :, 0:30], 0.0), writes=[("uT", j) for j in range(8)])

        KSTAGE = int(os.environ.get("KSTAGE", "99"))
        KTILES = int(os.environ.get("KTILES", str(NT)))
        KSAMPLE = int(os.environ.get("KSAMPLE", "1"))
        KSHIFT = int(os.environ.get("KSHIFT", "1"))
        shift_q = []

        def shift_piece(src, dst, L, s_, k, nk):
            n = (L - 1) * 512 // nk
            P.op("act", lambda e: e.dma_start(
                out=dst[s_, k * n:(k + 1) * n].rearrange("(a e) -> a e", a=16),
                in_=src[s_, 1:L, :].rearrange("l c -> (l c)")[k * n:(k + 1) * n].rearrange("(a e) -> a e", a=16)),
                reads=[], writes=[], dma=True)

        for s_ in range(TS):
            for k in range(8):
                shift_q.append((cc_d, wcs_d, 2048, s_, k, 8))
            for k in range(2):
                shift_q.append((cb_d, wbs_d, 512, s_, k, 2))
            shift_q.append((ca_d, was_d, 128, s_, 0, 1))

        def pump_shift():
            if KSHIFT and shift_q:
                shift_piece(*shift_q.pop(0))


        def finish_tile(sample, t0, N):
            if int(os.environ.get("KNOSTORE", "0")):
                return
            store_rows(lambda c: xT[:, c, 0:N], [("xT", kc) for kc in range(8)], ys_d if sample else y_d,
                       0 if sample else t0, N, 8)

        def tile(tt, sample):
            N = TS if sample else T
            t0 = S if sample else tt * T
            par = tt % 2
            last = (tt == NT - 1) and not sample
            def rope_load():
                P.op("sp", lambda e: e.dma_start(out=rope[:, :, 0:N], in_=rope_d[:, :, t0:t0 + N]),
                     writes=["rope"], dma=True)
            P.tag = "%dA" % tt
            if sample:
                load_xT(xs_d, 0, N)
                sample_conv_prefetch()
                rope_load()
            else:
                rope_load()
                load_xT(x_d, t0, N)
            if KSTAGE <= 1:
                return finish_tile(sample, t0, N)
            rmsnorm_to_h(N, CF_GMIX)
            nblk = max(1, N // 128)
            bw = min(128, N)
            psrc = pss_d if sample else p_d
            prow0 = 0 if sample else t0
            for blk in range(nblk):
                P.op("sp", lambda e, blk=blk: e.dma_start(out=pst[0:bw, :], in_=psrc[prow0 + blk * 128:prow0 + blk * 128 + bw, :]),
                     writes=["pst"], dma=True)
                b = nb()

                def f(e, b=b):
                    for cc in range(2):
                        ins = e.transpose(ps[b][:, cc * 128:cc * 128 + bw], pst[0:bw, cc * 128:(cc + 1) * 128], identf[0:bw, 0:bw])
                    return ins
                P.op("pe", f, reads=["pst", "cF"], writes=[("ps", b)])
                P.op("act", lambda e, b=b, blk=blk: e.copy(
                    out=pT[:, :, blk * 128:blk * 128 + bw], in_=ps[b][:, 0:256].rearrange("p (c n) -> p c n", c=2)[:, :, 0:bw]),
                    reads=[("ps", b)], writes=["pT"])

            hreads = [("hT", kc) for kc in range(8)]
            hfn = lambda k: hT[:, k, 0:N]
            P.tag = "%dB1" % tt
            for j in range(8):
                bb = linear(hfn, hreads, N, fine=(j == 0))
                i = rot("sig")
                P.op("act", lambda e, bb=bb, i=i: e.activation(out=sig[:, i, 0:N], in_=ps[bb][:, 0:N], func=AF.Sigmoid),
                     reads=[("ps", bb)], writes=[("sig", i)])
                ba = linear(hfn, hreads, N)
                if sample:
                    P.op("dve", lambda e, ba=ba, i=i, j=j: e.tensor_tensor(
                        out=utf[:, j, 0:N], in0=ps[ba][:, 0:N], in1=sig[:, i, 0:N], op=ALU.mult),
                        reads=[("ps", ba), ("sig", i)], writes=[("utf", j)])
                else:
                    P.op("dve", lambda e, ba=ba, i=i, j=j: e.tensor_tensor(
                        out=uT[:, j, 30:30 + N], in0=ps[ba][:, 0:N], in1=sig[:, i, 0:N], op=ALU.mult),
                        reads=[("ps", ba), ("sig", i)], writes=[("uT", j)])
                    if last:
                        P.op("dve", lambda e, ba=ba, i=i, j=j: e.tensor_tensor(
                            out=utf[:, j, 0:32], in0=ps[ba][:, N - 32:N], in1=sig[:, i, N - 32:N], op=ALU.mult),
                            reads=[("ps", ba), ("sig", i)], writes=[("utf", j)])
            if KSTAGE <= 2:
                return finish_tile(sample, t0, N)
            if last:
                store_rows(lambda c: utf[:, c, 2:32], [("utf", j) for j in range(8)], convp_d, 0, 30, 8)
            P.tag = "%dB2" % tt
            Cc = rope[:, 0, 0:N]
            Ss = rope[:, 1, 0:N]

            def roped(bz):
                i = rot("zf")
                ib = rot("zb")
                P.op("act", lambda e: e.copy(out=zb[:, ib, 0:N], in_=ps[bz][:, 0:N]),
                     reads=[("ps", bz)], writes=[("zb", ib)])
                br = nb()
                P.op("pe", lambda e: e.matmul(ps[br][:, 0:N], lhsT=rmb, rhs=zb[:, ib, 0:N], start=True, stop=True),
                     reads=[("zb", ib), "cA"], writes=[("ps", br)])
                P.op("dve", lambda e: e.tensor_tensor(out=zf[:, i, 0:N], in0=ps[bz][:, 0:N], in1=Cc, op=ALU.mult),
                     reads=[("ps", bz), "rope"], writes=[("zf", i)])
                it = rot("t1")
                P.op("dve", lambda e: e.tensor_tensor(out=t1[:, it, 0:N], in0=ps[br][:, 0:N], in1=Ss, op=ALU.mult),
                     reads=[("ps", br), "rope"], writes=[("t1", it)])
                P.op("dve", lambda e: e.tensor_tensor(out=zf[:, i, 0:N], in0=zf[:, i, 0:N], in1=t1[:, it, 0:N], op=ALU.add),
                     reads=[("zf", i), ("t1", it)], writes=[("zf", i)])
                return i

            def cache_rows_out(src_ap_fn, src_reads, g, kv, qb_list):
                dst, L = [(wap_d, 128), (wbp_d, 512), (wcp_d, 2048)][g]
                for qb in qb_list:
                    tok0 = t0 + qb * 128
                    r0 = tok0 - (S - L)
                    if r0 < 0:
                        continue
                    ic = rot("cst")
                    b = nb()

                    def f(e, b=b, qb=qb):
                        for cc in range(2):
                            ins = e.transpose(ps[b][:, cc * 128:(cc + 1) * 128],
                                              src_ap_fn(cc)[:, qb * 128:(qb + 1) * 128], identf)
                        return ins
                    P.op("pe", f, reads=list(src_reads) + ["cF"], writes=[("ps", b)])
                    P.op("act", lambda e, b=b, ic=ic: e.copy(out=cst[:, ic, 0:256], in_=ps[b][:, 0:256]),
                         reads=[("ps", b)], writes=[("cst", ic)])
                    P.op("sp", lambda e, ic=ic, r0=r0, dst=dst, kv=kv: e.dma_start(
                        out=dst[r0:r0 + 128, kv * 256:(kv + 1) * 256], in_=cst[:, ic, 0:256]),
                        reads=[("cst", ic)], dma=True)

            for c in range(6):
                bz = linear(hfn, hreads, N)
                i = roped(bz)
                if sample:
                    P.op("act", lambda e, i=i, c=c: e.copy(out=zbq[:, c, 0:N], in_=zf[:, i, 0:N]),
                         reads=[("zf", i)], writes=["zbq"])
                    continue
                P.op("act", lambda e, i=i, c=c: e.copy(out=qm[0:64, 0, c, 0:N], in_=zf[0:64, i, 0:N]),
                     reads=[("zf", i)], writes=[("qm", c)])
                P.op("dve", lambda e, i=i, c=c: e.tensor_copy(out=qm[64:128, 1, c, 0:N], in_=zf[64:128, i, 0:N]),
                     reads=[("zf", i)], writes=[("qm", c)])
            kpair = {}
            for c in range(6):
                bz = linear(hfn, hreads, N)
                i = roped(bz)
                g = c // 2
                if sample:
                    P.op("act", lambda e, i=i, c=c: e.copy(out=kvn[:, c, 0:N], in_=zf[:, i, 0:N]),
                         reads=[("zf", i)], writes=[("kvn", c)])
                else:
                    if g < 2:
                        P.op("act", lambda e, i=i, c=c: e.copy(out=kAB[:, c, par, 0:N], in_=zf[:, i, 0:N]),
                             reads=[("zf", i)], writes=["kAB"])
                    else:
                        P.op("act", lambda e, i=i, c=c: e.copy(out=kC[:, c - 4, t0:t0 + N], in_=zf[:, i, 0:N]),
                             reads=[("zf", i)], writes=["kC"])
                    dst, L = [(wap_d, 128), (wbp_d, 512), (wcp_d, 2048)][g]
                    for qb in range(4):
                        r0 = t0 + qb * 128 - (S - L)
                        if r0 < 0:
                            continue
                        ic = rot("cst")
                        b = nb()
                        P.op("pe", lambda e, b=b, i=i, qb=qb: e.transpose(ps[b][:, 0:128], zf[:, i, qb * 128:(qb + 1) * 128], identf),
                             reads=[("zf", i), "cF"], writes=[("ps", b)])
                        P.op("act", lambda e, b=b, ic=ic: e.copy(out=cst[:, ic, 0:128], in_=ps[b][:, 0:128]),
                             reads=[("ps", b)], writes=[("cst", ic)])
                        P.op("act", lambda e, ic=ic, r0=r0, dst=dst, c=c: e.dma_start(
                            out=dst[r0:r0 + 128, (c % 2) * 128:(c % 2) * 128 + 128], in_=cst[:, ic, 0:128]),
                            reads=[("cst", ic)], dma=True)
            for c in range(6):
                bz = linear(hfn, hreads, N)
                g = c // 2
                cc = c % 2
                i = rot("zf")
                P.op("act", lambda e, bz=bz, i=i: e.copy(out=zf[:, i, 0:N], in_=ps[bz][:, 0:N]),
                     reads=[("ps", bz)], writes=[("zf", i)])
                if sample:
                    P.op("dve", lambda e, i=i, c=c: e.tensor_copy(out=kvn[:, 6 + c, 0:N], in_=zf[:, i, 0:N]),
                         reads=[("zf", i)], writes=[("kvn", 6 + c)])
                    continue
                dst, L = [(wap_d, 128), (wbp_d, 512), (wcp_d, 2048)][g]
                for qb in range(4):
                    r0 = t0 + qb * 128 - (S - L)
                    if r0 < 0 and g != 0:
                        continue
                    b = nb()
                    P.op("pe", lambda e, b=b, i=i, qb=qb: e.transpose(ps[b][:, 0:128], zf[:, i, qb * 128:(qb + 1) * 128], identf),
                         reads=[("zf", i), "cF"], writes=[("ps", b)])
                    if g == 0:
                        P.op("dve", lambda e, b=b, qb=qb, cc=cc: e.tensor_copy(
                            out=vAB[:, par, qb, cc * 128:(cc + 1) * 128], in_=ps[b][:, 0:128]),
                            reads=[("ps", b)], writes=["vAB"])
                    if r0 >= 0:
                        ic = rot("cst")
                        P.op("act", lambda e, b=b, ic=ic: e.copy(out=cst[:, ic, 0:128], in_=ps[b][:, 0:128]),
                             reads=[("ps", b)], writes=[("cst", ic)])
                        P.op("act", lambda e, ic=ic, r0=r0, dst=dst, cc=cc: e.dma_start(
                            out=dst[r0:r0 + 128, 256 + cc * 128:256 + cc * 128 + 128], in_=cst[:, ic, 0:128]),
                            reads=[("cst", ic)], dma=True)
                if g == 1:
                    for r in range(4):
                        b = nb()
                        P.op("pe", lambda e, b=b, i=i, r=r: e.transpose(ps[b][:, 0:128], zf[:, i, r:T:4], identf),
                             reads=[("zf", i), "cF"], writes=[("ps", b)])
                        P.op("dve", lambda e, b=b, r=r, cc=cc: e.tensor_copy(
                            out=vAB[:, par, 4 + r, cc * 128:(cc + 1) * 128], in_=ps[b][:, 0:128]),
                            reads=[("ps", b)], writes=["vAB"])
                if g == 2:
                    for r0_ in range(0, 16, 4):
                        b = nb()

                        def f(e, b=b, i=i, r0_=r0_):
                            for q in range(4):
                                ins = e.transpose(ps[b][0:32, q * 128:(q + 1) * 128], zf[:, i, (r0_ + q):T:16], identf)
                            return ins
                        P.op("pe", f, reads=[("zf", i), "cF"], writes=[("ps", b)])
                        P.op("dve", lambda e, b=b, r0_=r0_, cc=cc: e.tensor_copy(
                            out=vC[32 * tt:32 * tt + 32, r0_:r0_ + 4, cc * 128:(cc + 1) * 128],
                            in_=ps[b][0:32, 0:512].rearrange("p (q c) -> p q c", q=4)),
                            reads=[("ps", b)], writes=["vC"])

            if KSTAGE <= 3:
                return finish_tile(sample, t0, N)
            P.tag = "%dC" % tt
            attn_gen = None
            if not sample and KSTAGE > 4:
                attn_gen = prompt_attention(tt, N)

            def attn_step(n):
                nonlocal attn_gen
                for _ in range(n):
                    if attn_gen is None:
                        return
                    try:
                        next(attn_gen)
                    except StopIteration:
                        attn_gen = None
            if not sample:
                for j in range(8):
                    P.tag = "%dC" % tt
                    dbuf = rot("diag")
                    P.op("sp", lambda e, j=j, dbuf=dbuf: e.dma_start(
                        out=diag[:, dbuf, :, :].rearrange("p a b -> p (a b)"), in_=dscr[j]),
                        reads=[("dscr", j)], writes=[("diag", dbuf, tap) for tap in range(31)], dma=True)
                    b = nb()

                    def f(e, j=j, dbuf=dbuf, b=b):
                        for tap in range(31):
                            ins = e.matmul(ps[b][:, 0:N], lhsT=diag[:, dbuf, tap, :], rhs=uT[:, j, tap:tap + N],
                                           start=(tap == 0), stop=(tap == 30))
                        return ins
                    P.op("pe", f, reads=[("diag", dbuf, tap) for tap in range(31)] + [("uT", j)], writes=[("ps", b)])
                    conv_epilogue(j, b, N)
                    P.tag = "%dD" % tt
                    attn_step(3)
                P.tag = "%dC" % tt
                if not last:
                    P.op("dve", lambda e: e.tensor_copy(out=uT[:, :, 0:30], in_=uT[:, :, T:T + 30]),
                         reads=[("uT", j) for j in range(8)], writes=[("uT", j) for j in range(8)])
            else:
                sample_conv(N)
            P.tag = "%dD" % tt
            attn_step(1000)
            P.tag = "%dC" % tt
            ln_finish(N)

            if KSTAGE <= 4:
                return finish_tile(sample, t0, N)
            P.tag = "%dD" % tt
            if sample:
                sample_attention(N)
            else:
                attn_step(1000)

            if KSTAGE <= 5:
                return finish_tile(sample, t0, N)
            P.tag = "%dE" % tt
            cTreads = [("cT", kc) for kc in range(8)]
            gsig = {}

            def gate_pair(j):
                bgA = linear(hfn, hreads, N)
                iA = rot("sig", 4)
                P.op("act", lambda e: e.activation(out=sig[:, iA, 0:N], in_=ps[bgA][:, 0:N], func=AF.Sigmoid),
                     reads=[("ps", bgA)], writes=[("sig", iA)])
                bgB = linear(hfn, hreads, N)
                iB = rot("sig", 4)
                P.op("act", lambda e: e.activation(out=sig[:, iB, 0:N], in_=ps[bgB][:, 0:N], func=AF.Sigmoid),
                     reads=[("ps", bgB)], writes=[("sig", iB)])
                gsig[j] = (iA, iB)
            gate_pair(0)
            for j in range(8):
                if j + 1 < 8:
                    gate_pair(j + 1)
                i, i2 = gsig[j]
                bcv = linear(lambda k: cT[:, k, 0:N], cTreads, N)
                it = rot("t1")
                P.op("dve", lambda e, bcv=bcv, i=i, it=it: e.tensor_tensor(
                    out=t1[:, it, 0:N], in0=ps[bcv][:, 0:N], in1=sig[:, i, 0:N], op=ALU.mult),
                    reads=[("ps", bcv), ("sig", i)], writes=[("t1", it)])
                bat = linear(lambda k: oT[:, k, 0:N], ["oT"], N)
                P.op("dve", lambda e, bat=bat, i2=i2: e.tensor_tensor(
                    out=sig[:, i2, 0:N], in0=ps[bat][:, 0:N], in1=sig[:, i2, 0:N], op=ALU.mult),
                    reads=[("ps", bat), ("sig", i2)], writes=[("sig", i2)])
                P.op("dve", lambda e, i2=i2, it=it, j=j: e.tensor_tensor(
                    out=mg[:, j, 0:N], in0=sig[:, i2, 0:N], in1=t1[:, it, 0:N], op=ALU.add),
                    reads=[("sig", i2), ("t1", it)], writes=[("uT", j)])
            P.tag = "%dF" % tt
            mreads = [("uT", kc) for kc in range(8)]
            for j in range(8):
                b = linear(lambda k: mg[:, k, 0:N], mreads, N)
                P.op("dve", lambda e, b=b, j=j: e.tensor_tensor(out=xT[:, j, 0:N], in0=ps[b][:, 0:N], in1=xT[:, j, 0:N], op=ALU.add),
                     reads=[("ps", b), ("xT", j)], writes=[("xT", j)])
            P.tag = "%dG" % tt
            rmsnorm_to_h(N, CF_GFFN)
            for half in range(2):
                for f_ in range(11):
                    bg = linear(hfn, hreads, N, fine=(half == 0 and f_ == 0))
                    i = rot("sig")
                    P.op("act", lambda e, bg=bg, i=i: e.activation(out=sig[:, i, 0:N], in_=ps[bg][:, 0:N], func=AF.Silu),
                         reads=[("ps", bg)], writes=[("sig", i)])
                    bu = linear(hfn, hreads, N)
                    P.op("dve", lambda e, bu=bu, i=i, f_=f_: e.tensor_tensor(
                        out=actb[:, f_, 0:N], in0=ps[bu][:, 0:N], in1=sig[:, i, 0:N], op=ALU.mult),
                        reads=[("ps", bu), ("sig", i)], writes=[("actb", f_)])
                areads = [("actb", f_) for f_ in range(11)]
                for j in range(8):
                    b = linear(lambda k: actb[:, k, 0:N], areads, N)
                    P.op("dve", lambda e, b=b, j=j: e.tensor_tensor(out=xT[:, j, 0:N], in0=ps[b][:, 0:N], in1=xT[:, j, 0:N], op=ALU.add),
                         reads=[("ps", b), ("xT", j)], writes=[("xT", j)])
            P.tag = "%dH" % tt
            rmsnorm_to_h(N, CF_GPLE)
            for j in range(8):
                bg = linear(hfn, hreads, N, fine=(j == 0))
                i = rot("sig")
                P.op("act", lambda e, bg=bg, i=i: e.activation(out=sig[:, i, 0:N], in_=ps[bg][:, 0:N], func=AF.Sigmoid),
                     reads=[("ps", bg)], writes=[("sig", i)])
                bp = linear(lambda k: pT[:, k, 0:N], ["pT"], N)
                P.op("dve", lambda e, bp=bp, i=i: e.tensor_tensor(out=sig[:, i, 0:N], in0=ps[bp][:, 0:N], in1=sig[:, i, 0:N], op=ALU.mult),
                     reads=[("ps", bp), ("sig", i)], writes=[("sig", i)])
                P.op("dve", lambda e, i=i, j=j: e.tensor_tensor(out=xT[:, j, 0:N], in0=sig[:, i, 0:N], in1=xT[:, j, 0:N], op=ALU.add),
                     reads=[("sig", i), ("xT", j)], writes=[("xT", j)])
            P.tag = "%dI" % tt
            def fin(kc):
                P.op("dve", lambda e, kc=kc: e.scalar_tensor_tensor(
                    out=xT[:, kc, 0:N], in0=xT[:, kc, 0:N], scalar=cF[:, CF_GFIN + kc:CF_GFIN + kc + 1],
                    in1=rs[:, 0, 0:N], op0=ALU.mult, op1=ALU.mult),
                    reads=[("xT", kc), ("rs", 0), "cF"], writes=[("xT", kc)])
            rmsnorm_to_h(N, CF_GFIN, out_fp32=fin)
            store_rows(lambda c: xT[:, c, 0:N], [("xT", kc) for kc in range(8)], ys_d if sample else y_d,
                       0 if sample else t0, N, 8)

        lnb = {}

        def conv_epilogue(j, b, N, src_is_sbuf=None):
            if j == 0:
                lnb["sum"] = reserve()
                lnb["sq"] = reserve()
            src = ps[b][:, 0:N] if src_is_sbuf is None else src_is_sbuf
            rd = [("ps", b)] if src_is_sbuf is None else [("t1", 0)]
            P.op("act", lambda e: e.activation(out=cT[:, j, 0:N], in_=src, func=AF.Identity,
                                              bias=cF[:, CF_BDW + j:CF_BDW + j + 1]),
                 reads=rd + ["cF"], writes=[("cT", j)])
            i = rot("sq")
            P.op("act", lambda e: e.activation(out=sq[:, i, 0:N], in_=src, func=AF.Square,
                                              bias=cF[:, CF_BDW + j:CF_BDW + j + 1]),
                 reads=rd + ["cF"], writes=[("sq", i)])
            bs, bq = lnb["sum"], lnb["sq"]
            P.op("pe", lambda e: e.matmul(ps[bs][:, 0:N], lhsT=onesb, rhs=cT[:, j, 0:N], start=(j == 0), stop=(j == 7)),
                 reads=[("cT", j), "cA"], writes=[("ps", bs)])
            P.op("pe", lambda e: e.matmul(ps[bq][:, 0:N], lhsT=onesb, rhs=sq[:, i, 0:N], start=(j == 0), stop=(j == 7)),
                 reads=[("sq", i), "cA"], writes=[("ps", bq)])

        def ln_finish(N):
            bs, bq = lnb["sum"], lnb["sq"]
            P.op("act", lambda e: e.mul(out=rs[:, 1, 0:N], in_=ps[bs][:, 0:N], mul=1.0 / D),
                 reads=[("ps", bs)], writes=[("rs", 1)])
            P.op("dve", lambda e: e.tensor_tensor(out=rs[:, 2, 0:N], in0=rs[:, 1, 0:N], in1=rs[:, 1, 0:N], op=ALU.mult),
                 reads=[("rs", 1)], writes=[("rs", 2)])
            P.op("dve", lambda e: e.scalar_tensor_tensor(out=rs[:, 2, 0:N], in0=ps[bq][:, 0:N], scalar=1.0 / D,
                                                        in1=rs[:, 2, 0:N], op0=ALU.mult, op1=ALU.subtract),
                 reads=[("ps", bq), ("rs", 2)], writes=[("rs", 2)])
            P.op("act", lambda e: e.activation(out=rs[:, 2, 0:N], in_=rs[:, 2, 0:N], func=AF.Ln, bias=epsc),
                 reads=[("rs", 2), "cF"], writes=[("rs", 2)])
            P.op("act", lambda e: e.activation(out=rs[:, 2, 0:N], in_=rs[:, 2, 0:N], func=AF.Exp, scale=-0.5),
                 reads=[("rs", 2)], writes=[("rs", 2)])
            for j in range(8):
                it = rot("t1", 3)
                le = "dve"
                P.op(le, lambda e, j=j, it=it: e.tensor_tensor(out=t1[:, it, 0:N], in0=cT[:, j, 0:N], in1=rs[:, 1, 0:N], op=ALU.subtract),
                     reads=[("cT", j), ("rs", 1)], writes=[("t1", it)])
                P.op(le, lambda e, j=j, it=it: e.tensor_tensor(out=t1[:, it, 0:N], in0=t1[:, it, 0:N], in1=rs[:, 2, 0:N], op=ALU.mult),
                     reads=[("t1", it), ("rs", 2)], writes=[("t1", it)])
                P.op("act", lambda e, j=j, it=it: e.activation(out=cT[:, j, 0:N], in_=t1[:, it, 0:N], func=AF.Silu,
                                                             scale=cF[:, CF_LNG + j:CF_LNG + j + 1],
                                                             bias=cF[:, CF_LNB + j:CF_LNB + j + 1]),
                     reads=[("t1", it), "cF"], writes=[("cT", j)])
            release(bs)
            release(bq)

        def prompt_attention(tt, N):
            first_write = {0: True, 1: True}
            for g in range(3):
                nsb = 16 if g == 2 else 4
                wsb = 32 if g == 2 else 128
                for cc in range(2):
                    c = 2 * g + cc
                    bO = reserve()
                    bL = reserve()
                    P.op("dve", lambda e, bO=bO: e.memset(ps[bO][:, :], 0.0), writes=[("ps", bO)])
                    P.op("dve", lambda e, bL=bL: e.memset(ps[bL][:, :], 0.0), writes=[("ps", bL)])
                    started = set()
                    pending = [None]
                    for hh in range(2):
                        for ksel in (0, 1):
                            if g == 2 and ksel == 1:
                                continue
                            if g == 1 and ksel == 1 and tt == 0:
                                continue
                            sc = []
                            for sbk in range(nsb):
                                if g == 0:
                                    B = 4 * tt + sbk - ksel
                                    if B < 0:
                                        continue
                                    kt = kAB[:, c, (B // 4) % 2, (B % 4) * 128:(B % 4) * 128 + 128]
                                    qv = qm[:, hh, c, sbk * 128:(sbk + 1) * 128]
                                    vt = vAB[:, (B // 4) % 2, B % 4, cc * 128 + hh * 64:cc * 128 + hh * 64 + 64]
                                elif g == 1:
                                    kp = (tt - ksel) % 2
                                    kt = kAB[:, c, kp, sbk:T:4]
                                    qv = qm[:, hh, c, sbk:T:4]
                                    vt = vAB[:, kp, 4 + sbk, cc * 128 + hh * 64:cc * 128 + hh * 64 + 64]
                                else:
                                    kt = kC[:, cc, sbk:S:16]
                                    qv = qm[:, hh, c, sbk:T:16]
                                    vt = vC[:, sbk, cc * 128 + hh * 64:cc * 128 + hh * 64 + 64]
                                sc.append((sbk, kt, qv, vt))
                            if not sc:
                                continue
                            bsc = nb()

                            def fsc(e, sc=sc, bsc=bsc, wsb=wsb):
                                for (sbk, kt, qv, vt) in sc:
                                    ins = e.matmul(ps[bsc][:, sbk * wsb:(sbk + 1) * wsb], lhsT=kt, rhs=qv, start=True, stop=True)
                                return ins
                            P.op("pe", fsc, reads=["kAB", "kC", ("qm", c)], writes=[("ps", bsc)])
                            ip = rot("pexp")
                            P.op("act", lambda e, bsc=bsc, ip=ip: e.activation(out=pexp[:, ip, 0:512], in_=ps[bsc][:, 0:512],
                                                                             func=AF.Exp, scale=0.125),
                                 reads=[("ps", bsc)], writes=[("pexp", ip)])
                            if g == 2:
                                mcol = CA_MC + 512 * tt
                            else:
                                mcol = CA_MOWN if ksel == 0 else CA_MPREV
                            P.op("dve", lambda e, ip=ip, mcol=mcol: e.tensor_tensor(
                                out=pexp[:, ip, 0:512], in0=pexp[:, ip, 0:512], in1=cA[:, mcol:mcol + 512], op=ALU.mult),
                                reads=[("pexp", ip), "cA"], writes=[("pexp", ip)])
                            pv = []
                            for (sbk, kt, qv, vt) in sc:
                                key = (hh, sbk)
                                st = key not in started
                                started.add(key)
                                if g == 2:
                                    fin_ = True
                                elif g == 1:
                                    fin_ = (ksel == 1) or tt == 0
                                else:
                                    fin_ = (ksel == 1) or (4 * tt + sbk - 1 < 0)
                                pv.append((sbk, vt, st, fin_))

                            def fpv(e, pv=pv, hh=hh, ip=ip, bO=bO, bL=bL, wsb=wsb):
                                for (sbk, vt, st, fin_) in pv:
                                    e.matmul(ps[bO][hh * 64:(hh + 1) * 64, sbk * wsb:(sbk + 1) * wsb], lhsT=vt,
                                             rhs=pexp[:, ip, sbk * wsb:(sbk + 1) * wsb], start=False, stop=fin_,
                                             skip_group_check=True)
                                    ins = e.matmul(ps[bL][hh * 64:(hh + 1) * 64, sbk * wsb:(sbk + 1) * wsb],
                                                   lhsT=onesb[:, 0:64],
                                                   rhs=pexp[:, ip, sbk * wsb:(sbk + 1) * wsb], start=False, stop=fin_,
                                                   skip_group_check=True)
                                return ins
                            if pending[0] is not None:
                                pending[0]()
                            pending[0] = (lambda fpv=fpv, ip=ip, bO=bO, bL=bL: P.op(
                                "pe", fpv, reads=[("pexp", ip), "vAB", "vC", "cA"], writes=[("ps", bO), ("ps", bL)]))
                            yield
                    if pending[0] is not None:
                        pending[0]()
                        pending[0] = None
                    if g == 0:
                        dO = accO[:, cc, 0:N]
                        dL = accL[:, cc, 0:N]
                        P.op("act", lambda e, dO=dO, bO=bO: e.copy(out=dO, in_=ps[bO][:, 0:N]),
                             reads=[("ps", bO)], writes=[("accO", cc)])
                        P.op("dve", lambda e, dL=dL, bL=bL: e.tensor_copy(out=dL, in_=ps[bL][:, 0:N]),
                             reads=[("ps", bL)], writes=[("accL", cc)])
                    else:
                        accumulate_v(bO, bL, cc, g, N, first_write)
                    release(bO)
                    release(bL)
                    yield
            for cc in range(2):
                P.op("act", lambda e, cc=cc: e.activation(out=accL[:, cc, 0:N], in_=accL[:, cc, 0:N], func=AF.Ln),
                     reads=[("accL", cc)], writes=[("accL", cc)])
                P.op("act", lambda e, cc=cc: e.activation(out=accL[:, cc, 0:N], in_=accL[:, cc, 0:N], func=AF.Exp, scale=-1.0),
                     reads=[("accL", cc)], writes=[("accL", cc)])
                P.op("dve", lambda e, cc=cc: e.tensor_tensor(out=oT[:, cc, 0:N], in0=accO[:, cc, 0:N], in1=accL[:, cc, 0:N], op=ALU.mult),
                     reads=[("accL", cc), ("accO", cc)], writes=["oT"])

        def accumulate_v(bO, bL, cc, g, N, first_write):
            r = 4 if g == 1 else 16
            m = N // r
            dO = accO[:, cc, 0:N].rearrange("p (m r) -> p r m", r=r)
            dL = accL[:, cc, 0:N].rearrange("p (m r) -> p r m", r=r)
            sO = ps[bO][:, 0:N].rearrange("p (r m) -> p r m", r=r)
            sL = ps[bL][:, 0:N].rearrange("p (r m) -> p r m", r=r)
            P.op("dve", lambda e: e.tensor_tensor(out=dO, in0=sO, in1=dO, op=ALU.add),
                 reads=[("ps", bO), ("accO", cc)], writes=[("accO", cc)])
            P.op("dve", lambda e: e.tensor_tensor(out=dL, in0=sL, in1=dL, op=ALU.add),
                 reads=[("ps", bL), ("accL", cc)], writes=[("accL", cc)])

        def sample_conv_prefetch():
            P.op("sp", lambda e: e.dma_start(out=wrep, in_=wrep_d[:, :]), writes=["rope"], dma=True)
            for g4 in range(4):
                i = 0
                P.op("sp", lambda e, g4=g4, i=i: e.dma_start(out=stS[:, i, :], in_=st_d[g4 * 120:(g4 + 1) * 120, :]),
                     writes=[("stS", i)], dma=True)
                P.op("dve", lambda e, g4=g4, i=i: e.tensor_tensor(out=prod[:, g4, :], in0=stS[:, i, :], in1=wrep[:, :], op=ALU.mult),
                     reads=[("stS", i), "rope"], writes=["kC"])

        def sample_conv(N):
            for j in range(8):
                b = nb()

                def f(e, j=j, b=b):
                    for g4 in range(4):
                        ins = e.matmul(ps[b][:, 0:N], lhsT=prod[:, g4, j * 128:(j + 1) * 128],
                                       rhs=cA[0:120, CA_SEL + 16 * g4:CA_SEL + 16 * g4 + 16], start=(g4 == 0), stop=(g4 == 3))
                    return ins
                P.op("pe", f, reads=["kC", "cA"], writes=[("ps", b)])
                P.op("dve", lambda e, j=j, b=b: e.scalar_tensor_tensor(
                    out=t1[:, 0, 0:N], in0=utf[:, j, 0:N], scalar=cF[:, CF_WDW + j * 31 + 30:CF_WDW + j * 31 + 31],
                    in1=ps[b][:, 0:N], op0=ALU.mult, op1=ALU.add),
                    reads=[("utf", j), ("ps", b), "cF"], writes=[("t1", 0)])
                conv_epilogue(j, b, N, src_is_sbuf=t1[:, 0, 0:N])
            P.op("sp", lambda e: e.dma_start(out=convs_d[:, 0:29 * D],
                                            in_=st_d.rearrange("(s j) c -> s (j c)", j=30)[:, D:30 * D]),
                 dma=True)
            store_rows(lambda c: utf[:, c, 0:N], [("utf", j) for j in range(8)], convs_d, 0, N, 8, dst_col0=29 * D)

        def sample_attention(N):
            caches = [(ca_d, 128, 1), (cb_d, 512, 4), (cc_d, 2048, 16)]
            outs = [was_d, wbs_d, wcs_d]
            for g in range(3):
                L = caches[g][1]
                for kv in range(2):
                    store_rows(lambda c, g=g, kv=kv: kvn[:, kv * 6 + 2 * g + c, 0:N],
                               [("kvn", kv * 6 + 2 * g + c) for c in range(2)], outs[g], 0, N, 2,
                               dst_col0=(L - 1) * 512 + kv * 256)
            bO = reserve()
            bL = reserve()
            P.op("dve", lambda e: e.memset(ps[bO][:, :], 0.0), writes=[("ps", bO)])
            P.op("dve", lambda e: e.memset(ps[bL][:, :], 0.0), writes=[("ps", bL)])
            pend_pv = [None]
            for s in range(N):
                for g in range(3):
                    src, L, dil = caches[g]
                    ik = rot("kt")
                    P.op("sp", lambda e, s=s, src=src, L=L, dil=dil, ik=ik: e.dma_start(
                        out=ktile[:, ik, :], in_=src[s, 0:L:dil, :]), writes=[("kt", ik)], dma=True)
                    iq = rot("qrep")
                    bq = nb()
                    for cc in range(2):
                        c = 2 * g + cc
                        P.op("dve", lambda e, c=c, cc=cc, s=s, iq=iq: e.tensor_tensor(
                            out=qrep[:, iq, cc, :], in0=identb, in1=zbq[:, c, s:s + 1].to_broadcast([128, 128]), op=ALU.mult),
                            reads=["zbq", "cA"], writes=[("qrep", iq, cc)])
                        P.op("pe", lambda e, cc=cc, bq=bq, iq=iq: e.matmul(ps[bq][:, cc * 128:(cc + 1) * 128], lhsT=onesb, rhs=qrep[:, iq, cc, :],
                                                                  start=True, stop=True),
                             reads=[("qrep", iq, cc), "cA"], writes=[("ps", bq)])
                    it = rot("t1")
                    P.op("dve", lambda e, ik=ik, bq=bq, it=it: e.tensor_tensor(
                        out=t1[:, it, 0:256], in0=ktile[:, ik, 0:256], in1=ps[bq][:, 0:256], op=ALU.mult),
                        reads=[("kt", ik), ("ps", bq)], writes=[("t1", it)])
                    col = (s * 3 + g) * 4
                    P.op("dve", lambda e, it=it, col=col: e.tensor_reduce(
                        out=ssc[:, col:col + 4], in_=t1[:, it, 0:256].rearrange("p (h d) -> p h d", h=4),
                        axis=AX.X, op=ALU.add),
                        reads=[("t1", it)], writes=[("ssc", col)])
                    P.op("act", lambda e, col=col: e.activation(out=pss_[:, col:col + 4], in_=ssc[:, col:col + 4], func=AF.Exp, scale=0.125),
                         reads=[("ssc", col)], writes=[("pss", col)])
                    iv = rot("zb")
                    P.op("act", lambda e, ik=ik, iv=iv: e.copy(out=zb[:, iv, 0:256], in_=ktile[:, ik, 256:512]),
                         reads=[("kt", ik)], writes=[("zb", iv)])

                    def fpv(e, s=s, g=g, col=col, iv=iv):
                        ins = None
                        for slot in range(4):
                            cc, hh = slot // 2, slot % 2
                            e.matmul(ps[bO][hh * 64:(hh + 1) * 64, cc * 16 + s:cc * 16 + s + 1],
                                     lhsT=zb[:, iv, slot * 64:(slot + 1) * 64], rhs=pss_[:, col + slot:col + slot + 1],
                                     start=False, stop=(g == 2), skip_group_check=True)
                            ins = e.matmul(ps[bL][hh * 64:(hh + 1) * 64, cc * 16 + s:cc * 16 + s + 1],
                                           lhsT=onesb[:, 0:64], rhs=pss_[:, col + slot:col + slot + 1],
                                           start=False, stop=(g == 2), skip_group_check=True)
                        return ins
                    if pend_pv[0] is not None:
                        pend_pv[0]()
                    pend_pv[0] = (lambda fpv=fpv, iv=iv, col=col: P.op(
                        "pe", fpv, reads=[("zb", iv), ("pss", col), "cA"], writes=[("ps", bO), ("ps", bL)]))
            if pend_pv[0] is not None:
                pend_pv[0]()
            bS = nb()
            P.op("dve", lambda e: e.tensor_tensor(out=zbk[:, :, 0:N], in0=zbq[:, :, 0:N], in1=kvn[:, 0:6, 0:N], op=ALU.mult),
                 reads=["zbq"] + [("kvn", c) for c in range(6)], writes=["zbk"])
            P.op("pe", lambda e: e.matmul(ps[bS][:, 0:6 * N], lhsT=blkb, rhs=zbk[:, :, 0:N].rearrange("p c n -> p (c n)"),
                                          start=True, stop=True),
                 reads=["zbk", "cA"], writes=[("ps", bS)])
            P.op("act", lambda e: e.activation(out=sig[:, 0, 0:6 * N], in_=ps[bS][:, 0:6 * N], func=AF.Exp, scale=0.125),
                 reads=[("ps", bS)], writes=[("sig", 0)])
            P.op("dve", lambda e: e.tensor_tensor(out=sig[:, 1, 0:6 * N], in0=sig[:, 0, 0:6 * N],
                                                 in1=kvn[:, 6:12, 0:N].rearrange("p c n -> p (c n)"), op=ALU.mult),
                 reads=[("sig", 0)] + [("kvn", 6 + c) for c in range(6)], writes=[("sig", 1)])
            for cc in range(2):
                P.op("dve", lambda e, cc=cc: e.tensor_copy(out=accO[:, cc, 0:N], in_=ps[bO][:, cc * 16:cc * 16 + N]),
                     reads=[("ps", bO)], writes=[("accO", cc)])
                P.op("dve", lambda e, cc=cc: e.tensor_copy(out=accL[:, cc, 0:N], in_=ps[bL][:, cc * 16:cc * 16 + N]),
                     reads=[("ps", bL)], writes=[("accL", cc)])
                for g in range(3):
                    c = 2 * g + cc
                    P.op("dve", lambda e, cc=cc, c=c: e.tensor_tensor(out=accO[:, cc, 0:N], in0=accO[:, cc, 0:N],
                                                                     in1=sig[:, 1, c * N:(c + 1) * N], op=ALU.add),
                         reads=[("sig", 1), ("accO", cc)], writes=[("accO", cc)])
                    P.op("dve", lambda e, cc=cc, c=c: e.tensor_tensor(out=accL[:, cc, 0:N], in0=accL[:, cc, 0:N],
                                                                     in1=sig[:, 0, c * N:(c + 1) * N], op=ALU.add),
                         reads=[("sig", 0), ("accL", cc)], writes=[("accL", cc)])
                P.op("dve", lambda e, cc=cc: e.reciprocal(out=accL[:, cc, 0:N], in_=accL[:, cc, 0:N]),
                     reads=[("accL", cc)], writes=[("accL", cc)])
                P.op("dve", lambda e, cc=cc: e.tensor_tensor(out=oT[:, cc, 0:N], in0=accO[:, cc, 0:N], in1=accL[:, cc, 0:N], op=ALU.mult),
                     reads=[("accL", cc), ("accO", cc)], writes=["oT"])
            release(bO)
            release(bL)

        zbq = sb("zbq", [128, 6, TS], BF16)
        zbk = sb("zbk", [128, 6, TS], BF16)

        per = TS // 4
        for tt in range(NT):
            if tt < KTILES and KSTAGE > 0:
                tile(tt, False)
        if KSAMPLE and KSTAGE > 0:
            tile(NT, True)
        while shift_q:
            pump_shift()
        if os.environ.get("KTAGS"):
            P.trace_tags = []
        P.emit()
        if P.trace_tags is not None:
            import json
            json.dump(P.trace_tags, open(os.environ["KTAGS"], "w"))
    return nc


_CACHE = {}


def pack_wstream(w):
    L, offs, tot = slab_offsets()
    out = np.empty((128, tot), np.float32)
    for (name, col0, k0, nk), off in zip(L, offs):
        W = w[name]
        blk = W[k0 * 128:(k0 + nk) * 128, col0:col0 + 128]
        out[:, off:off + nk * 128] = blk.reshape(nk, 128, 128).transpose(1, 0, 2).reshape(128, nk * 128)
    return out


def make_in_maps(x_prompt, x_sample, state_conv, cache_win_a, cache_win_b, cache_win_c, p_prompt, p_sample,
                 w_in, g_mix, w_dw, b_dw, ln_g, ln_b, w_conv_out, w_attn_out, w_o, g_ffn, w_ffn_in, w_ffn_out,
                 g_ple, w_ple_gate, w_ple_proj, g_final, cores=range(NCORES)):
    f = lambda a: np.asarray(a, np.float32)
    w = dict(w_in=f(w_in)[0], w_conv_out=f(w_conv_out)[0], w_attn_out=f(w_attn_out)[0], w_o=f(w_o)[0],
             w_ffn_in=f(w_ffn_in)[0], w_ffn_out=f(w_ffn_out)[0], w_ple_gate=f(w_ple_gate)[0],
             w_ple_proj=f(w_ple_proj)[0])
    wstream = pack_wstream(w)
    ca, rope = build_consts()
    cf = np.zeros((128, CF_N), np.float32)
    cf[:, CF_ID:CF_ID + 128] = np.eye(128, dtype=np.float32)
    cf[:, CF_EPS] = EPS
    cf[:, CF_GMIX:CF_GMIX + 8] = colvec(f(g_mix)[0])
    cf[:, CF_GFFN:CF_GFFN + 8] = colvec(f(g_ffn)[0])
    cf[:, CF_GPLE:CF_GPLE + 8] = colvec(f(g_ple)[0])
    cf[:, CF_GFIN:CF_GFIN + 8] = colvec(f(g_final))
    cf[:, CF_BDW:CF_BDW + 8] = colvec(f(b_dw)[0])
    cf[:, CF_LNG:CF_LNG + 8] = colvec(f(ln_g)[0])
    cf[:, CF_LNB:CF_LNB + 8] = colvec(f(ln_b)[0])
    wd = f(w_dw)[0]
    cf[:, CF_WDW:CF_WDW + 248] = wd.reshape(31, 8, 128).transpose(2, 1, 0).reshape(128, 248)
    wrep = np.ascontiguousarray(np.tile(wd[0:30], (4, 1)))
    xp, xs = f(x_prompt), f(x_sample)
    pp, psm = f(p_prompt)[0], f(p_sample)[0]
    stc = f(state_conv)[0]
    cwa, cwb, cwc = f(cache_win_a)[0], f(cache_win_b)[0], f(cache_win_c)[0]
    in_maps = []
    for c in cores:
        sl = slice(c * TS, (c + 1) * TS)
        in_maps.append(dict(
            x=np.ascontiguousarray(xp[c]), p=np.ascontiguousarray(pp[c]),
            xs=np.ascontiguousarray(xs[sl, 0]), pss=np.ascontiguousarray(psm[sl, 0]),
            state=np.ascontiguousarray(stc[sl].reshape(TS * 30, D)),
            cache_a=np.ascontiguousarray(cwa[sl].reshape(TS, 128, 512)),
            cache_b=np.ascontiguousarray(cwb[sl].reshape(TS, 512, 512)),
            cache_c=np.ascontiguousarray(cwc[sl].reshape(TS, 2048, 512)),
            wstream=wstream, constA=ca, constF=cf, rope=rope, wrep=wrep))
    return in_maps


def kernel(**inputs):
    in_maps = make_in_maps(**inputs)
    if "nc" not in _CACHE:
        _CACHE["nc"] = build_program()
    nc = _CACHE["nc"]
    res = run_bass_kernel_spmd(nc, in_maps, core_ids=list(range(NCORES)))
    R = res.results
    cat = lambda k: np.stack([r[k] for r in R], axis=0)
    y_prompt = cat("y").reshape(8, S, D)
    y_sample = np.concatenate([r["ys"] for r in R], axis=0).reshape(128, 1, D)
    new_conv_prompt = cat("conv_p").reshape(1, 8, 30, D)
    nwa_p = cat("wa_p").reshape(1, 8, 128, 2, 4, 64)
    nwb_p = cat("wb_p").reshape(1, 8, 512, 2, 4, 64)
    nwc_p = cat("wc_p").reshape(1, 8, 2048, 2, 4, 64)
    new_conv_sample = np.concatenate([r["conv_s"] for r in R], axis=0).reshape(1, 128, 30, D)
    nwa_s = np.concatenate([r["wa_s"] for r in R], axis=0).reshape(1, 128, 128, 2, 4, 64)
    nwb_s = np.concatenate([r["wb_s"] for r in R], axis=0).reshape(1, 128, 512, 2, 4, 64)
    nwc_s = np.concatenate([r["wc_s"] for r in R], axis=0).reshape(1, 128, 2048, 2, 4, 64)
    return (y_prompt, y_sample, new_conv_prompt, nwa_p, nwb_p, nwc_p, new_conv_sample, nwa_s, nwb_s, nwc_s)
```
